# Optimizing a Trainium2 kernel written in Bass

```python
import math
import jax, jax.numpy as jnp
from jax import lax
import numpy as np

D_MODEL = 1024
BATCH = 4
SEQ = 4096
DEPTH = 1
DEC_BATCH = 128
DEC_SEQ = 8
PAST_LEN = 8192
PAGE_SIZE = 128

N_META = 16
D_FF = 2816
DN_DK = 128
DN_DV = 128
DN_HEADS = D_MODEL // DN_DV
DN_CONV = 4
DN_CHUNK = 64
SWA_HD = 128
SWA_HEADS = D_MODEL // SWA_HD
SWA_KV_HEADS = SWA_HEADS // 4
SWA_GROUP = SWA_HEADS // SWA_KV_HEADS
WINDOW = 128
SWA_BLOCK = 128
RMS_EPS = 1e-6
L2_EPS = 1e-6

DN_QK_W = DN_HEADS * DN_DK
DN_V_W = DN_HEADS * DN_DV
DN_CONV_W = 2 * DN_QK_W + DN_V_W
SWA_Q_W = SWA_HEADS * SWA_HD
SWA_KV_W = SWA_KV_HEADS * SWA_HD
IN_SIZES = (DN_CONV_W, DN_V_W, DN_HEADS, DN_HEADS, SWA_Q_W, SWA_KV_W, SWA_KV_W, D_MODEL, D_MODEL)
D_IN = sum(IN_SIZES)

kernel_name = 'hybrid_gdn_swa_macaron_step'


def rms_norm(x, g):
    xf = x.astype(jnp.float32)
    y = xf * lax.rsqrt(jnp.mean(xf * xf, axis=-1, keepdims=True) + RMS_EPS)
    return (y * g.astype(jnp.float32)).astype(x.dtype)


def l2_normalize(x):
    return x * lax.rsqrt(jnp.sum(x * x, axis=-1, keepdims=True) + L2_EPS)


def swiglu(x, w_gate, w_up, w_down):
    return (jax.nn.silu(x @ w_gate) * (x @ w_up)) @ w_down


def half_ffn(h, g_pre, g_post, w_gate, w_up, w_down):
    return h + 0.5 * rms_norm(swiglu(rms_norm(h, g_pre), w_gate, w_up, w_down), g_post)


def causal_conv(xp, w):
    k_w = w.shape[0]
    t = xp.shape[1] - (k_w - 1)
    out = xp[:, 0:t] * w[0]
    for j in range(1, k_w):
        out = out + xp[:, j:j + t] * w[j]
    return out


def alibi_slopes():
    return jnp.exp2(-8.0 * jnp.arange(1, SWA_HEADS + 1, dtype=jnp.float32) / SWA_HEADS)


def gated_delta_chunked(q, k, v, beta, g):
    n, t, h, dk = q.shape
    dv = v.shape[-1]
    c = DN_CHUNK
    nz = t // c

    def blocks(x):
        return jnp.moveaxis(x.reshape((n, nz, c) + x.shape[2:]), 2, 3)

    q, k, v, beta, g = (blocks(x) for x in (q, k, v, beta, g))
    gc = jnp.cumsum(g, axis=-1)
    causal = jnp.tril(jnp.ones((c, c), bool))
    strict = jnp.tril(jnp.ones((c, c), bool), -1)
    diff = gc[..., :, None] - gc[..., None, :]
    decay = jnp.where(causal, jnp.exp(jnp.where(causal, diff, 0.0)), 0.0)
    kk = jnp.einsum('nzhid,nzhjd->nzhij', k, k)
    a_mat = jnp.where(strict, beta[..., :, None] * kk * decay, 0.0) + jnp.eye(c, dtype=q.dtype)
    rhs = jnp.concatenate([v * beta[..., None], k * (beta * jnp.exp(gc))[..., None]], axis=-1)
    sol = lax.linalg.triangular_solve(a_mat, rhs, left_side=True, lower=True, unit_diagonal=True)
    u_base, w = sol[..., :dv], sol[..., dv:]
    qk = jnp.einsum('nzhid,nzhjd->nzhij', q, k) * decay
    q_dec = q * jnp.exp(gc)[..., None]
    k_dec = k * jnp.exp(gc[..., -1:] - gc)[..., None]
    c_dec = jnp.exp(gc[..., -1])

    def step(s, xs):
        u_b, w_c, qk_c, qd_c, kd_c, cd_c = xs
        u = u_b - jnp.einsum('nhcd,nhde->nhce', w_c, s)
        o = jnp.einsum('nhcd,nhde->nhce', qd_c, s) + jnp.einsum('nhij,nhje->nhie', qk_c, u)
        s = s * cd_c[..., None, None] + jnp.einsum('nhcd,nhce->nhde', kd_c, u)
        return s, o

    xs = tuple(jnp.moveaxis(x, 1, 0) for x in (u_base, w, qk, q_dec, k_dec, c_dec))
    s0 = jnp.zeros((n, h, dk, dv), jnp.float32)
    s, o = lax.scan(step, s0, xs)
    o = jnp.transpose(o, (1, 0, 3, 2, 4)).reshape(n, t, h, dv)
    return o, s


def gated_delta_prompt(q, k, v, beta, g):
    pad = (-q.shape[1]) % DN_CHUNK

    def padf(x):
        return jnp.pad(x, ((0, 0), (pad, 0)) + ((0, 0),) * (x.ndim - 2))

    o, s = gated_delta_chunked(padf(q), padf(k), padf(v), padf(beta), padf(g))
    return o[:, pad:], s


def gated_delta_recurrent(q, k, v, beta, g, s0):
    def step(s, xs):
        q_t, k_t, v_t, b_t, g_t = xs
        s = s * jnp.exp(g_t)[..., None, None]
        u = b_t[..., None] * (v_t - jnp.einsum('nhd,nhde->nhe', k_t, s))
        s = s + k_t[..., :, None] * u[..., None, :]
        return s, jnp.einsum('nhd,nhde->nhe', q_t, s)

    xs = tuple(jnp.moveaxis(x, 1, 0) for x in (q, k, v, beta, g))
    s, o = lax.scan(step, s0.astype(jnp.float32), xs)
    return jnp.moveaxis(o, 0, 1), s


def sink_softmax(scores, dist, mask, sinks, slopes):
    m = slopes.reshape(SWA_KV_HEADS, SWA_GROUP, 1, 1)
    logits = jnp.where(mask, scores - m * jnp.minimum(dist, WINDOW).astype(jnp.float32), -jnp.inf)
    sink = jnp.broadcast_to(sinks.astype(jnp.float32).reshape(SWA_KV_HEADS, SWA_GROUP, 1, 1), logits.shape[:-1] + (1,))
    return jax.nn.softmax(jnp.concatenate([logits, sink], axis=-1), axis=-1)[..., :-1]


def swa_banded(q, k, v, sinks, slopes, n_keep):
    n, t = q.shape[:2]
    pad = (-t) % SWA_BLOCK
    nb = (t + pad) // SWA_BLOCK

    def padf(x):
        return jnp.pad(x, ((0, 0), (pad, 0), (0, 0), (0, 0)))

    def band(x):
        xb = x.reshape(n, nb, SWA_BLOCK, SWA_KV_HEADS, SWA_HD)
        prev = jnp.concatenate([jnp.zeros_like(xb[:, :1]), xb[:, :-1]], axis=1)
        return jnp.concatenate([prev, xb], axis=2)

    qb = padf(q).reshape(n, nb, SWA_BLOCK, SWA_KV_HEADS, SWA_GROUP, SWA_HD).astype(jnp.float32)
    kb, vb = band(padf(k)), band(padf(v))
    k_meta, v_meta = k[:, :N_META], v[:, :N_META]
    pos = (jnp.arange(t + pad) - pad).reshape(nb, SWA_BLOCK)
    kpos = jnp.concatenate([pos - SWA_BLOCK, pos], axis=1)
    dist_meta = pos[:, :, None] - jnp.arange(N_META)[None, None, :]
    dist_band = pos[:, :, None] - kpos[:, None, :]
    dist = jnp.concatenate([dist_meta, dist_band], axis=-1)
    mask = jnp.concatenate([dist_meta >= 0,
                            (dist_band >= 0) & (dist_band <= WINDOW) & (kpos[:, None, :] >= N_META)], axis=-1)
    scale = SWA_HD ** -0.5
    scores = jnp.concatenate([
        jnp.einsum('nbqkgd,nskd->nbkgqs', qb, k_meta.astype(jnp.float32)),
        jnp.einsum('nbqkgd,nbskd->nbkgqs', qb, kb.astype(jnp.float32))], axis=-1) * scale
    p = sink_softmax(scores, dist[:, None, None], mask[:, None, None], sinks, slopes).astype(v.dtype)
    o = (jnp.einsum('nbkgqs,nskd->nbqkgd', p[..., :N_META], v_meta)
         + jnp.einsum('nbkgqs,nbskd->nbqkgd', p[..., N_META:], vb))
    o = o.reshape(n, t + pad, SWA_HEADS, SWA_HD)[:, pad:]
    return o, (k_meta, v_meta, k[:, -n_keep:], v[:, -n_keep:])


def swa_step(q, k, v, k_meta, v_meta, k_buf, v_buf, sinks, slopes):
    n, t = q.shape[:2]
    w = k_buf.shape[1]
    k_all = jnp.concatenate([k_buf.astype(k.dtype), k], axis=1)
    v_all = jnp.concatenate([v_buf.astype(v.dtype), v], axis=1)
    qg = q.reshape(n, t, SWA_KV_HEADS, SWA_GROUP, SWA_HD).astype(jnp.float32)
    qpos = PAST_LEN + jnp.arange(t)
    kpos = PAST_LEN - w + jnp.arange(w + t)
    dist_meta = qpos[:, None] - jnp.arange(N_META)[None, :]
    dist_band = qpos[:, None] - kpos[None, :]
    dist = jnp.concatenate([dist_meta, dist_band], axis=-1)
    mask = jnp.concatenate([dist_meta >= 0,
                            (dist_band >= 0) & (dist_band <= WINDOW) & (kpos[None, :] >= N_META)], axis=-1)
    scale = SWA_HD ** -0.5
    scores = jnp.concatenate([
        jnp.einsum('ntkgd,nskd->nkgts', qg, k_meta.astype(jnp.float32)),
        jnp.einsum('ntkgd,nskd->nkgts', qg, k_all.astype(jnp.float32))], axis=-1) * scale
    p = sink_softmax(scores, dist, mask, sinks, slopes).astype(v.dtype)
    o = (jnp.einsum('nkgts,nskd->ntkgd', p[..., :N_META], v_meta.astype(v.dtype))
         + jnp.einsum('nkgts,nskd->ntkgd', p[..., N_META:], v_all))
    return o.reshape(n, t, SWA_HEADS, SWA_HD), (k_all[:, -w:], v_all[:, -w:])


def token_mixer(u, conv_hist, w_in, conv_w, a_log, dt_bias, dn_norm_w, w_out, dn_core, swa_core):
    n, t, _ = u.shape
    f32 = jnp.float32
    split_at = np.cumsum(IN_SIZES)[:-1].tolist()
    qkv_pre, z, b, a, sq, sk, sv, g_dn, g_swa = jnp.split(u @ w_in, split_at, axis=-1)
    xp = jnp.concatenate([conv_hist.astype(u.dtype), qkv_pre], axis=1)
    new_conv = xp[:, -(DN_CONV - 1):]
    qkv = jax.nn.silu(causal_conv(xp, conv_w)).astype(f32)
    dq, dk, dv = jnp.split(qkv, [DN_QK_W, 2 * DN_QK_W], axis=-1)
    dq = l2_normalize(dq.reshape(n, t, DN_HEADS, DN_DK)) * (DN_DK ** -0.5)
    dk = l2_normalize(dk.reshape(n, t, DN_HEADS, DN_DK))
    dv = dv.reshape(n, t, DN_HEADS, DN_DV)
    beta = jax.nn.sigmoid(b.astype(f32))
    g = -jnp.exp(a_log.astype(f32)) * jax.nn.softplus(a.astype(f32) + dt_bias.astype(f32))
    o_dn, s_new = dn_core(dq, dk, dv, beta, g)
    o_dn = rms_norm(o_dn, dn_norm_w) * jax.nn.silu(z.astype(f32).reshape(n, t, DN_HEADS, DN_DV))
    o_dn = o_dn.reshape(n, t, DN_V_W).astype(u.dtype)
    o_sw, swa_states = swa_core(sq.reshape(n, t, SWA_HEADS, SWA_HD),
                                sk.reshape(n, t, SWA_KV_HEADS, SWA_HD),
                                sv.reshape(n, t, SWA_KV_HEADS, SWA_HD))
    o_sw = o_sw.reshape(n, t, SWA_Q_W)
    y = jax.nn.sigmoid(g_dn) * o_dn + jax.nn.sigmoid(g_swa) * o_sw
    return y @ w_out, (new_conv, s_new.astype(u.dtype)) + swa_states


def setup_inputs(seed: int = 0) -> dict:
    key = jax.random.key(seed)
    ks = jax.random.split(key, 40)
    f32 = jnp.float32
    cnt = [0]

    def nk():
        cnt[0] += 1
        return ks[cnt[0] - 1]

    def nrm(shape, scale):
        return jax.random.normal(nk(), shape, f32) * scale

    def gain(width):
        return 1.0 + 0.05 * jax.random.normal(nk(), (DEPTH, width), f32)

    n_keep = min(WINDOW, PAST_LEN)
    inp = {}
    inp['x_prompt'] = nrm((BATCH, SEQ, D_MODEL), 1.0)
    inp['x_sample'] = nrm((DEC_BATCH, DEC_SEQ, D_MODEL), 1.0)
    inp['state_dn_conv'] = nrm((DEPTH, DEC_BATCH, DN_CONV - 1, DN_CONV_W), 1.0)
    inp['state_dn_ssm'] = nrm((DEPTH, DEC_BATCH, DN_HEADS, DN_DK, DN_DV), 0.05)
    inp['cache_swa_meta_k'] = nrm((DEPTH, DEC_BATCH, N_META, SWA_KV_HEADS, SWA_HD), 1.0)
    inp['cache_swa_meta_v'] = nrm((DEPTH, DEC_BATCH, N_META, SWA_KV_HEADS, SWA_HD), 1.0)
    inp['cache_swa_k'] = nrm((DEPTH, DEC_BATCH, n_keep, SWA_KV_HEADS, SWA_HD), 1.0)
    inp['cache_swa_v'] = nrm((DEPTH, DEC_BATCH, n_keep, SWA_KV_HEADS, SWA_HD), 1.0)
    inp['meta_tokens'] = nrm((N_META, D_MODEL), 1.0)
    inp['ffn1_norm_pre'] = gain(D_MODEL)
    inp['ffn1_norm_post'] = gain(D_MODEL)
    inp['ffn1_w_gate'] = nrm((DEPTH, D_MODEL, D_FF), D_MODEL ** -0.5)
    inp['ffn1_w_up'] = nrm((DEPTH, D_MODEL, D_FF), D_MODEL ** -0.5)
    inp['ffn1_w_down'] = nrm((DEPTH, D_FF, D_MODEL), D_FF ** -0.5)
    inp['mix_norm_pre'] = gain(D_MODEL)
    inp['mix_norm_post'] = gain(D_MODEL)
    inp['w_in'] = nrm((DEPTH, D_MODEL, D_IN), D_MODEL ** -0.5)
    inp['dn_conv_w'] = nrm((DEPTH, DN_CONV, DN_CONV_W), DN_CONV ** -0.5)
    inp['dn_a_log'] = jnp.log(jax.random.uniform(nk(), (DEPTH, DN_HEADS), f32, 1.0, 16.0))
    dt = jnp.exp(jax.random.uniform(nk(), (DEPTH, DN_HEADS), f32, math.log(1e-3), math.log(1e-1)))
    inp['dn_dt_bias'] = dt + jnp.log(-jnp.expm1(-dt))
    inp['dn_norm_w'] = gain(DN_DV)
    inp['swa_sinks'] = nrm((DEPTH, SWA_HEADS), 0.5)
    inp['w_out'] = nrm((DEPTH, D_MODEL, D_MODEL), D_MODEL ** -0.5)
    inp['ffn2_norm_pre'] = gain(D_MODEL)
    inp['ffn2_norm_post'] = gain(D_MODEL)
    inp['ffn2_w_gate'] = nrm((DEPTH, D_MODEL, D_FF), D_MODEL ** -0.5)
    inp['ffn2_w_up'] = nrm((DEPTH, D_MODEL, D_FF), D_MODEL ** -0.5)
    inp['ffn2_w_down'] = nrm((DEPTH, D_FF, D_MODEL), D_FF ** -0.5)
    return inp


def reference(x_prompt, x_sample, state_dn_conv, state_dn_ssm, cache_swa_meta_k, cache_swa_meta_v,
              cache_swa_k, cache_swa_v, meta_tokens, ffn1_norm_pre, ffn1_norm_post, ffn1_w_gate,
              ffn1_w_up, ffn1_w_down, mix_norm_pre, mix_norm_post, w_in, dn_conv_w, dn_a_log,
              dn_dt_bias, dn_norm_w, swa_sinks, w_out, ffn2_norm_pre, ffn2_norm_post, ffn2_w_gate,
              ffn2_w_up, ffn2_w_down):
    slopes = alibi_slopes()
    n_keep = min(WINDOW, PAST_LEN)

    def run_layer(h, l, conv_hist, dn_core, swa_core):
        h = half_ffn(h, ffn1_norm_pre[l], ffn1_norm_post[l], ffn1_w_gate[l], ffn1_w_up[l], ffn1_w_down[l])
        y, st = token_mixer(rms_norm(h, mix_norm_pre[l]), conv_hist, w_in[l], dn_conv_w[l], dn_a_log[l],
                            dn_dt_bias[l], dn_norm_w[l], w_out[l], dn_core, swa_core)
        h = h + rms_norm(y, mix_norm_post[l])
        h = half_ffn(h, ffn2_norm_pre[l], ffn2_norm_post[l], ffn2_w_gate[l], ffn2_w_up[l], ffn2_w_down[l])
        return h, st

    n_p = x_prompt.shape[0]
    meta = jnp.broadcast_to(meta_tokens.astype(x_prompt.dtype)[None], (n_p, N_META, D_MODEL))
    hp = jnp.concatenate([meta, x_prompt], axis=1)
    hs = x_sample
    p_st, s_st = [], []
    for l in range(DEPTH):
        conv0 = jnp.zeros((n_p, DN_CONV - 1, DN_CONV_W), hp.dtype)
        hp, st = run_layer(hp, l, conv0, gated_delta_prompt,
                           lambda q, k, v, l=l: swa_banded(q, k, v, swa_sinks[l], slopes, n_keep))
        p_st.append(st)
        hs, st = run_layer(hs, l, state_dn_conv[l],
                           lambda q, k, v, b, g, l=l: gated_delta_recurrent(q, k, v, b, g, state_dn_ssm[l]),
                           lambda q, k, v, l=l: swa_step(q, k, v, cache_swa_meta_k[l], cache_swa_meta_v[l],
                                                         cache_swa_k[l], cache_swa_v[l], swa_sinks[l], slopes))
        s_st.append(st)
    p_conv, p_ssm, p_meta_k, p_meta_v, p_win_k, p_win_v = [jnp.stack(a) for a in zip(*p_st)]
    s_conv, s_ssm, s_win_k, s_win_v = [jnp.stack(a) for a in zip(*s_st)]
    y_prompt = hp[:, N_META:]
    return (y_prompt, hs, p_conv, p_ssm, p_meta_k, p_meta_v, p_win_k, p_win_v, s_conv, s_ssm, s_win_k, s_win_v)
```

```python
import contextlib
import itertools
import numpy as np
import ml_dtypes
import concourse.bass as bass
import concourse.mybir as mybir
from concourse.bass_utils import run_bass_kernel_spmd

F32 = mybir.dt.float32
BF16 = mybir.dt.bfloat16
AF = mybir.ActivationFunctionType
ALU = mybir.AluOpType
AX = mybir.AxisListType

D = 1024
DFF = 2816
NFF = 22
NCORES = 8
SEQ = 4096
NMETA = 16
PADF = 112
TP = PADF + NMETA + SEQ
NSUBP = TP // 128
DEC_B = 128
DEC_T = 8
NSEQ = DEC_B // NCORES
RMS_EPS = 1e-6
L2_EPS = 1e-6
SLOT = 2816


class Buf:
    def __init__(self, name, t):
        self.name = name
        self.t = t
        self.st = {}
        self.excl = False
        self.defkey = None

    def __call__(self, *keys):
        return _KV(self, keys if keys else (None,))

    def __getitem__(self, idx):
        return V(self, self.t[idx], (self.defkey,))


class _KV:
    def __init__(self, buf, keys):
        self.buf = buf
        self.keys = keys

    def __getitem__(self, idx):
        return V(self.buf, self.buf.t[idx], self.keys)


class V:
    def __init__(self, buf, ap, keys):
        self.buf = buf
        self.ap = ap
        self.keys = keys

    def bitcast(self, dt):
        return V(self.buf, self.ap.bitcast(dt), self.keys)

    def m(self, fn):
        return V(self.buf, fn(self.ap), self.keys)

    def bc(self, axis, n):
        a = self.ap.unsqueeze(axis)
        shp = list(a.shape)
        shp[axis] = n
        return V(self.buf, a.broadcast_to(shp), self.keys)


class Op:
    __slots__ = ("eng", "fn", "deps", "idx", "sig", "semval", "stream", "sval")

    def __init__(self, eng, fn):
        self.eng = eng
        self.fn = fn
        self.deps = []
        self.idx = -1
        self.sig = False
        self.semval = 0
        self.stream = None
        self.sval = 0


ENGS = ("pe", "dve", "act", "pool", "sp")


class Builder:
    def __init__(self, nc):
        self.nc = nc
        self.stack = contextlib.ExitStack()
        self.ops = {e: [] for e in ENGS}
        self.streams = {}
        self.nbuf = 0

    def sb(self, name, shape, dt):
        t = self.stack.enter_context(self.nc.sbuf_tensor(name, list(shape), dt))
        return Buf(name, t)

    def ps(self, name, shape, dt):
        t = self.stack.enter_context(self.nc.psum_tensor(name, list(shape), dt))
        b = Buf(name, t)
        b.excl = True
        return b

    def emit(self, eng, fn, outs=(), ins=(), stream=None):
        if not getattr(self, "enabled", True):
            return None
        op = Op(eng, fn)
        deps = set()
        for v in ins:
            st = v.buf.st
            for k in v.keys:
                ents = list(st.values()) if k is None else [st.get(k), st.get(None)]
                for e in ents:
                    if e is not None:
                        if e[0] is not None:
                            deps.add(e[0])
                        if v.buf.excl:
                            deps.update(r for r in e[1] if r.eng != eng)
        for v in outs:
            st = v.buf.st
            for k in v.keys:
                ents = list(st.values()) if k is None else [st.get(k), st.get(None)]
                for e in ents:
                    if e is not None:
                        if e[0] is not None:
                            deps.add(e[0])
                        deps.update(e[1])
        for v in ins:
            st = v.buf.st
            for k in v.keys:
                e = st.get(k)
                if e is None:
                    e = st[k] = [None, []]
                e[1].append(op)
        for v in outs:
            st = v.buf.st
            for k in v.keys:
                if k is None:
                    st.clear()
                st[k] = [op, []]
        deps.discard(op)
        if stream is not None:
            op.stream = stream
            self.streams[stream] = self.streams.get(stream, 0) + 16
            op.sval = self.streams[stream]
        op.deps = [d for d in deps if not (d.eng == "pe" and eng == "pe" and d.stream is None)]
        for d in op.deps:
            if d.stream is None:
                d.sig = True
        op.idx = len(self.ops[eng])
        self.ops[eng].append(op)
        return op

    def finalize(self):
        nc = self.nc
        st = self.stack
        esem = {e: st.enter_context(nc.semaphore("sem_" + e)) for e in ENGS}
        ssem = {s: st.enter_context(nc.semaphore("ds_" + s)) for s in self.streams}
        last_ops = []
        for e in ENGS:
            cands = [op for op in self.ops[e] if op.stream is None]
            if cands and e != "sp":
                cands[-1].sig = True
                last_ops.append(cands[-1])
        for e in ENGS:
            c = 0
            for op in self.ops[e]:
                if op.sig and op.stream is None:
                    c += 1
                    op.semval = c
        block = st.enter_context(nc.Block())
        final_streams = dict(self.streams)

        def run(e, eng):
            known = {}
            for op in self.ops[e]:
                need = {}
                for d in op.deps:
                    if d.stream is not None:
                        key, val = ("s", d.stream), d.sval
                    else:
                        key, val = ("e", d.eng), d.semval
                    if val > need.get(key, 0):
                        need[key] = val
                for key, val in need.items():
                    if known.get(key, 0) >= val:
                        continue
                    known[key] = val
                    sem = ssem[key[1]] if key[0] == "s" else esem[key[1]]
                    eng.wait_ge(sem, val)
                ins = op.fn(eng)
                if op.stream is not None:
                    ins.then_inc(ssem[op.stream], 16)
                elif op.sig:
                    ins.then_inc(esem[e], 1)
            if e == "sp":
                for s, val in final_streams.items():
                    if known.get(("s", s), 0) < val:
                        eng.wait_ge(ssem[s], val)
                for lo in last_ops:
                    if known.get(("e", lo.eng), 0) < lo.semval:
                        eng.wait_ge(esem[lo.eng], lo.semval)

        @block.tensor
        def _(eng):
            run("pe", eng)

        @block.vector
        def _(eng):
            run("dve", eng)

        @block.scalar
        def _(eng):
            run("act", eng)

        @block.gpsimd
        def _(eng):
            run("pool", eng)

        @block.sync
        def _(eng):
            run("sp", eng)

    def close(self):
        self.stack.close()

    def mm(self, out, lhsT, rhs, start=True, stop=True):
        return self.emit("pe", lambda e: e.matmul(out.ap, lhsT.ap, rhs.ap, start=start, stop=stop),
                         outs=[out], ins=[lhsT, rhs])

    def tr(self, out, in_, ident):
        return self.emit("pe", lambda e: e.transpose(out.ap, in_.ap, ident.ap), outs=[out], ins=[in_, ident])

    def act(self, out, in_, func, scale=1.0, bias=0.0, accum=None, extra_ins=()):
        sc = scale.ap if isinstance(scale, V) else scale
        bi = bias.ap if isinstance(bias, V) else bias
        ins = [in_] + [x for x in (scale, bias) if isinstance(x, V)] + list(extra_ins)
        outs = [out] + ([accum] if accum is not None else [])
        if accum is None:
            fn = lambda e: e.activation(out.ap, in_.ap, func, bias=bi, scale=sc)
        else:
            fn = lambda e: e.activation(out.ap, in_.ap, func, bias=bi, scale=sc, accum_out=accum.ap)
        return self.emit("act", fn, outs=outs, ins=ins)

    def tt(self, out, a, b, op, eng="dve"):
        return self.emit(eng, lambda e: e.tensor_tensor(out.ap, a.ap, b.ap, op), outs=[out], ins=[a, b])

    def ts(self, out, a, s1, op0, s2=None, op1=None, eng="dve"):
        a1 = s1.ap if isinstance(s1, V) else s1
        a2 = s2.ap if isinstance(s2, V) else s2
        ins = [a] + [x for x in (s1, s2) if isinstance(x, V)]
        if op1 is None:
            fn = lambda e: e.tensor_scalar(out.ap, a.ap, a1, None, op0)
        else:
            fn = lambda e: e.tensor_scalar(out.ap, a.ap, a1, a2, op0, op1)
        return self.emit(eng, fn, outs=[out], ins=ins)

    def stt(self, out, a, s, b, op0, op1):
        a1 = s.ap if isinstance(s, V) else s
        ins = [a, b] + ([s] if isinstance(s, V) else [])
        return self.emit("dve", lambda e: e.scalar_tensor_tensor(out.ap, a.ap, a1, b.ap, op0, op1),
                         outs=[out], ins=ins)

    def copy(self, out, in_, eng="dve"):
        if eng == "act":
            return self.emit("act", lambda e: e.copy(out.ap, in_.ap), outs=[out], ins=[in_])
        if eng == "dve":
            return self.emit(eng, lambda e: e.tensor_scalar(out.ap, in_.ap, 1.0, None, ALU.mult), outs=[out], ins=[in_])
        return self.emit(eng, lambda e: e.tensor_copy(out.ap, in_.ap), outs=[out], ins=[in_])

    def memset(self, out, val, eng="dve"):
        return self.emit(eng, lambda e: e.memset(out.ap, val), outs=[out])

    def dma(self, out, in_, stream, eng="sp", ins=(), outs=()):
        oa = out.ap if isinstance(out, V) else out
        ia = in_.ap if isinstance(in_, V) else in_
        o = [out] if isinstance(out, V) else []
        i = [in_] if isinstance(in_, V) else []
        return self.emit(eng, lambda e: e.dma_start(out=oa, in_=ia), outs=o + list(outs), ins=i + list(ins),
                         stream=stream)


NSUB = 3
NT = 128 * NSUB
NEG = -30000.0
NV = 137
SLOTW = 1408


def _inproj_order():
    rest = list(range(24, 60))
    out = []
    for q in range(6):
        out += list(range(4 * q, 4 * q + 4))
        out += rest[6 * q:6 * q + 6]
    return out


INPROJ_ORDER = _inproj_order()


class WStream:
    def __init__(self, B, nslots):
        self.B = B
        self.nslots = nslots
        self.ring = B.sb("wring", [128, nslots, SLOTW], BF16)
        self.sched = []
        self.nload = 0
        self.nuse = 0

    def add(self, dram_ap, size):
        self.sched.append((dram_ap, size))

    def prefetch(self):
        if self.nload >= len(self.sched):
            return
        ap, size = self.sched[self.nload]
        s = self.nload % self.nslots
        if getattr(self, "halfw", False):
            self.B.dma(self.ring(s)[:, s, 0:size // 2], ap[:, 0:size // 2], stream="w%d" % s, eng="pool")
        else:
            self.B.dma(self.ring(s)[:, s, 0:size], ap, stream="w%d" % s, eng="pool")
        self.nload += 1

    def start(self):
        for _ in range(self.nslots):
            self.prefetch()

    def get(self):
        assert self.nuse < self.nload, "weight schedule underflow"
        s = self.nuse % self.nslots
        self.nuse += 1
        ring = self.ring

        def view(lo, hi):
            return ring(s)[:, s, lo:hi]
        return view

    def done(self):
        self.prefetch()


def build_program(cfg):
    nc = bass.Bass("TRN2", target_bir_lowering=False)
    B = Builder(nc)
    nmac = cfg.get("nmac", 11)
    do_sample = cfg.get("sample", True)
    do_mixer = cfg.get("mixer", True)
    do_ffn2 = cfg.get("ffn2", True)
    dbg = cfg.get("dbg", False)

    def din(name, shape, dt=F32):
        return nc.dram_tensor(name, list(shape), dt, kind="ExternalInput").ap()

    def dout(name, shape, dt=F32):
        return nc.dram_tensor(name, list(shape), dt, kind="ExternalOutput").ap()

    xpT = din("xpT", [D, TP])
    ypT = dout("ypT", [D, TP])
    xsT = din("xsT", [D, 128])
    ysT = dout("ysT", [D, 128])
    wgu = [din("wgu%d" % i, [2 * NFF, 128, 1024]) for i in (1, 2)]
    wdn = [din("wdn%d" % i, [8, 128, DFF]) for i in (1, 2)]
    win = din("win", [60, 128, 1024])
    wba = din("wba", [128, 128])
    wout = din("wout", [8, 128, 1024])
    gains = din("gains", [128, 6, 8])
    cmask = din("cmask", [128, 9, 128])
    cbias = din("cbias", [128, 4, 8, 128])
    cbiasm = din("cbiasm", [16, 3, 8, 128])
    vecs = din("vecs", [128, NV])
    shistT = din("shistT", [3072, NSEQ * 3])
    s_ssm_in = din("s_ssm_in", [NSEQ, 8, 128, 128])
    kbT_in = din("kbT_in", [NSEQ, 2, 128, 128])
    kmT_in = din("kmT_in", [NSEQ, 2, 128, 16])
    vb_in = din("vb_in", [NSEQ, 128, 2, 128])
    vm_in = din("vm_in", [NSEQ, 16, 2, 128])
    kb_in = din("kb_in", [NSEQ, 128, 2, 128])
    pconv_o = dout("pconv_o", [3072, 3])
    pssm_o = dout("pssm_o", [8, 128, 128])
    pmeta_o = dout("pmeta_o", [4, 128, 128])
    pwin_o = dout("pwin_o", [4, 128, 128])
    sconv_o = dout("sconv_o", [3072, NSEQ * 3])
    sssm_o = dout("sssm_o", [NSEQ, 8, 128, 128])
    swink_o = dout("swink_o", [NSEQ, 128, 2, 128])
    swinv_o = dout("swinv_o", [NSEQ, 128, 2, 128])
    dbg_o = dout("dbg_o", [128, 8, NT]) if dbg else None

    cm = B.sb("cm", [128, 9, 128], F32)
    ident_b = B.sb("ident_b", [128, 128], BF16)
    ones_b = B.sb("ones_b", [128, 128], BF16)
    cb = B.sb("cb", [128, 2, 8, 128], BF16)
    cbm = B.sb("cbm", [16, 8, 128], BF16)
    vc = B.sb("vc", [128, NV], F32)
    negA = B.sb("negA", [128, 8], F32)
    esink = B.sb("esink", [128, 8], F32)
    gn = B.sb("gn", [128, 6, 8], F32)
    gnh = B.sb("gnh", [128, 6, 8], F32)
    mhalf = B.sb("mhalf", [128, 1], F32)
    epsb = B.sb("epsb", [128, 1], F32)
    h = B.sb("h", [128, 8, NT], F32)
    u = B.sb("u", [128, 8, NT], BF16)
    sqs = B.sb("sqs", [128, 2, NT], BF16)
    shr = B.sb("shr", [128, 27, NT], BF16)
    shr2 = B.sb("shr2", [128, 9, NT], BF16)
    yo = B.sb("yo", [128, 8, NT], F32)
    sg = B.sb("sg", [128, 2, NT], F32)
    ms = B.sb("ms", [128, NT], F32)
    rinv = B.sb("rinv", [128, NT], F32)
    tmpn = B.sb("tmpn", [128, 2, NT], F32)
    PS = B.ps("PS", [128, 8, 512], F32)
    W = WStream(B, cfg.get("nslots", 6))
    W.halfw = cfg.get("halfw", False)
    psn = [0]

    I_f = cm[:, 7, :]
    ONES_f = cm[:, 8, :]

    reserved = set()

    def bank():
        while True:
            b = psn[0] % 8
            psn[0] += 1
            if b not in reserved:
                return b

    def psb(b, lo=0, hi=512):
        return PS(b)[:, b, lo:hi]

    def psbf(b, n):
        return PS(b)[:, b, 0:(n + 1) // 2].bitcast(BF16)

    B.dma(cm[:, :, :], cmask, "c0")
    B.dma(gn[:, :, :], gains, "c1")
    B.dma(vc[:, :], vecs, "c2")
    B.dma(cb[:, :, :, :], cbias[:, 0:2], "c3", eng="pool")
    B.copy(ident_b[:, :], cm[:, 7, :])
    B.copy(ones_b[:, :], cm[:, 8, :])
    B.ts(gnh[:, :, :], gn[:, :, :], 0.5, ALU.mult)
    B.memset(mhalf[:, :], -0.5)
    B.memset(epsb[:, :], RMS_EPS)
    convw = lambda c, j: vc[:, c * 4 + j:c * 4 + j + 1]
    dnw = vc[:, 96:97]
    alog = vc[:, 97:105]
    dtb = vc[:, 105:113]
    sinks = vc[:, 113:121]
    seqsel = vc[:, 121:137]
    B.act(negA[:, :], alog, AF.Exp)
    B.ts(negA[:, :], negA[:, :], -1.0, ALU.mult)
    B.act(esink[:, :], sinks, AF.Exp)

    def sched_ffn(i):
        for j in range(NFF):
            W.add(wgu[i][2 * j], 1024)
            W.add(wgu[i][2 * j + 1], 1024)
        for d in range(8):
            W.add(wdn[i][d, :, 0:1408], 1408)
            W.add(wdn[i][d, :, 1408:2816], 1408)

    def sched_mix():
        for j in INPROJ_ORDER:
            W.add(win[j], 1024)
        W.add(wba, 128)
        for d in range(8):
            W.add(wout[d], 1024)

    tiles = [("p", m) for m in range(nmac)] + ([("s", 0)] if do_sample else [])
    for _ in tiles:
        sched_ffn(0)
        if do_mixer:
            sched_mix()
        if do_ffn2:
            sched_ffn(1)
    W.start()

    sqn = [0]

    class Stats:
        def __init__(self, N):
            self.N = N
            self.b = bank()
            reserved.add(self.b)
            self.pend = []
            self.n = 0

        def add(self, src_ps):
            k = sqn[0] % 2
            sqn[0] += 1
            B.act(sqs(k)[:, k, :self.N], src_ps, AF.Square)
            self.pend.append(k)
            if len(self.pend) > 1:
                self.flush1()

        def flush1(self):
            k = self.pend.pop(0)
            B.mm(psb(self.b, 0, self.N), ones_b[:, :], sqs(k)[:, k, :self.N], start=(self.n == 0), stop=(self.n == 7))
            self.n += 1

        def finish(self):
            while self.pend:
                self.flush1()
            reserved.discard(self.b)
            return self.b

    def rms_rinv(src, N, scale, stats=None):
        if stats is not None:
            b = stats.finish()
        else:
            b = bank()
        for c in range(8 if stats is None else 0):
            k = sqn[0] % 2
            sqn[0] += 1
            B.act(sqs(k)[:, k, :N], src(c)[:, c, :N], AF.Square)
            B.mm(psb(b, 0, N), ones_b[:, :], sqs(k)[:, k, :N], start=(c == 0), stop=(c == 7))
        B.act(ms[:, :N], psb(b, 0, N), AF.Ln, scale=scale, bias=epsb[:, 0:1])
        B.act(rinv[:, :N], ms[:, :N], AF.Exp, scale=-0.5)

    def prenorm(gi, N):
        rms_rinv(h, N, 1.0 / D)
        for c in range(8):
            B.stt(u(c)[:, c, :N], h(c)[:, c, :N], gn[:, gi, c:c + 1], rinv[:, :N], ALU.mult, ALU.mult)

    def postnorm_add(gi, N, half, out_ap=None, stats=None):
        rms_rinv(yo, N, 1.0 / D, stats)
        gsrc = gnh if half else gn
        for c in range(8):
            k = c % 2
            B.stt(tmpn(k)[:, k, :N], yo(c)[:, c, :N], gsrc[:, gi, c:c + 1], rinv[:, :N], ALU.mult, ALU.mult)
            if out_ap is None:
                B.tt(h(c)[:, c, :N], h(c)[:, c, :N], tmpn(k)[:, k, :N], ALU.add, eng="pool")
            else:
                B.tt(yo(c)[:, c, :N], h(c)[:, c, :N], tmpn(k)[:, k, :N], ALU.add, eng="pool")
                B.dma(out_ap[c * 128:(c + 1) * 128, :], yo(c)[:, c, :N], "yout%d" % c)

    def ffn(fi, N, out_ap=None):
        gi = 0 if fi == 0 else 4
        prenorm(gi, N)
        for j in range(NFF):
            wg = W.get()
            bg = bank()
            for c in range(8):
                B.mm(psb(bg, 0, N), wg(c * 128, c * 128 + 128), u(c)[:, c, :N], start=(c == 0), stop=(c == 7))
            W.done()
            wu = W.get()
            bu = bank()
            for c in range(8):
                B.mm(psb(bu, 0, N), wu(c * 128, c * 128 + 128), u(c)[:, c, :N], start=(c == 0), stop=(c == 7))
            W.done()
            k = j % 2
            B.act(sg(k)[:, k, :N], psb(bg, 0, N), AF.Silu)
            B.tt(shr(j)[:, j, :N], sg(k)[:, k, :N], psb(bu, 0, N), ALU.mult)
        st = Stats(N)
        for d in range(8):
            bo = bank()
            for hf in range(2):
                wd = W.get()
                for jj in range(11):
                    j = hf * 11 + jj
                    B.mm(psb(bo, 0, N), wd(jj * 128, jj * 128 + 128), shr(j)[:, j, :N], start=(j == 0),
                         stop=(j == NFF - 1))
                W.done()
            B.copy(yo(d)[:, d, :N], psb(bo, 0, N), eng="act")
            st.add(psb(bo, 0, N))
        postnorm_add(gi + 1, N, True, out_ap, st)

    def alias(parent, ap):
        x = Buf(parent.name + "_al", ap)
        x.st = parent.st
        return x

    if do_mixer:
        qk = B.sb("qk", [128, 2, 8, NT], BF16)
        vT = B.sb("vT", [128, 8, NT], BF16)
        ybuf = u
        xp = B.sb("xp", [128, 4, NT + 3], F32)
        xps = alias(xp, xp.t[:, :, 0:NSEQ * 11].rearrange("p k (n j) -> p k n j", j=11))
        acc = B.sb("acc", [128, 4, NT], F32)
        qs = B.sb("qs", [128, 3, NT], F32)
        ms2 = tmpn
        oh16 = B.sb("oh16", [128, 16, 16], BF16)
        hist = B.sb("hist", [128, 24, 3], F32)
        ba = B.sb("ba", [128, NSUB, 16], F32)
        beta = B.sb("beta", [128, NSUB, 8], F32)
        gg = B.sb("gg", [128, NSUB, 8], F32)
        ge = B.sb("ge", [128, NSUB, 8], F32)
        gx = B.sb("gx", [128, NSUB, 8], F32)
        sm = B.sb("sm", [128, 8, 8], F32)
        cdec_s = B.sb("cdec_s", [128, NSEQ, 8], F32)
        Rs = B.sb("Rs", [128, NSEQ, 8], F32)
        kvf = B.sb("kvf", [128, 4, 128], F32)
        kvtok = B.sb("kvtok", [128, 4, 128], F32)
        skprev = B.sb("skprev", [128, 2, 128], BF16)
        kmT = B.sb("kmT", [128, 2, 16], BF16)
        vm = B.sb("vm", [16, 2, 128], BF16)
        vsw = B.sb("vsw", [128, 2, 2, 128], BF16)
        S = B.sb("S", [128, 8, 128], F32)
        Sbf = B.sb("Sbf", [128, 8, 128], BF16)
        def tset0():
            tA_ = B.sb("tA", [128, 4, 128], F32)
            tB_ = B.sb("tB", [128, 4, 128], F32)
            tC_ = B.sb("tC", [128, 4, 128], F32)
            dI_ = B.sb("decI", [128, 4, 128], F32)
            rest = [B.sb(nm, [128, 4, 128], BF16) for nm in ("PU", "PL", "XU", "XL", "qkTm", "r0", "uu", "Vtok", "kdec",
                                                            "on_t")]
            return [tA_, tB_, tC_, dI_] + rest + [B.sb("ssum", [128, 3, 4], F32)]

        R1 = B.sb("R1", [128, 4 * 512 + 10 * 256 + 16], F32)

        def tset1():
            out = []
            off = 0
            for i in range(4):
                x = alias(R1, R1.t[:, off:off + 512].rearrange("p (h n) -> p h n", h=4))
                x.defkey = "f%d" % i
                out.append(x)
                off += 512
            for i in range(10):
                x = alias(R1, R1.t[:, off:off + 256].bitcast(BF16).rearrange("p (h n) -> p h n", h=4))
                x.defkey = "b%d" % i
                out.append(x)
                off += 256
            x = alias(R1, R1.t[:, off:off + 12].rearrange("p (a b) -> p a b", a=3))
            x.defkey = "ss"
            out.append(x)
            return out

        TS = [tset0(), tset1()]
        shist = alias(R1, R1.t[:, 0:1152].rearrange("p (c x) -> p c x", c=24))
        sconv = alias(R1, R1.t[:, 1152:2304].rearrange("p (c x) -> p c x", c=24))
        tA, tB, tC, decI = TS[0][0:4]
        lg = decI
        PTo = B.sb("PTo", [128, 4, 128], BF16)
        PTp = B.sb("PTp", [128, 4, 128], BF16)
        PTm = B.sb("PTm", [16, 4, 128], BF16)
        rden = B.sb("rden", [128, 4, 128], F32)
        svb_t = B.sb("svb_t", [128, 2, NT], BF16)
        KQT = alias(S, S.t[:, :, :].rearrange("p a b -> p (a b)").bitcast(BF16).rearrange(
            "p (h w t) -> p h w t", h=8, w=2))
        Sn = B.sb("Sn", [128, 2, 8, 128], BF16)
        Uexp = alias(qs, qs.t[:, :, :].rearrange("p a b -> p (a b)")[:, 0:1024].bitcast(BF16).rearrange(
            "p (n d) -> p n d", n=NSEQ))
        Sold = B.sb("Sold", [128, 4, 128], F32)
        Snew = B.sb("Snew", [128, 4, 128], F32)
        kbT = B.sb("kbT", [128, 2, 4, 128], BF16)
        vb = B.sb("vb", [128, 2, 4, 128], BF16)
        kmTs = B.sb("kmTs", [128, NSEQ, 16], BF16)
        vms = B.sb("vms", [16, NSEQ, 128], BF16)

        B.memset(oh16[:, :, :], 0.0)
        for c in range(16):
            B.memset(oh16[:, c, c:c + 1], 1.0)
        B.memset(hist[:, :, :], 0.0)
        B.memset(S[:, :, :], 0.0)
        B.memset(Sbf[:, :, :], 0.0)

    def barrier_shr():
        B.emit("pool", lambda e: e.memset(tmpn.t[0:1, 0, 0:1], 0.0), outs=[shr[:, :, :], tmpn(0)[0:1, 0, 0:1]])

    def stage(name):
        if cfg.get("stop") == name:
            B.enabled = False

    def inproj(kind, m, N):
        nsub = N // 128
        stage("ip_start")
        prenorm(2, N)
        bss = bank()
        reserved.add(bss)

        def qkv_post(items):
            ks = [c % 4 for c, _ in items]
            if kind == "p":
                for (c, b), k in zip(items, ks):
                    B.copy(xp(k)[:, k, 0:3], hist(c)[:, c, :], eng="pool")
                for (c, b), k in zip(items, ks):
                    B.copy(xp(k)[:, k, 3:3 + N], psb(b, 0, N), eng="act")
                for (c, b), k in zip(items, ks):
                    B.copy(hist(c)[:, c, :], xp(k)[:, k, N:N + 3], eng="pool")
                for (c, b), k in zip(items, ks):
                    B.ts(acc(k)[:, k, :N], xp(k)[:, k, 0:N], convw(c, 0), ALU.mult)
                for j in range(1, 4):
                    for (c, b), k in zip(items, ks):
                        B.stt(acc(k)[:, k, :N], xp(k)[:, k, j:j + N], convw(c, j), acc(k)[:, k, :N], ALU.mult, ALU.add)
            else:
                f3n = lambda a: a.rearrange("p (n j) -> p n j", j=3)
                f8 = lambda a: a.rearrange("p (n t) -> p n t", t=8)
                for (c, b), k in zip(items, ks):
                    B.copy(xps(k)[:, k, :, 0:3], shist(c)[:, c, :].m(f3n), eng="pool")
                for (c, b), k in zip(items, ks):
                    B.copy(xps(k)[:, k, :, 3:11], psb(b, 0, N).m(f8), eng="act")
                for (c, b), k in zip(items, ks):
                    B.copy(sconv(c)[:, c, :].m(f3n), xps(k)[:, k, :, 8:11], eng="pool")
                for (c, b), k in zip(items, ks):
                    B.ts(acc(k)[:, k, :N].m(f8), xps(k)[:, k, :, 0:8], convw(c, 0), ALU.mult)
                for j in range(1, 4):
                    for (c, b), k in zip(items, ks):
                        B.stt(acc(k)[:, k, :N].m(f8), xps(k)[:, k, :, j:j + 8], convw(c, j), acc(k)[:, k, :N].m(f8),
                              ALU.mult, ALU.add)
            return items, ks

        def qkv_post2(items, ks):
            for (c, b), k in zip(items, ks):
                if c < 16:
                    which = 1 if c < 8 else 0
                    hd_i = c % 8
                    B.act(qk((which, hd_i))[:, which, hd_i, :N], acc(k)[:, k, :N], AF.Silu)
                else:
                    B.act(vT(c - 16)[:, c - 16, :N], acc(k)[:, k, :N], AF.Silu)
            for (c, b), k in zip(items, ks):
                if c < 16:
                    which = 1 if c < 8 else 0
                    hd_i = c % 8
                    k2 = sqn[0] % 2
                    sqn[0] += 1
                    B.act(sqs(k2)[:, k2, :N], qk((which, hd_i))[:, which, hd_i, :N], AF.Square)
                    B.mm(PS(bss)[0:16, bss, 0:N], oh16[:, c, :], sqs(k2)[:, k2, :N], start=(c == 0), stop=(c == 15))

        pair = []
        pend2 = None
        for blk in INPROJ_ORDER:
            wv = W.get()
            b = bank()
            for c in range(8):
                B.mm(psb(b, 0, N), wv(c * 128, c * 128 + 128), u(c)[:, c, :N], start=(c == 0), stop=(c == 7))
            W.done()
            if blk == 24:
                stage("ip_blk24")
            if blk == 44:
                stage("ip_blk44")
            if blk < 24:
                pair.append((blk, b))
                if len(pair) == 4:
                    if pend2 is not None:
                        qkv_post2(*pend2)
                    pend2 = qkv_post(pair)
                    pair = []
                continue
            if True:
                if blk < 32:
                    c = blk - 24
                    B.act(shr(c)[:, c, :N], psb(b, 0, N), AF.Silu)
                elif blk < 40:
                    c = blk - 32
                    B.copy(shr2(c)[:, c, :N], psb(b, 0, N), eng="act")
                elif blk < 44:
                    i4 = blk - 40
                    if i4 < 2:
                        B.copy(shr(24 + i4)[:, 24 + i4, :N], psb(b, 0, N), eng="act")
                    else:
                        B.copy(svb_t(i4 - 2)[:, i4 - 2, :N], psb(b, 0, N), eng="act")
                    if kind == "s":
                        B.copy(kvf(i4)[:, i4, :], psb(b, 0, 128), eng="act")
                    elif m == 0:
                        if not cfg.get("no_kvf"):
                            B.copy(kvf(i4)[:, i4, :], psb(b, 0, 128), eng="act")
                        if not cfg.get("no_pm"):
                            B.dma(pmeta_o[i4], kvf(i4)[:, i4, :], "pm%d" % i4)
                    elif m == nmac - 1:
                        B.copy(kvf(i4)[:, i4, :], psb(b, N - 128, N), eng="act")
                        B.dma(pwin_o[i4], kvf(i4)[:, i4, :], "pm%d" % i4)
                else:
                    c = blk - 44
                    k = c % 2
                    B.act(sg(k)[:, k, :N], psb(b, 0, N), AF.Tanh, scale=0.5)
                    B.ts(shr(8 + c)[:, 8 + c, :N], sg(k)[:, k, :N], 0.5, ALU.mult, 0.5, ALU.add)
        if pend2 is not None:
            qkv_post2(*pend2)
        stage("ip_ba")
        wv = W.get()
        b = bank()
        for s in range(nsub):
            for c in range(8):
                B.mm(psb(b, s * 16, s * 16 + 16), u(c)[:, c, s * 128:(s + 1) * 128], wv(c * 16, c * 16 + 16),
                     start=(c == 0), stop=(c == 7))
        W.done()
        B.copy(ba[:, 0:nsub, :], psb(b, 0, nsub * 16).m(lambda a: a.rearrange("p (s x) -> p s x", x=16)), eng="act")
        B.act(ms[0:16, :N], PS(bss)[0:16, bss, 0:N], AF.Ln, bias=epsb[0:16, 0:1])
        B.act(rinv[0:16, :N], ms[0:16, :N], AF.Exp, scale=-0.5)
        reserved.discard(bss)
        for c in range(16):
            which = 1 if c < 8 else 0
            hd_i = c % 8
            k = c % 2
            B.ts(ms2(k)[0:16, k, :N], rinv[0:16, :N], cm[0:16, 7, c:c + 1], ALU.mult)
            bb = bank()
            B.mm(psb(bb, 0, N), cm[0:16, 8, :], ms2(k)[0:16, k, :N])
            sc = (128.0 ** -0.5) if which == 1 else 1.0
            B.stt(qk((which, hd_i))[:, which, hd_i, :N], qk((which, hd_i))[:, which, hd_i, :N], sc, psb(bb, 0, N),
                  ALU.mult, ALU.mult)
        B.act(beta[:, 0:nsub, :], ba[:, 0:nsub, 0:8], AF.Exp, scale=-1.0)
        B.act(beta[:, 0:nsub, :], beta[:, 0:nsub, :], AF.Ln, bias=1.0)
        B.act(beta[:, 0:nsub, :], beta[:, 0:nsub, :], AF.Exp, scale=-1.0)
        B.tt(gg[:, 0:nsub, :], ba[:, 0:nsub, 8:16], dtb.bc(1, nsub), ALU.add)
        B.act(ge[:, 0:nsub, :], gg[:, 0:nsub, :], AF.Exp)
        B.act(gg[:, 0:nsub, :], ge[:, 0:nsub, :], AF.Ln, bias=1.0)
        B.act(gx[:, 0:nsub, :], gg[:, 0:nsub, :], AF.Exp, scale=-1.0)
        B.stt(gx[:, 0:nsub, :], ge[:, 0:nsub, :], 1.0, gx[:, 0:nsub, :], ALU.add, ALU.mult)
        B.stt(gg[:, 0:nsub, :], gx[:, 0:nsub, :], -1.0, gg[:, 0:nsub, :], ALU.add, ALU.add)
        B.tt(gg[:, 0:nsub, :], gg[:, 0:nsub, :], negA[:, :].bc(1, nsub), ALU.mult)

    snrot = [0]

    def dn(kind, sl, gs):
        cs = slice(sl * 128, sl * 128 + 128)
        smp = kind == "s"
        Mincl = cm[:, 3, :] if smp else cm[:, 0, :]
        Mgt = cm[:, 4, :] if smp else cm[:, 1, :]
        maskS = cm[:, 5, :] if smp else cm[:, 2, :]
        Mall = cm[:, 6, :] if smp else cm[:, 8, :]
        nlev = 1 if smp else 5
        g_s = gg[:, sl, :]
        bsm = bank()
        B.mm(psb(bsm, 0, 8), Mincl, g_s)
        B.mm(psb(bsm, 8, 16), Mall, g_s)
        gc = sm[:, 0, :]
        egc = sm[:, 1, :]
        negegc = sm[:, 2, :]
        kd = sm[:, 3, :]
        kscale = sm[:, 4, :]
        cdec = sm[:, 5, :]
        B.copy(gc, psb(bsm, 0, 8))
        B.act(egc, psb(bsm, 0, 8), AF.Exp)
        B.ts(negegc, egc, -1.0, ALU.mult)
        B.tt(kd, psb(bsm, 8, 16), gc, ALU.subtract)
        B.act(kscale, kd, AF.Exp)
        if not smp:
            B.act(cdec, psb(bsm, 8, 16), AF.Exp)
        else:
            B.tt(Rs[:, :, :], g_s.bc(1, NSEQ), seqsel.bc(2, 8), ALU.mult)
            b2 = bank()
            B.mm(psb(b2, 0, 128), ONES_f, Rs[:, :, :].m(lambda a: a.rearrange("p n h -> p (n h)")))
            B.act(cdec_s[:, :, :].m(lambda a: a.rearrange("p n h -> p (n h)")), psb(b2, 0, 128), AF.Exp)
            xb = [bank() for _ in range(4)]
            for n in range(NSEQ):
                r = snrot[0] % 2
                snrot[0] += 1
                B.dma(Sn(r)[:, r, :, :], s_ssm_in[n].rearrange("h k v -> k h v"), "sn%d" % r, eng="pool")
                for hh in range(8):
                    for w in range(2):
                        c0 = (hh % 2) * 256 + w * 128 + n * 8
                        B.mm(PS(xb[hh // 2])[:, xb[hh // 2], c0:c0 + 8], Sn(r)[:, r, hh, :],
                             qk[:, w, hh, n * 8:n * 8 + 8])
            for i in range(4):
                B.copy(KQT[:, 2 * i:2 * i + 2, :, :].m(lambda a: a.rearrange("p h w t -> p (h w t)")), psb(xb[i]),
                       eng=("act" if i % 2 else "dve"))

        def grp(G, T):
            tA, tB, tC, decI, PU, PL, XU, XL, qkTm, r0, uu, Vtok, kdec, on_t, ssum = T
            o_t = tA
            hs = [4 * G + i for i in range(4)]
            bt = beta[:, sl, 4 * G:4 * G + 4]
            B.tt(tA[:, :, :], Mgt.bc(1, 4), g_s.m(lambda a: a[:, 4 * G:4 * G + 4]).bc(2, 128), ALU.mult, eng="pool")
            bD = bank()
            for i in range(4):
                B.mm(psb(bD, i * 128, i * 128 + 128), tA[:, i, :], Mincl)
            B.act(decI[:, :, :].m(lambda a: a.rearrange("p h n -> p (h n)")), psb(bD), AF.Exp)
            B.tt(decI[:, :, :], decI[:, :, :], Mincl.bc(1, 4), ALU.mult, eng="pool")
            yield
            bK = bank()
            bQ = bank()
            for i, hh in enumerate(hs):
                B.mm(psb(bK, i * 128, i * 128 + 128), qk[:, 0, hh, cs], qk[:, 0, hh, cs])
            for i, hh in enumerate(hs):
                B.mm(psb(bQ, i * 128, i * 128 + 128), qk[:, 0, hh, cs], qk[:, 1, hh, cs])
            f3 = lambda a: a.rearrange("p (h n) -> p h n", h=4)
            B.tt(qkTm[:, :, :], psb(bQ).m(f3), decI[:, :, :], ALU.mult)
            B.tt(tB[:, :, :], psb(bK).m(f3), decI[:, :, :], ALU.mult)
            B.tt(tC[:, :, :], maskS.bc(1, 4), bt.bc(2, 128), ALU.mult, eng="pool")
            B.tt(tB[:, :, :], tB[:, :, :], tC[:, :, :], ALU.mult, eng="pool")
            B.copy(PU[:, :, :], tB[:, :, :], eng="act")
            yield
            bT = bank()
            tv = psbf(bT, 512).m(f3)
            for i in range(4):
                B.tr(V(PS, tv.ap[:, i, :], (bT,)), PU[:, i, :], ident_b[:, :])
            B.copy(PL[:, :, :], tv, eng="act")
            B.tt(XU[:, :, :], ident_b[:, :].bc(1, 4), PU[:, :, :], ALU.subtract)
            B.tt(XL[:, :, :], ident_b[:, :].bc(1, 4), PL[:, :, :], ALU.subtract, eng="pool")
            yield
            b1 = bank()
            b2 = bank()
            for i in range(4):
                B.mm(psb(b1, i * 128, i * 128 + 128), PL[:, i, :], PU[:, i, :])
            for i in range(4):
                B.mm(psb(b2, i * 128, i * 128 + 128), PU[:, i, :], PL[:, i, :])
            yield
            B.copy(PU[:, :, :], psb(b1).m(f3), eng="act")
            B.copy(PL[:, :, :], psb(b2).m(f3))
            yield
            for lv in range(nlev):
                last = lv == nlev - 1
                b3 = bank()
                b4 = bank()
                for i in range(4):
                    B.mm(psb(b3, i * 128, i * 128 + 128), XL[:, i, :], PU[:, i, :])
                for i in range(4):
                    B.mm(psb(b4, i * 128, i * 128 + 128), XU[:, i, :], PL[:, i, :])
                if not last:
                    b1 = bank()
                    b2 = bank()
                    for i in range(4):
                        B.mm(psb(b1, i * 128, i * 128 + 128), PL[:, i, :], PU[:, i, :])
                    for i in range(4):
                        B.mm(psb(b2, i * 128, i * 128 + 128), PU[:, i, :], PL[:, i, :])
                yield
                B.tt(XU[:, :, :], XU[:, :, :], psb(b3).m(f3), ALU.add)
                B.tt(XL[:, :, :], XL[:, :, :], psb(b4).m(f3), ALU.add, eng="dve")
                if not last:
                    B.copy(PU[:, :, :], psb(b1).m(f3), eng="act")
                    B.copy(PL[:, :, :], psb(b2).m(f3), eng="act")
                yield
            bA = bank()
            for i in range(4):
                B.tr(psb(bA, i * 128, i * 128 + 128), tB[:, i, :], I_f)
            B.tt(tC[:, :, :], psb(bA).m(f3), I_f.bc(1, 4), ALU.add)
            B.copy(decI[:, :, :], XU[:, :, :], eng="act")
            yield
            bR = bank()
            for i in range(4):
                B.mm(psb(bR, i * 128, i * 128 + 128), tC[:, i, :], decI[:, i, :])
            B.tt(PU[:, :, :], I_f.bc(1, 4), psb(bR).m(f3), ALU.subtract)
            yield
            bX = bank()
            for i in range(4):
                B.mm(psb(bX, i * 128, i * 128 + 128), XL[:, i, :], PU[:, i, :])
            B.tt(XU[:, :, :], XU[:, :, :], psb(bX).m(f3), ALU.add)
            yield
            bKS = bank()
            bQS = bank()
            if not smp:
                for i, hh in enumerate(hs):
                    B.mm(psb(bKS, i * 128, i * 128 + 128), qk[:, 0, hh, cs], Sbf(G)[:, hh, :])
                for i, hh in enumerate(hs):
                    B.mm(psb(bQS, i * 128, i * 128 + 128), qk[:, 1, hh, cs], Sbf(G)[:, hh, :])
                KSv = psb(bKS).m(f3)
                QSv = psb(bQS).m(f3)
            else:
                kv_ = psbf(bKS, 512).m(f3)
                qv_ = psbf(bQS, 512).m(f3)
                for i, hh in enumerate(hs):
                    B.tr(V(PS, kv_.ap[:, i, :], (bKS,)), KQT[:, hh, 0, :], ident_b[:, :])
                    B.tr(V(PS, qv_.ap[:, i, :], (bQS,)), KQT[:, hh, 1, :], ident_b[:, :])
                KSv = kv_
                QSv = qv_
            bV = bank()
            vv = psbf(bV, 512).m(f3)
            for i, hh in enumerate(hs):
                B.tr(V(PS, vv.ap[:, i, :], (bV,)), vT[:, hh, cs], ident_b[:, :])
            B.copy(Vtok[:, :, :], vv, eng="act")
            yield
            B.tt(tB[:, :, :], KSv, negegc.m(lambda a: a[:, 4 * G:4 * G + 4]).bc(2, 128), ALU.mult)
            B.tt(r0[:, :, :], tB[:, :, :], Vtok[:, :, :], ALU.add, eng="pool")
            yield
            bU = bank()
            for i in range(4):
                B.mm(psb(bU, i * 128, i * 128 + 128), XU[:, i, :], r0[:, i, :])
            B.tt(uu[:, :, :], psb(bU).m(f3), bt.bc(2, 128), ALU.mult)
            yield
            bO = bank()
            for i in range(4):
                B.mm(psb(bO, i * 128, i * 128 + 128), qkTm[:, i, :], uu[:, i, :])
            B.tt(tB[:, :, :], QSv, egc.m(lambda a: a[:, 4 * G:4 * G + 4]).bc(2, 128), ALU.mult)
            B.tt(o_t[:, :, :], tB[:, :, :], psb(bO).m(f3), ALU.add)
            yield
            B.tt(tC[:, :, :], o_t[:, :, :], o_t[:, :, :], ALU.mult, eng="pool")
            B.emit("dve", lambda e: e.tensor_reduce(ssum.t[:, 0, :], tC.t[:, :, :], AX.X, ALU.add),
                   outs=[ssum[:, 0, :]], ins=[tC[:, :, :]])
            B.act(ssum[:, 1, :], ssum[:, 0, :], AF.Ln, scale=1.0 / 128.0, bias=epsb[:, 0:1])
            B.act(ssum[:, 2, :], ssum[:, 1, :], AF.Exp, scale=-0.5)
            B.tt(on_t[:, :, :], o_t[:, :, :], ssum[:, 2, :].bc(2, 128), ALU.mult, eng="pool")
            yield
            bN = bank()
            nv = psbf(bN, 512).m(f3)
            for i in range(4):
                B.tr(V(PS, nv.ap[:, i, :], (bN,)), on_t[:, i, :], ident_b[:, :])
            zsv = shr[:, 4 * G:4 * G + 4, cs]
            gdv = shr[:, 8 + 4 * G:8 + 4 * G + 4, cs]
            B.stt(tB[:, :, :], nv, dnw, zsv, ALU.mult, ALU.mult)
            B.tt(ybuf[:, 4 * G:4 * G + 4, cs], tB[:, :, :], gdv, ALU.mult, eng="pool")
            yield
            bKd = bank()
            kdv = psbf(bKd, 512).m(f3)
            for i, hh in enumerate(hs):
                B.tr(V(PS, kdv.ap[:, i, :], (bKd,)), qk[:, 0, hh, cs], ident_b[:, :])
            B.tt(kdec[:, :, :], kdv, kscale.m(lambda a: a[:, 4 * G:4 * G + 4]).bc(2, 128), ALU.mult)
            yield
            if not smp:
                bS = bank()
                for i in range(4):
                    B.mm(psb(bS, i * 128, i * 128 + 128), kdec[:, i, :], uu[:, i, :])
                Sg = S(G)[:, 4 * G:4 * G + 4, :]
                B.tt(Sg, Sg, cdec.m(lambda a: a[:, 4 * G:4 * G + 4]).bc(2, 128), ALU.mult, eng="pool")
                B.tt(Sg, Sg, psb(bS).m(f3), ALU.add)
                B.copy(Sbf(G)[:, 4 * G:4 * G + 4, :], Sg, eng="act")
                yield
            else:
                for i, hh in enumerate(hs):
                    B.tt(Uexp[:, :, :], uu[:, i, :].bc(1, NSEQ), seqsel.bc(2, 128), ALU.mult)
                    for q4 in range(4):
                        bS = bank()
                        B.mm(psb(bS), kdec[:, i, :],
                             Uexp[:, 4 * q4:4 * q4 + 4, :].m(lambda a: a.rearrange("p n d -> p (n d)")))
                        k2 = (hh * 4 + q4) % 2
                        X = (Sold, Snew)[k2]
                        B.dma(X[:, :, :], s_ssm_in[4 * q4:4 * q4 + 4, hh].rearrange("n k v -> k n v"), "so%d" % k2)
                        B.tt(X[:, :, :], X[:, :, :], cdec_s[:, 4 * q4:4 * q4 + 4, hh].bc(2, 128), ALU.mult,
                             eng="pool")
                        B.tt(X[:, :, :], X[:, :, :], psb(bS).m(f3), ALU.add)
                        B.dma(sssm_o[4 * q4:4 * q4 + 4, hh].rearrange("n k v -> k n v"), X[:, :, :], "sw%d" % k2)


        if smp:
            for G in range(2):
                for _ in grp(G, TS[0]):
                    pass
        else:
            if gs >= 1:
                gens = [itertools.chain(grp(0, TS[0]), swa_p(sl, gs, 0, TS[0])),
                        itertools.chain(grp(1, TS[1]), swa_p(sl, gs, 1, TS[1]))]
            else:
                gens = [grp(0, TS[0]), grp(1, TS[1])]
            while gens:
                for gen in list(gens):
                    try:
                        next(gen)
                    except StopIteration:
                        gens.remove(gen)
    def swa_p(sl, gs, g, T):
        cs = slice(sl * 128, sl * 128 + 128)
        par = gs % 2
        f3 = lambda a: a.rearrange("p (h n) -> p h n", h=4)
        fl = lambda a: a.rearrange("p h n -> p (h n)")
        scale = 128.0 ** -0.5
        lg_, rden_, PTo_, PTp_ = T[1], T[2], T[4], T[5]
        PTm_ = T[6]
        sk_cur = shr[:, 24 + g, cs]
        sq_g = shr2[:, 4 * g:4 * g + 4, cs]
        use_prev = gs >= 2
        bt_ = bank()
        tv = psbf(bt_, 128)
        B.tr(tv, svb_t[:, g, cs], ident_b[:, :])
        bO = bank()
        B.mm(psb(bO).m(f3), sk_cur, sq_g)
        if use_prev:
            bP = bank()
            kprev = shr[:, 24 + g, (sl - 1) * 128:sl * 128] if sl > 0 else skprev[:, g, :]
            B.mm(psb(bP).m(f3), kprev, sq_g)
        bM = bank()
        B.mm(PS(bM)[0:16, bM, :].m(f3), kmT[:, g, :], sq_g)
        yield
        B.copy(vsw[:, par, g, :], tv, eng="act")
        if g == 0 and gs <= 2:
            B.dma(cbm[:, :, :], cbiasm[:, 1 if gs == 1 else 2], "c4", eng="pool")
        B.stt(lg_[:, :, :], psb(bO).m(f3), scale, cb[:, 1, 4 * g:4 * g + 4, :], ALU.mult, ALU.add)
        B.act(PTo_[:, :, :], lg_[:, :, :], AF.Exp)
        if use_prev:
            B.stt(lg_[:, :, :], psb(bP).m(f3), scale, cb[:, 0, 4 * g:4 * g + 4, :], ALU.mult, ALU.add)
            B.act(PTp_[:, :, :], lg_[:, :, :], AF.Exp)
        B.stt(lg_[0:16, :, :], PS(bM)[0:16, bM, :].m(f3), scale, cbm[:, 4 * g:4 * g + 4, :], ALU.mult, ALU.add)
        B.act(PTm_[0:16, :, :], lg_[0:16, :, :], AF.Exp)
        yield
        bV = bank()
        bDn = bank()
        pvl = [(vsw[:, par, g, :], ones_b[:, :], PTo_[:, :, :].m(fl))]
        if use_prev:
            pvl.append((vsw[:, 1 - par, g, :], ones_b[:, :], PTp_[:, :, :].m(fl)))
        pvl.append((vm[:, g, :], ones_b[0:16, :], PTm_[0:16, :, :].m(fl)))
        for i, (lv_, lo_, r_) in enumerate(pvl):
            B.mm(psb(bDn), lo_, r_, start=(i == 0), stop=(i == len(pvl) - 1))
        for i, (lv_, lo_, r_) in enumerate(pvl):
            B.mm(psb(bV), lv_, r_, start=(i == 0), stop=(i == len(pvl) - 1))
        yield
        B.tt(rden_[:, :, :], psb(bDn).m(f3), esink[:, 4 * g:4 * g + 4].bc(2, 128), ALU.add)
        B.act(rden_[:, :, :], rden_[:, :, :], AF.Ln)
        B.act(rden_[:, :, :], rden_[:, :, :], AF.Exp, scale=-1.0)
        B.tt(lg_[:, :, :], psb(bV).m(f3), rden_[:, :, :], ALU.mult)
        B.tt(lg_[:, :, :], lg_[:, :, :], shr[:, 16 + 4 * g:16 + 4 * g + 4, cs], ALU.mult, eng="pool")
        yv = ybuf[:, 4 * g:4 * g + 4, cs]
        B.tt(yv, yv, lg_[:, :, :], ALU.add, eng="pool")
        yield

    def swa(kind, sl, gs):
        cs = slice(sl * 128, sl * 128 + 128)
        smp = kind == "s"
        par = gs % 2
        f3 = lambda a: a.rearrange("p (h n) -> p h n", h=4)
        fl = lambda a: a.rearrange("p h n -> p (h n)")
        scale = 128.0 ** -0.5
        for g in range(2):
            sk_cur = shr[:, 24 + g, cs]
            sq_g = shr2[:, 4 * g:4 * g + 4, cs]
            bt_ = bank()
            tv = psbf(bt_, 128)
            B.tr(tv, svb_t[:, g, cs], ident_b[:, :])
            B.copy(vsw[:, par, g, :], tv, eng="act")
            if kind == "p" and gs == 0:
                B.copy(kmT[:, g, :], shr[:, 24 + g, 112:128], eng="act")
                bt2 = bank()
                tv2 = PS(bt2)[0:16, bt2, 0:64].bitcast(BF16)
                B.tr(tv2, svb_t[:, g, 112:128], ident_b[:, :])
                B.copy(vm[:, g, :], tv2, eng="act")
            if smp:
                B.dma(kmTs[:, :, :], kmT_in[:, g].rearrange("n d s -> d n s"), "c6", eng="pool")
                B.dma(vms[:, :, :], vm_in[:, :, g, :].rearrange("n s d -> s n d"), "c7", eng="pool")
            use_own = smp or gs >= 1
            use_prev = smp or gs >= 2
            var = 2 if smp else (0 if gs == 0 else (1 if gs == 1 else 2))
            if g == 0 and (smp or gs <= 2):
                B.dma(cbm[:, :, :], cbiasm[:, var], "c4", eng="pool")
            bV = bank()
            bDn = bank()
            pvl = []
            dnl = []

            if use_own:
                bO = bank()
                B.mm(psb(bO).m(f3), sk_cur, sq_g)
                bi = cb[:, 1, 4 * g:4 * g + 4, :]
                B.stt(lg[:, :, :], psb(bO).m(f3), scale, bi, ALU.mult, ALU.add)
                B.act(PTo[:, :, :], lg[:, :, :], AF.Exp)
                pvl.append((psb(bV), vsw[:, par, g, :], PTo[:, :, :].m(fl)))
                dnl.append((psb(bDn), ones_b[:, :], PTo[:, :, :].m(fl)))
            if use_prev:
                bP = bank()
                if not smp:
                    kprev = shr[:, 24 + g, (sl - 1) * 128:sl * 128] if sl > 0 else skprev[:, g, :]
                    B.mm(psb(bP).m(f3), kprev, sq_g)
                else:
                    for n in range(NSEQ):
                        sl4 = (n // 4) % 2
                        if n % 4 == 0:
                            B.dma(kbT(sl4)[:, sl4, :, :], kbT_in[n:n + 4, g].rearrange("n d w -> d n w"),
                                  "kb%d" % sl4, eng="pool")
                        for hh in range(4):
                            B.mm(PS(bP)[:, bP, hh * 128 + n * 8:hh * 128 + n * 8 + 8], kbT(sl4)[:, sl4, n % 4, :],
                                 shr2[:, 4 * g + hh, n * 8:n * 8 + 8],
                                 start=(n == 0 and hh == 0), stop=(n == NSEQ - 1 and hh == 3))
                bi = cb[:, 0, 4 * g:4 * g + 4, :]
                B.stt(lg[:, :, :], psb(bP).m(f3), scale, bi, ALU.mult, ALU.add)
                B.act(PTp[:, :, :], lg[:, :, :], AF.Exp)
                dnl.append((psb(bDn), ones_b[:, :], PTp[:, :, :].m(fl)))
                if not smp:
                    pvl.append((psb(bV), vsw[:, 1 - par, g, :], PTp[:, :, :].m(fl)))
                else:
                    for n in range(NSEQ):
                        for hh in range(4):
                            o = PS(bV)[:, bV, hh * 128 + n * 8:hh * 128 + n * 8 + 8]
                            pvl.append((o, ("vb", n), PTp[:, hh, n * 8:n * 8 + 8]))
            bM = bank()
            if not smp:
                B.mm(PS(bM)[0:16, bM, :].m(f3), kmT[:, g, :], sq_g)
            else:
                for n in range(NSEQ):
                    for hh in range(4):
                        B.mm(PS(bM)[0:16, bM, hh * 128 + n * 8:hh * 128 + n * 8 + 8], kmTs[:, n, :],
                             shr2[:, 4 * g + hh, n * 8:n * 8 + 8],
                             start=(n == 0 and hh == 0), stop=(n == NSEQ - 1 and hh == 3))
            B.stt(lg[0:16, :, :], PS(bM)[0:16, bM, :].m(f3), scale, cbm[:, 4 * g:4 * g + 4, :], ALU.mult, ALU.add)
            B.act(PTm[:, :, :], lg[0:16, :, :], AF.Exp)
            dnl.append((psb(bDn), ones_b[0:16, :], PTm[:, :, :].m(fl)))
            if not smp:
                pvl.append((psb(bV), vm[:, g, :], PTm[:, :, :].m(fl)))
            else:
                for n in range(NSEQ):
                    for hh in range(4):
                        o = PS(bV)[:, bV, hh * 128 + n * 8:hh * 128 + n * 8 + 8]
                        pvl.append((o, vms[:, n, :], PTm[:, hh, n * 8:n * 8 + 8]))
            for i, (o, l, r) in enumerate(dnl):
                B.mm(o, l, r, start=(i == 0), stop=(i == len(dnl) - 1))
            for i, (o, l, r) in enumerate(pvl):
                if isinstance(l, tuple):
                    n = l[1]
                    sl4 = (n // 4) % 2
                    if n % 4 == 0 and (i == 0 or pvl[i - 1][1] != l):
                        B.dma(vb(sl4)[:, sl4, :, :], vb_in[n:n + 4, :, g, :].rearrange("n w d -> w n d"),
                              "vb%d" % sl4, eng="pool")
                    l = vb(sl4)[:, sl4, n % 4, :]
                B.mm(o, l, r, start=(i == 0), stop=(i == len(pvl) - 1))
            B.tt(rden[:, :, :], psb(bDn).m(f3), esink[:, 4 * g:4 * g + 4].bc(2, 128), ALU.add)
            B.act(rden[:, :, :], rden[:, :, :], AF.Ln)
            B.act(rden[:, :, :], rden[:, :, :], AF.Exp, scale=-1.0)
            B.tt(lg[:, :, :], psb(bV).m(f3), rden[:, :, :], ALU.mult)
            B.tt(lg[:, :, :], lg[:, :, :], shr[:, 16 + 4 * g:16 + 4 * g + 4, cs], ALU.mult, eng="pool")
            yv = ybuf[:, 4 * g:4 * g + 4, cs]
            B.tt(yv, yv, lg[:, :, :], ALU.add, eng="pool")

    def mixer(kind, m, N, out_ap=None):
        nsub = N // 128
        barrier_shr()
        if kind == "s":
            B.emit("pool", lambda e: e.memset(tmpn.t[0:1, 1, 0:1], 0.0), outs=[V(R1, R1.t[:, :], (None,)), tmpn(1)[0:1, 1, 0:1]])
            B.dma(V(R1, shist.t[:, :, :], (None,)), shistT.rearrange("(c p) x -> p c x", p=128), "c5")
            B.dma(cb[:, 0, :, :], cbias[:, 3], "c3", eng="pool")
            B.dma(cb[:, 1, :, :], cbias[:, 2], "c3b", eng="pool")
        inproj(kind, m, N)
        if kind == "s":
            B.dma(sconv_o.rearrange("(c p) x -> p c x", p=128), sconv[:, :, :], "sc")
            bt_ = bank()
            for i4 in range(4):
                B.tr(psb(bt_, i4 * 128, i4 * 128 + 128), kvf[:, i4, :], cm[:, 7, :])
            B.copy(kvtok[:, :, :], psb(bt_).m(lambda a: a.rearrange("p (h n) -> p h n", h=4)))
            B.dma(swink_o[:, 0:120], kb_in[:, 8:128], "wk0")
            B.dma(swinv_o[:, 0:120], vb_in[:, 8:128], "wv0")
            for n in range(NSEQ):
                B.dma(swink_o[n, 120:128], kvtok[n * 8:n * 8 + 8, 0:2, :], "wk1")
                B.dma(swinv_o[n, 120:128], kvtok[n * 8:n * 8 + 8, 2:4, :], "wv1")
        for sl in range(nsub):
            gs = m * NSUB + sl
            if "dn" not in cfg.get("skip", ()):
                dn(kind, sl, gs)
            if "swa" not in cfg.get("skip", ()) and (kind == "s" or gs == 0):
                swa(kind, sl, gs)
        if kind == "p":
            for g in range(2):
                B.copy(skprev[:, g, :], shr[:, 24 + g, N - 128:N], eng="act")
        stage("outproj")
        st = Stats(N)
        for d in range(8):
            wv = W.get()
            bo = bank()
            for c in range(8):
                B.mm(psb(bo, 0, N), wv(c * 128, c * 128 + 128), ybuf[:, c, :N], start=(c == 0), stop=(c == 7))
            W.done()
            B.copy(yo(d)[:, d, :N], psb(bo, 0, N), eng="act")
            st.add(psb(bo, 0, N))
        postnorm_add(3, N, False, out_ap, st)
        barrier_shr()

    for kind, m in tiles:
        if kind == "p":
            N = NT
            src = xpT[:, m * NT:(m + 1) * NT]
            dst = ypT[:, m * NT:(m + 1) * NT]
        else:
            N = 128
            src = xsT
            dst = ysT
        for c in range(8):
            B.dma(h(c)[:, c, :N], src[c * 128:(c + 1) * 128, :], "xin%d" % c)
        last_is_ffn2 = do_ffn2
        ffn(0, N, None if (do_mixer or do_ffn2) else dst)
        stage("ffn1_done")
        if do_mixer:
            mixer(kind, m, N, None if do_ffn2 else dst)
        if do_ffn2:
            ffn(1, N, dst)
        if do_mixer and kind == "p" and m == nmac - 1:
            pcv = pconv_o.rearrange("(c p) j -> p c j", p=128)
            for q3 in range(3):
                B.dma(pcv[:, 8 * q3:8 * q3 + 8, :], hist[:, 8 * q3:8 * q3 + 8, :], "pc%d" % q3)
            B.dma(pssm_o.rearrange("h k v -> k h v"), S[:, :, :], "pss")

    B.finalize()
    B.close()
    return nc


def _tile_w(w):
    R, C = w.shape
    return np.ascontiguousarray(w.reshape(R // 128, 128, C // 128, 128).transpose(2, 1, 0, 3)).reshape(C // 128, 128, R)


def _consts():
    f = np.float32
    idx = np.arange(128)
    s_ = idx[:, None]
    i_ = idx[None, :]
    same = (s_ // 8) == (i_ // 8)
    cm = np.zeros((128, 9, 128), f)
    cm[:, 0] = s_ <= i_
    cm[:, 1] = s_ > i_
    cm[:, 2] = s_ < i_
    cm[:, 3] = same & (s_ <= i_)
    cm[:, 4] = same & (s_ > i_)
    cm[:, 5] = same & (s_ < i_)
    cm[:, 6] = same
    cm[:, 7] = np.eye(128)
    cm[:, 8] = 1.0
    slopes = (2.0 ** (-(np.arange(8) + 1.0))).astype(np.float64)
    j = idx[:, None, None].astype(np.float64)
    hh = slopes[None, :, None]
    i = idx[None, None, :].astype(np.float64)
    cb = np.zeros((128, 4, 8, 128), f)
    cb[:, 0] = np.where(i <= j, -hh * (i + 128 - j), NEG)
    cb[:, 1] = np.where(i >= j, -hh * (i - j), NEG)
    samej = (idx[:, None, None] // 8) == (idx[None, None, :] // 8)
    cb[:, 2] = np.where(samej & (i >= j), -hh * (i - j), NEG)
    t = (idx[None, None, :] % 8).astype(np.float64)
    cb[:, 3] = np.where(j >= t, -hh * (t + 128 - j), NEG)
    sp = np.arange(16)[:, None, None].astype(np.float64)
    cbm = np.zeros((16, 3, 8, 128), f)
    pos0 = i - 112
    cbm[:, 0] = np.where(pos0 >= sp, -hh * np.minimum(pos0 - sp, 128), NEG)
    pos1 = i + 16
    cbm[:, 1] = -hh * np.minimum(pos1 - sp, 128) + 0 * sp
    cbm[:, 2] = -hh * 128.0 + 0 * sp + 0 * i
    return cm, cb, cbm


def prep_inputs(inp, cfg):
    f = np.float32
    A = lambda k: np.asarray(inp[k], f)
    wg, wd = [], []
    for nm in ("ffn1", "ffn2"):
        g = _tile_w(A(nm + "_w_gate")[0])
        up = _tile_w(A(nm + "_w_up")[0])
        wg.append(np.ascontiguousarray(np.stack([g, up], axis=1).reshape(2 * NFF, 128, 1024)))
        wd.append(_tile_w(A(nm + "_w_down")[0]))
    gl = [A(k)[0] for k in ("ffn1_norm_pre", "ffn1_norm_post", "mix_norm_pre", "mix_norm_post",
                            "ffn2_norm_pre", "ffn2_norm_post")]
    gains = np.ascontiguousarray(np.stack([g.reshape(8, 128).T for g in gl], axis=1))
    w_in = A("w_in")[0]
    cols = np.concatenate([np.arange(0, 4096), np.arange(4112, 7696), np.arange(4096, 4112)])
    w_in = w_in[:, cols]
    win = _tile_w(np.ascontiguousarray(w_in[:, :7680]))
    wba = np.ascontiguousarray(w_in[:, 7680:].reshape(8, 128, 16).transpose(1, 0, 2)).reshape(128, 128)
    wout = _tile_w(A("w_out")[0])
    cm, cb, cbm = _consts()
    vecs = np.zeros((128, NV), f)
    cw = A("dn_conv_w")[0]
    vecs[:, 0:96] = cw.reshape(4, 24, 128).transpose(2, 1, 0).reshape(128, 96)
    vecs[:, 96] = A("dn_norm_w")[0]
    vecs[:, 97:105] = A("dn_a_log")[0][None, :]
    vecs[:, 105:113] = A("dn_dt_bias")[0][None, :]
    vecs[:, 113:121] = A("swa_sinks")[0][None, :]
    vecs[:, 121:137] = (np.arange(128)[:, None] // 8) == np.arange(16)[None, :]
    xp = A("x_prompt")
    xs = A("x_sample")
    meta = A("meta_tokens")
    sconv = A("state_dn_conv")[0]
    sssm = A("state_dn_ssm")[0]
    cmk, cmv = A("cache_swa_meta_k")[0], A("cache_swa_meta_v")[0]
    ck, cv = A("cache_swa_k")[0], A("cache_swa_v")[0]
    maps = []
    for c in range(NCORES):
        m = {}
        seq = np.zeros((TP, D), f)
        if c < 4:
            seq[PADF:PADF + NMETA] = meta
            seq[PADF + NMETA:] = xp[c]
        m["xpT"] = np.ascontiguousarray(seq.T)
        sl = slice(NSEQ * c, NSEQ * (c + 1))
        m["xsT"] = np.ascontiguousarray(xs[sl].reshape(128, D).T)
        m["wgu1"], m["wgu2"] = wg
        m["wdn1"], m["wdn2"] = wd
        m["win"], m["wba"], m["wout"] = win, wba, wout
        m["gains"], m["cmask"], m["cbias"], m["cbiasm"], m["vecs"] = gains, cm, cb, cbm, vecs
        m["shistT"] = np.ascontiguousarray(sconv[sl].transpose(2, 0, 1)).reshape(3072, NSEQ * 3)
        m["s_ssm_in"] = np.ascontiguousarray(sssm[sl])
        m["kb_in"] = np.ascontiguousarray(ck[sl])
        m["kbT_in"] = np.ascontiguousarray(ck[sl].transpose(0, 2, 3, 1))
        m["kmT_in"] = np.ascontiguousarray(cmk[sl].transpose(0, 2, 3, 1))
        m["vb_in"] = np.ascontiguousarray(cv[sl])
        m["vm_in"] = np.ascontiguousarray(cmv[sl])
        maps.append(m)
    return maps


_CACHE = {}


def run_device(inp, cfg):
    key = repr(sorted(cfg.items()))
    if key not in _CACHE:
        _CACHE[key] = build_program(cfg)
    nc = _CACHE[key]
    maps = prep_inputs(inp, cfg)
    res = run_bass_kernel_spmd(nc, maps, core_ids=list(range(NCORES)))
    return res.results


def assemble(r):
    f = np.float32
    y_p = np.stack([r[b]["ypT"][:, 128:].T for b in range(4)]).astype(f)
    y_s = np.concatenate([r[c]["ysT"].T.reshape(NSEQ, DEC_T, D) for c in range(NCORES)]).astype(f)
    p_conv = np.stack([r[b]["pconv_o"].T for b in range(4)])[None].astype(f)
    p_ssm = np.stack([r[b]["pssm_o"] for b in range(4)])[None].astype(f)

    def kv(name, i0, lo=0):
        return np.stack([np.stack([r[b][name][i0][:, lo:].T, r[b][name][i0 + 1][:, lo:].T], axis=1)
                         for b in range(4)])[None].astype(f)
    p_mk, p_mv = kv("pmeta_o", 0, 112), kv("pmeta_o", 2, 112)
    p_wk, p_wv = kv("pwin_o", 0), kv("pwin_o", 2)
    s_conv = np.concatenate([r[c]["sconv_o"].reshape(3072, NSEQ, 3).transpose(1, 2, 0) for c in range(NCORES)])[None]
    s_ssm = np.concatenate([r[c]["sssm_o"] for c in range(NCORES)])[None]
    s_wk = np.concatenate([r[c]["swink_o"] for c in range(NCORES)])[None]
    s_wv = np.concatenate([r[c]["swinv_o"] for c in range(NCORES)])[None]
    return (y_p, y_s, p_conv, p_ssm, p_mk, p_mv, p_wk, p_wv,
            np.ascontiguousarray(s_conv).astype(f), s_ssm.astype(f), s_wk.astype(f), s_wv.astype(f))


def kernel(**inp):
    r = run_device(inp, {})
    return assemble(r)
```

```python
import contextlib
import itertools
import numpy as np
import ml_dtypes
import concourse.bass as bass
import concourse.mybir as mybir
from concourse.bass_utils import run_bass_kernel_spmd

F32 = mybir.dt.float32
BF16 = mybir.dt.bfloat16
AF = mybir.ActivationFunctionType
ALU = mybir.AluOpType
AX = mybir.AxisListType

D = 1024
DFF = 2816
NFF = 22
NCORES = 8
SEQ = 4096
NMETA = 16
PADF = 112
TP = PADF + NMETA + SEQ
NSUBP = TP // 128
DEC_B = 128
DEC_T = 8
NSEQ = DEC_B // NCORES
RMS_EPS = 1e-6
L2_EPS = 1e-6
SLOT = 2816


class Buf:
    def __init__(self, name, t):
        self.name = name
        self.t = t
        self.st = {}
        self.excl = False
        self.defkey = None

    def __call__(self, *keys):
        return _KV(self, keys if keys else (None,))

    def __getitem__(self, idx):
        return V(self, self.t[idx], (self.defkey,))


class _KV:
    def __init__(self, buf, keys):
        self.buf = buf
        self.keys = keys

    def __getitem__(self, idx):
        return V(self.buf, self.buf.t[idx], self.keys)


class V:
    def __init__(self, buf, ap, keys):
        self.buf = buf
        self.ap = ap
        self.keys = keys

    def bitcast(self, dt):
        return V(self.buf, self.ap.bitcast(dt), self.keys)

    def m(self, fn):
        return V(self.buf, fn(self.ap), self.keys)

    def bc(self, axis, n):
        a = self.ap.unsqueeze(axis)
        shp = list(a.shape)
        shp[axis] = n
        return V(self.buf, a.broadcast_to(shp), self.keys)


class Op:
    __slots__ = ("eng", "fn", "deps", "idx", "sig", "semval", "stream", "sval")

    def __init__(self, eng, fn):
        self.eng = eng
        self.fn = fn
        self.deps = []
        self.idx = -1
        self.sig = False
        self.semval = 0
        self.stream = None
        self.sval = 0


ENGS = ("pe", "dve", "act", "pool", "sp")


class Builder:
    def __init__(self, nc):
        self.nc = nc
        self.stack = contextlib.ExitStack()
        self.ops = {e: [] for e in ENGS}
        self.streams = {}
        self.nbuf = 0

    def sb(self, name, shape, dt):
        t = self.stack.enter_context(self.nc.sbuf_tensor(name, list(shape), dt))
        return Buf(name, t)

    def ps(self, name, shape, dt):
        t = self.stack.enter_context(self.nc.psum_tensor(name, list(shape), dt))
        b = Buf(name, t)
        b.excl = True
        return b

    def emit(self, eng, fn, outs=(), ins=(), stream=None):
        if not getattr(self, "enabled", True):
            return None
        op = Op(eng, fn)
        deps = set()
        for v in ins:
            st = v.buf.st
            for k in v.keys:
                ents = list(st.values()) if k is None else [st.get(k), st.get(None)]
                for e in ents:
                    if e is not None:
                        if e[0] is not None:
                            deps.add(e[0])
                        if v.buf.excl:
                            deps.update(r for r in e[1] if r.eng != eng)
        for v in outs:
            st = v.buf.st
            for k in v.keys:
                ents = list(st.values()) if k is None else [st.get(k), st.get(None)]
                for e in ents:
                    if e is not None:
                        if e[0] is not None:
                            deps.add(e[0])
                        deps.update(e[1])
        for v in ins:
            st = v.buf.st
            for k in v.keys:
                e = st.get(k)
                if e is None:
                    e = st[k] = [None, []]
                e[1].append(op)
        for v in outs:
            st = v.buf.st
            for k in v.keys:
                if k is None:
                    st.clear()
                st[k] = [op, []]
        deps.discard(op)
        if stream is not None:
            op.stream = stream
            self.streams[stream] = self.streams.get(stream, 0) + 16
            op.sval = self.streams[stream]
        op.deps = [d for d in deps if not (d.eng == "pe" and eng == "pe" and d.stream is None)]
        for d in op.deps:
            if d.stream is None:
                d.sig = True
        op.idx = len(self.ops[eng])
        self.ops[eng].append(op)
        return op

    def finalize(self):
        nc = self.nc
        st = self.stack
        esem = {e: st.enter_context(nc.semaphore("sem_" + e)) for e in ENGS}
        ssem = {s: st.enter_context(nc.semaphore("ds_" + s)) for s in self.streams}
        last_ops = []
        for e in ENGS:
            cands = [op for op in self.ops[e] if op.stream is None]
            if cands and e != "sp":
                cands[-1].sig = True
                last_ops.append(cands[-1])
        for e in ENGS:
            c = 0
            for op in self.ops[e]:
                if op.sig and op.stream is None:
                    c += 1
                    op.semval = c
        block = st.enter_context(nc.Block())
        final_streams = dict(self.streams)

        def run(e, eng):
            known = {}
            for op in self.ops[e]:
                need = {}
                for d in op.deps:
                    if d.stream is not None:
                        key, val = ("s", d.stream), d.sval
                    else:
                        key, val = ("e", d.eng), d.semval
                    if val > need.get(key, 0):
                        need[key] = val
                for key, val in need.items():
                    if known.get(key, 0) >= val:
                        continue
                    known[key] = val
                    sem = ssem[key[1]] if key[0] == "s" else esem[key[1]]
                    eng.wait_ge(sem, val)
                ins = op.fn(eng)
                if op.stream is not None:
                    ins.then_inc(ssem[op.stream], 16)
                elif op.sig:
                    ins.then_inc(esem[e], 1)
            if e == "sp":
                for s, val in final_streams.items():
                    if known.get(("s", s), 0) < val:
                        eng.wait_ge(ssem[s], val)
                for lo in last_ops:
                    if known.get(("e", lo.eng), 0) < lo.semval:
                        eng.wait_ge(esem[lo.eng], lo.semval)

        @block.tensor
        def _(eng):
            run("pe", eng)

        @block.vector
        def _(eng):
            run("dve", eng)

        @block.scalar
        def _(eng):
            run("act", eng)

        @block.gpsimd
        def _(eng):
            run("pool", eng)

        @block.sync
        def _(eng):
            run("sp", eng)

    def close(self):
        self.stack.close()

    def mm(self, out, lhsT, rhs, start=True, stop=True):
        return self.emit("pe", lambda e: e.matmul(out.ap, lhsT.ap, rhs.ap, start=start, stop=stop),
                         outs=[out], ins=[lhsT, rhs])

    def tr(self, out, in_, ident):
        return self.emit("pe", lambda e: e.transpose(out.ap, in_.ap, ident.ap), outs=[out], ins=[in_, ident])

    def act(self, out, in_, func, scale=1.0, bias=0.0, accum=None, extra_ins=()):
        sc = scale.ap if isinstance(scale, V) else scale
        bi = bias.ap if isinstance(bias, V) else bias
        ins = [in_] + [x for x in (scale, bias) if isinstance(x, V)] + list(extra_ins)
        outs = [out] + ([accum] if accum is not None else [])
        if accum is None:
            fn = lambda e: e.activation(out.ap, in_.ap, func, bias=bi, scale=sc)
        else:
            fn = lambda e: e.activation(out.ap, in_.ap, func, bias=bi, scale=sc, accum_out=accum.ap)
        return self.emit("act", fn, outs=outs, ins=ins)

    def tt(self, out, a, b, op, eng="dve"):
        return self.emit(eng, lambda e: e.tensor_tensor(out.ap, a.ap, b.ap, op), outs=[out], ins=[a, b])

    def ts(self, out, a, s1, op0, s2=None, op1=None, eng="dve"):
        a1 = s1.ap if isinstance(s1, V) else s1
        a2 = s2.ap if isinstance(s2, V) else s2
        ins = [a] + [x for x in (s1, s2) if isinstance(x, V)]
        if op1 is None:
            fn = lambda e: e.tensor_scalar(out.ap, a.ap, a1, None, op0)
        else:
            fn = lambda e: e.tensor_scalar(out.ap, a.ap, a1, a2, op0, op1)
        return self.emit(eng, fn, outs=[out], ins=ins)

    def stt(self, out, a, s, b, op0, op1):
        a1 = s.ap if isinstance(s, V) else s
        ins = [a, b] + ([s] if isinstance(s, V) else [])
        return self.emit("dve", lambda e: e.scalar_tensor_tensor(out.ap, a.ap, a1, b.ap, op0, op1),
                         outs=[out], ins=ins)

    def copy(self, out, in_, eng="dve"):
        if eng == "act":
            return self.emit("act", lambda e: e.copy(out.ap, in_.ap), outs=[out], ins=[in_])
        if eng == "dve":
            return self.emit(eng, lambda e: e.tensor_scalar(out.ap, in_.ap, 1.0, None, ALU.mult), outs=[out], ins=[in_])
        return self.emit(eng, lambda e: e.tensor_copy(out.ap, in_.ap), outs=[out], ins=[in_])

    def memset(self, out, val, eng="dve"):
        return self.emit(eng, lambda e: e.memset(out.ap, val), outs=[out])

    def dma(self, out, in_, stream, eng="sp", ins=(), outs=()):
        oa = out.ap if isinstance(out, V) else out
        ia = in_.ap if isinstance(in_, V) else in_
        o = [out] if isinstance(out, V) else []
        i = [in_] if isinstance(in_, V) else []
        return self.emit(eng, lambda e: e.dma_start(out=oa, in_=ia), outs=o + list(outs), ins=i + list(ins),
                         stream=stream)


NSUB = 3
NT = 128 * NSUB
NEG = -30000.0
NV = 137
SLOTW = 1024


def _inproj_order():
    rest = list(range(24, 60))
    out = []
    for q in range(6):
        out += list(range(4 * q, 4 * q + 4))
        out += rest[6 * q:6 * q + 6]
    return out


INPROJ_ORDER = _inproj_order()


class WStream:
    def __init__(self, B, nslots):
        self.B = B
        self.nslots = nslots
        self.ring = B.sb("wring", [128, nslots, SLOTW], BF16)
        self.sched = []
        self.nload = 0
        self.nuse = 0

    def add(self, dram_ap, size):
        self.sched.append((dram_ap, size))

    def prefetch(self):
        if self.nload >= len(self.sched):
            return
        ap, size = self.sched[self.nload]
        s = self.nload % self.nslots
        if getattr(self, "halfw", False):
            self.B.dma(self.ring(s)[:, s, 0:size // 2], ap[:, 0:size // 2], stream="w%d" % s, eng="pool")
        else:
            self.B.dma(self.ring(s)[:, s, 0:size], ap, stream="w%d" % s, eng="pool")
        self.nload += 1

    def start(self):
        for _ in range(self.nslots):
            self.prefetch()

    def get(self):
        assert self.nuse < self.nload, "weight schedule underflow"
        s = self.nuse % self.nslots
        self.nuse += 1
        ring = self.ring

        def view(lo, hi):
            return ring(s)[:, s, lo:hi]
        return view

    def done(self):
        self.prefetch()


def build_program(cfg):
    nc = bass.Bass("TRN2", target_bir_lowering=False)
    B = Builder(nc)
    nmac = cfg.get("nmac", 11)
    do_sample = cfg.get("sample", True)
    do_mixer = cfg.get("mixer", True)
    do_ffn2 = cfg.get("ffn2", True)
    dbg = cfg.get("dbg", False)

    def din(name, shape, dt=F32):
        return nc.dram_tensor(name, list(shape), dt, kind="ExternalInput").ap()

    def dout(name, shape, dt=F32):
        return nc.dram_tensor(name, list(shape), dt, kind="ExternalOutput").ap()

    xpT = din("xpT", [D, TP])
    ypT = dout("ypT", [D, TP])
    xsT = din("xsT", [D, 128])
    ysT = dout("ysT", [D, 128])
    wgu = [din("wgu%d" % i, [2 * NFF, 128, 1024]) for i in (1, 2)]
    wdn = [din("wdn%d" % i, [8, 128, DFF]) for i in (1, 2)]
    win = din("win", [60, 128, 1024])
    wba = din("wba", [128, 128])
    wout = din("wout", [8, 128, 1024])
    gains = din("gains", [128, 6, 8])
    cmask = din("cmask", [128, 9, 128])
    cbias = din("cbias", [128, 4, 8, 128])
    cbiasm = din("cbiasm", [16, 3, 8, 128])
    vecs = din("vecs", [128, NV])
    shistT = din("shistT", [3072, NSEQ * 3])
    s_ssm_in = din("s_ssm_in", [NSEQ, 8, 128, 128])
    kbT_in = din("kbT_in", [NSEQ, 2, 128, 128])
    kmT_in = din("kmT_in", [NSEQ, 2, 128, 16])
    vb_in = din("vb_in", [NSEQ, 128, 2, 128])
    vm_in = din("vm_in", [NSEQ, 16, 2, 128])
    kb_in = din("kb_in", [NSEQ, 128, 2, 128])
    pconv_o = dout("pconv_o", [3072, 3])
    pssm_o = dout("pssm_o", [8, 128, 128])
    pmeta_o = dout("pmeta_o", [4, 128, 128])
    pwin_o = dout("pwin_o", [4, 128, 128])
    sconv_o = dout("sconv_o", [3072, NSEQ * 3])
    sssm_o = dout("sssm_o", [NSEQ, 8, 128, 128])
    swink_o = dout("swink_o", [NSEQ, 128, 2, 128])
    swinv_o = dout("swinv_o", [NSEQ, 128, 2, 128])
    dbg_o = dout("dbg_o", [128, 8, NT]) if dbg else None

    cm = B.sb("cm", [128, 9, 128], F32)
    ident_b = B.sb("ident_b", [128, 128], BF16)
    ones_b = B.sb("ones_b", [128, 128], BF16)
    cb = B.sb("cb", [128, 2, 8, 128], BF16)
    cbm = B.sb("cbm", [16, 8, 128], BF16)
    vc = B.sb("vc", [128, NV], F32)
    negA = B.sb("negA", [128, 8], F32)
    esink = B.sb("esink", [128, 8], F32)
    gn = B.sb("gn", [128, 6, 8], F32)
    gnh = B.sb("gnh", [128, 6, 8], F32)
    mhalf = B.sb("mhalf", [128, 1], F32)
    epsb = B.sb("epsb", [128, 1], F32)
    h = B.sb("h", [128, 8, NT], F32)
    u = B.sb("u", [128, 8, NT], BF16)
    sqs = B.sb("sqs", [128, 2, NT], BF16)
    shr = B.sb("shr", [128, 27, NT], BF16)
    shr2 = B.sb("shr2", [128, 9, NT], BF16)
    yo = B.sb("yo", [128, 8, NT], F32)
    sg = B.sb("sg", [128, 2, NT], F32)
    ms = B.sb("ms", [128, NT], F32)
    rinv = B.sb("rinv", [128, NT], F32)
    tmpn = B.sb("tmpn", [128, 2, NT], F32)
    PS = B.ps("PS", [128, 8, 512], F32)
    W = WStream(B, cfg.get("nslots", 8))
    W.halfw = cfg.get("halfw", False)
    psn = [0]

    I_f = cm[:, 7, :]
    ONES_f = cm[:, 8, :]

    reserved = set()

    def bank():
        while True:
            b = psn[0] % 8
            psn[0] += 1
            if b not in reserved:
                return b

    def psb(b, lo=0, hi=512):
        return PS(b)[:, b, lo:hi]

    def psbf(b, n):
        return PS(b)[:, b, 0:(n + 1) // 2].bitcast(BF16)

    B.dma(cm[:, :, :], cmask, "c0")
    B.dma(gn[:, :, :], gains, "c1")
    B.dma(vc[:, :], vecs, "c2")
    B.dma(cb[:, :, :, :], cbias[:, 0:2], "c3", eng="pool")
    B.copy(ident_b[:, :], cm[:, 7, :])
    B.copy(ones_b[:, :], cm[:, 8, :])
    B.ts(gnh[:, :, :], gn[:, :, :], 0.5, ALU.mult)
    B.memset(mhalf[:, :], -0.5)
    B.memset(epsb[:, :], RMS_EPS)
    convw = lambda c, j: vc[:, c * 4 + j:c * 4 + j + 1]
    dnw = vc[:, 96:97]
    alog = vc[:, 97:105]
    dtb = vc[:, 105:113]
    sinks = vc[:, 113:121]
    seqsel = vc[:, 121:137]
    B.act(negA[:, :], alog, AF.Exp)
    B.ts(negA[:, :], negA[:, :], -1.0, ALU.mult)
    B.act(esink[:, :], sinks, AF.Exp)

    def sched_ffn(i):
        for j in range(NFF):
            W.add(wgu[i][2 * j], 1024)
            W.add(wgu[i][2 * j + 1], 1024)
        for d in range(8):
            W.add(wdn[i][d, :, 0:1024], 1024)
            W.add(wdn[i][d, :, 1024:2048], 1024)
            W.add(wdn[i][d, :, 2048:2816], 768)

    def sched_mix():
        for j in INPROJ_ORDER:
            W.add(win[j], 1024)
        W.add(wba, 128)
        for d in range(8):
            W.add(wout[d], 1024)

    tiles = [("p", m) for m in range(nmac)] + ([("s", 0)] if do_sample else [])
    for _ in tiles:
        sched_ffn(0)
        if do_mixer:
            sched_mix()
        if do_ffn2:
            sched_ffn(1)
    W.start()

    sqn = [0]

    class Stats:
        def __init__(self, N):
            self.N = N
            self.b = bank()
            reserved.add(self.b)
            self.pend = []
            self.n = 0

        def add(self, src_ps):
            k = sqn[0] % 2
            sqn[0] += 1
            B.act(sqs(k)[:, k, :self.N], src_ps, AF.Square)
            self.pend.append(k)
            if len(self.pend) > 1:
                self.flush1()

        def flush1(self):
            k = self.pend.pop(0)
            B.mm(psb(self.b, 0, self.N), ones_b[:, :], sqs(k)[:, k, :self.N], start=(self.n == 0), stop=(self.n == 7))
            self.n += 1

        def finish(self):
            while self.pend:
                self.flush1()
            reserved.discard(self.b)
            return self.b

    def rms_rinv(src, N, scale, stats=None):
        if stats is not None:
            b = stats.finish()
        else:
            b = bank()
        for c in range(8 if stats is None else 0):
            k = sqn[0] % 2
            sqn[0] += 1
            B.act(sqs(k)[:, k, :N], src(c)[:, c, :N], AF.Square)
            B.mm(psb(b, 0, N), ones_b[:, :], sqs(k)[:, k, :N], start=(c == 0), stop=(c == 7))
        B.act(ms[:, :N], psb(b, 0, N), AF.Ln, scale=scale, bias=epsb[:, 0:1])
        B.act(rinv[:, :N], ms[:, :N], AF.Exp, scale=-0.5)

    def prenorm(gi, N):
        rms_rinv(h, N, 1.0 / D)
        for c in range(8):
            B.stt(u(c)[:, c, :N], h(c)[:, c, :N], gn[:, gi, c:c + 1], rinv[:, :N], ALU.mult, ALU.mult)

    def postnorm_add(gi, N, half, out_ap=None, stats=None):
        rms_rinv(yo, N, 1.0 / D, stats)
        gsrc = gnh if half else gn
        for c in range(8):
            k = c % 2
            B.stt(tmpn(k)[:, k, :N], yo(c)[:, c, :N], gsrc[:, gi, c:c + 1], rinv[:, :N], ALU.mult, ALU.mult)
            if out_ap is None:
                B.tt(h(c)[:, c, :N], h(c)[:, c, :N], tmpn(k)[:, k, :N], ALU.add, eng="pool")
            else:
                B.tt(yo(c)[:, c, :N], h(c)[:, c, :N], tmpn(k)[:, k, :N], ALU.add, eng="pool")
                B.dma(out_ap[c * 128:(c + 1) * 128, :], yo(c)[:, c, :N], "yout%d" % c)

    def ffn(fi, N, out_ap=None):
        gi = 0 if fi == 0 else 4
        prenorm(gi, N)
        for j in range(NFF):
            wg = W.get()
            bg = bank()
            for c in range(8):
                B.mm(psb(bg, 0, N), wg(c * 128, c * 128 + 128), u(c)[:, c, :N], start=(c == 0), stop=(c == 7))
            W.done()
            wu = W.get()
            bu = bank()
            for c in range(8):
                B.mm(psb(bu, 0, N), wu(c * 128, c * 128 + 128), u(c)[:, c, :N], start=(c == 0), stop=(c == 7))
            W.done()
            k = j % 2
            B.act(sg(k)[:, k, :N], psb(bg, 0, N), AF.Silu)
            B.tt(shr(j)[:, j, :N], sg(k)[:, k, :N], psb(bu, 0, N), ALU.mult)
        st = Stats(N)
        for d in range(8):
            bo = bank()
            for hf, (j0, nj) in enumerate(((0, 8), (8, 8), (16, 6))):
                wd = W.get()
                for jj in range(nj):
                    j = j0 + jj
                    B.mm(psb(bo, 0, N), wd(jj * 128, jj * 128 + 128), shr(j)[:, j, :N], start=(j == 0),
                         stop=(j == NFF - 1))
                W.done()
            B.copy(yo(d)[:, d, :N], psb(bo, 0, N), eng="act")
            st.add(psb(bo, 0, N))
        postnorm_add(gi + 1, N, True, out_ap, st)

    def alias(parent, ap):
        x = Buf(parent.name + "_al", ap)
        x.st = parent.st
        return x

    if do_mixer:
        qk = B.sb("qk", [128, 2, 8, NT], BF16)
        vT = B.sb("vT", [128, 8, NT], BF16)
        ybuf = u
        xp = B.sb("xp", [128, 4, NT + 3], F32)
        xps = alias(xp, xp.t[:, :, 0:NSEQ * 11].rearrange("p k (n j) -> p k n j", j=11))
        acc = B.sb("acc", [128, 4, NT], F32)
        qs = B.sb("qs", [128, 3, NT], F32)
        ms2 = tmpn
        oh16 = B.sb("oh16", [128, 16, 16], BF16)
        hist = B.sb("hist", [128, 24, 3], F32)
        ba = B.sb("ba", [128, NSUB, 16], F32)
        beta = B.sb("beta", [128, NSUB, 8], F32)
        gg = B.sb("gg", [128, NSUB, 8], F32)
        ge = B.sb("ge", [128, NSUB, 8], F32)
        gx = B.sb("gx", [128, NSUB, 8], F32)
        sm = B.sb("sm", [128, 8, 8], F32)
        cdec_s = B.sb("cdec_s", [128, NSEQ, 8], F32)
        Rs = B.sb("Rs", [128, NSEQ, 8], F32)
        kvf = B.sb("kvf", [128, 4, 128], F32)
        kvtok = B.sb("kvtok", [128, 4, 128], F32)
        skprev = B.sb("skprev", [128, 2, 128], BF16)
        kmT = B.sb("kmT", [128, 2, 16], BF16)
        vm = B.sb("vm", [16, 2, 128], BF16)
        vsw = B.sb("vsw", [128, 2, 2, 128], BF16)
        S = B.sb("S", [128, 8, 128], F32)
        Sbf = B.sb("Sbf", [128, 8, 128], BF16)
        def tset0():
            tA_ = B.sb("tA", [128, 4, 128], F32)
            tB_ = B.sb("tB", [128, 4, 128], F32)
            tC_ = B.sb("tC", [128, 4, 128], F32)
            dI_ = B.sb("decI", [128, 4, 128], F32)
            rest = [B.sb(nm, [128, 4, 128], BF16) for nm in ("PU", "PL", "XU", "XL", "qkTm", "r0", "uu", "Vtok", "kdec",
                                                            "on_t")]
            return [tA_, tB_, tC_, dI_] + rest + [B.sb("ssum", [128, 3, 4], F32)]

        R1 = B.sb("R1", [128, 4 * 512 + 10 * 256 + 16], F32)

        def tset1():
            out = []
            off = 0
            for i in range(4):
                x = alias(R1, R1.t[:, off:off + 512].rearrange("p (h n) -> p h n", h=4))
                x.defkey = "f%d" % i
                out.append(x)
                off += 512
            for i in range(10):
                x = alias(R1, R1.t[:, off:off + 256].bitcast(BF16).rearrange("p (h n) -> p h n", h=4))
                x.defkey = "b%d" % i
                out.append(x)
                off += 256
            x = alias(R1, R1.t[:, off:off + 12].rearrange("p (a b) -> p a b", a=3))
            x.defkey = "ss"
            out.append(x)
            return out

        TS = [tset0(), tset1()]
        shist = alias(R1, R1.t[:, 0:1152].rearrange("p (c x) -> p c x", c=24))
        sconv = alias(R1, R1.t[:, 1152:2304].rearrange("p (c x) -> p c x", c=24))
        tA, tB, tC, decI = TS[0][0:4]
        lg = decI
        PTo = B.sb("PTo", [128, 4, 128], BF16)
        PTp = B.sb("PTp", [128, 4, 128], BF16)
        PTm = B.sb("PTm", [16, 4, 128], BF16)
        rden = B.sb("rden", [128, 4, 128], F32)
        svb_t = B.sb("svb_t", [128, 2, NT], BF16)
        KQT = alias(S, S.t[:, :, :].rearrange("p a b -> p (a b)").bitcast(BF16).rearrange(
            "p (h w t) -> p h w t", h=8, w=2))
        Sn = B.sb("Sn", [128, 2, 8, 128], BF16)
        Uexp = alias(qs, qs.t[:, :, :].rearrange("p a b -> p (a b)")[:, 0:1024].bitcast(BF16).rearrange(
            "p (n d) -> p n d", n=NSEQ))
        Sold = B.sb("Sold", [128, 4, 128], F32)
        Snew = B.sb("Snew", [128, 4, 128], F32)
        kbT = B.sb("kbT", [128, 2, 4, 128], BF16)
        vb = B.sb("vb", [128, 2, 4, 128], BF16)
        kmTs = B.sb("kmTs", [128, NSEQ, 16], BF16)
        vms = B.sb("vms", [16, NSEQ, 128], BF16)

        B.memset(oh16[:, :, :], 0.0)
        for c in range(16):
            B.memset(oh16[:, c, c:c + 1], 1.0)
        B.memset(hist[:, :, :], 0.0)
        B.memset(S[:, :, :], 0.0)
        B.memset(Sbf[:, :, :], 0.0)

    def barrier_shr():
        B.emit("pool", lambda e: e.memset(tmpn.t[0:1, 0, 0:1], 0.0), outs=[shr[:, :, :], tmpn(0)[0:1, 0, 0:1]])

    def stage(name):
        if cfg.get("stop") == name:
            B.enabled = False

    def inproj(kind, m, N):
        nsub = N // 128
        stage("ip_start")
        prenorm(2, N)
        bss = bank()
        reserved.add(bss)

        def qkv_post(items):
            ks = [c % 4 for c, _ in items]
            if kind == "p":
                for (c, b), k in zip(items, ks):
                    B.copy(xp(k)[:, k, 0:3], hist(c)[:, c, :], eng="pool")
                for (c, b), k in zip(items, ks):
                    B.copy(xp(k)[:, k, 3:3 + N], psb(b, 0, N), eng="act")
                for (c, b), k in zip(items, ks):
                    B.copy(hist(c)[:, c, :], xp(k)[:, k, N:N + 3], eng="pool")
                for (c, b), k in zip(items, ks):
                    B.ts(acc(k)[:, k, :N], xp(k)[:, k, 0:N], convw(c, 0), ALU.mult)
                for j in range(1, 4):
                    for (c, b), k in zip(items, ks):
                        B.stt(acc(k)[:, k, :N], xp(k)[:, k, j:j + N], convw(c, j), acc(k)[:, k, :N], ALU.mult, ALU.add)
            else:
                f3n = lambda a: a.rearrange("p (n j) -> p n j", j=3)
                f8 = lambda a: a.rearrange("p (n t) -> p n t", t=8)
                for (c, b), k in zip(items, ks):
                    B.copy(xps(k)[:, k, :, 0:3], shist(c)[:, c, :].m(f3n), eng="pool")
                for (c, b), k in zip(items, ks):
                    B.copy(xps(k)[:, k, :, 3:11], psb(b, 0, N).m(f8), eng="act")
                for (c, b), k in zip(items, ks):
                    B.copy(sconv(c)[:, c, :].m(f3n), xps(k)[:, k, :, 8:11], eng="pool")
                for (c, b), k in zip(items, ks):
                    B.ts(acc(k)[:, k, :N].m(f8), xps(k)[:, k, :, 0:8], convw(c, 0), ALU.mult)
                for j in range(1, 4):
                    for (c, b), k in zip(items, ks):
                        B.stt(acc(k)[:, k, :N].m(f8), xps(k)[:, k, :, j:j + 8], convw(c, j), acc(k)[:, k, :N].m(f8),
                              ALU.mult, ALU.add)
            return items, ks

        def qkv_post2(items, ks):
            for (c, b), k in zip(items, ks):
                if c < 16:
                    which = 1 if c < 8 else 0
                    hd_i = c % 8
                    B.act(qk((which, hd_i))[:, which, hd_i, :N], acc(k)[:, k, :N], AF.Silu)
                else:
                    B.act(vT(c - 16)[:, c - 16, :N], acc(k)[:, k, :N], AF.Silu)
            for (c, b), k in zip(items, ks):
                if c < 16:
                    which = 1 if c < 8 else 0
                    hd_i = c % 8
                    k2 = sqn[0] % 2
                    sqn[0] += 1
                    B.act(sqs(k2)[:, k2, :N], qk((which, hd_i))[:, which, hd_i, :N], AF.Square)
                    B.mm(PS(bss)[0:16, bss, 0:N], oh16[:, c, :], sqs(k2)[:, k2, :N], start=(c == 0), stop=(c == 15))

        pair = []
        pend2 = None
        for blk in INPROJ_ORDER:
            wv = W.get()
            b = bank()
            for c in range(8):
                B.mm(psb(b, 0, N), wv(c * 128, c * 128 + 128), u(c)[:, c, :N], start=(c == 0), stop=(c == 7))
            W.done()
            if blk == 24:
                stage("ip_blk24")
            if blk == 44:
                stage("ip_blk44")
            if blk < 24:
                pair.append((blk, b))
                if len(pair) == 4:
                    if pend2 is not None:
                        qkv_post2(*pend2)
                    pend2 = qkv_post(pair)
                    pair = []
                continue
            if True:
                if blk < 32:
                    c = blk - 24
                    B.act(shr(c)[:, c, :N], psb(b, 0, N), AF.Silu)
                elif blk < 40:
                    c = blk - 32
                    B.copy(shr2(c)[:, c, :N], psb(b, 0, N), eng="act")
                elif blk < 44:
                    i4 = blk - 40
                    if i4 < 2:
                        B.copy(shr(24 + i4)[:, 24 + i4, :N], psb(b, 0, N), eng="act")
                    else:
                        B.copy(svb_t(i4 - 2)[:, i4 - 2, :N], psb(b, 0, N), eng="act")
                    if kind == "s":
                        B.copy(kvf(i4)[:, i4, :], psb(b, 0, 128), eng="act")
                    elif m == 0:
                        if not cfg.get("no_kvf"):
                            B.copy(kvf(i4)[:, i4, :], psb(b, 0, 128), eng="act")
                        if not cfg.get("no_pm"):
                            B.dma(pmeta_o[i4], kvf(i4)[:, i4, :], "pm%d" % i4)
                    elif m == nmac - 1:
                        B.copy(kvf(i4)[:, i4, :], psb(b, N - 128, N), eng="act")
                        B.dma(pwin_o[i4], kvf(i4)[:, i4, :], "pm%d" % i4)
                else:
                    c = blk - 44
                    k = c % 2
                    B.act(sg(k)[:, k, :N], psb(b, 0, N), AF.Tanh, scale=0.5)
                    B.ts(shr(8 + c)[:, 8 + c, :N], sg(k)[:, k, :N], 0.5, ALU.mult, 0.5, ALU.add)
        if pend2 is not None:
            qkv_post2(*pend2)
        stage("ip_ba")
        wv = W.get()
        b = bank()
        for s in range(nsub):
            for c in range(8):
                B.mm(psb(b, s * 16, s * 16 + 16), u(c)[:, c, s * 128:(s + 1) * 128], wv(c * 16, c * 16 + 16),
                     start=(c == 0), stop=(c == 7))
        W.done()
        B.copy(ba[:, 0:nsub, :], psb(b, 0, nsub * 16).m(lambda a: a.rearrange("p (s x) -> p s x", x=16)), eng="act")
        B.act(ms[0:16, :N], PS(bss)[0:16, bss, 0:N], AF.Ln, bias=epsb[0:16, 0:1])
        B.act(rinv[0:16, :N], ms[0:16, :N], AF.Exp, scale=-0.5)
        reserved.discard(bss)
        for c in range(16):
            which = 1 if c < 8 else 0
            hd_i = c % 8
            k = c % 2
            B.ts(ms2(k)[0:16, k, :N], rinv[0:16, :N], cm[0:16, 7, c:c + 1], ALU.mult)
            bb = bank()
            B.mm(psb(bb, 0, N), cm[0:16, 8, :], ms2(k)[0:16, k, :N])
            sc = (128.0 ** -0.5) if which == 1 else 1.0
            B.stt(qk((which, hd_i))[:, which, hd_i, :N], qk((which, hd_i))[:, which, hd_i, :N], sc, psb(bb, 0, N),
                  ALU.mult, ALU.mult)
        B.act(beta[:, 0:nsub, :], ba[:, 0:nsub, 0:8], AF.Exp, scale=-1.0)
        B.act(beta[:, 0:nsub, :], beta[:, 0:nsub, :], AF.Ln, bias=1.0)
        B.act(beta[:, 0:nsub, :], beta[:, 0:nsub, :], AF.Exp, scale=-1.0)
        B.tt(gg[:, 0:nsub, :], ba[:, 0:nsub, 8:16], dtb.bc(1, nsub), ALU.add)
        B.act(ge[:, 0:nsub, :], gg[:, 0:nsub, :], AF.Exp)
        B.act(gg[:, 0:nsub, :], ge[:, 0:nsub, :], AF.Ln, bias=1.0)
        B.act(gx[:, 0:nsub, :], gg[:, 0:nsub, :], AF.Exp, scale=-1.0)
        B.stt(gx[:, 0:nsub, :], ge[:, 0:nsub, :], 1.0, gx[:, 0:nsub, :], ALU.add, ALU.mult)
        B.stt(gg[:, 0:nsub, :], gx[:, 0:nsub, :], -1.0, gg[:, 0:nsub, :], ALU.add, ALU.add)
        B.tt(gg[:, 0:nsub, :], gg[:, 0:nsub, :], negA[:, :].bc(1, nsub), ALU.mult)

    snrot = [0]

    def dn(kind, sl, gs):
        cs = slice(sl * 128, sl * 128 + 128)
        smp = kind == "s"
        Mincl = cm[:, 3, :] if smp else cm[:, 0, :]
        Mgt = cm[:, 4, :] if smp else cm[:, 1, :]
        maskS = cm[:, 5, :] if smp else cm[:, 2, :]
        Mall = cm[:, 6, :] if smp else cm[:, 8, :]
        nlev = 1 if smp else 5
        g_s = gg[:, sl, :]
        bsm = bank()
        B.mm(psb(bsm, 0, 8), Mincl, g_s)
        B.mm(psb(bsm, 8, 16), Mall, g_s)
        gc = sm[:, 0, :]
        egc = sm[:, 1, :]
        negegc = sm[:, 2, :]
        kd = sm[:, 3, :]
        kscale = sm[:, 4, :]
        cdec = sm[:, 5, :]
        B.copy(gc, psb(bsm, 0, 8))
        B.act(egc, psb(bsm, 0, 8), AF.Exp)
        B.ts(negegc, egc, -1.0, ALU.mult)
        B.tt(kd, psb(bsm, 8, 16), gc, ALU.subtract)
        B.act(kscale, kd, AF.Exp)
        if not smp:
            B.act(cdec, psb(bsm, 8, 16), AF.Exp)
        else:
            B.tt(Rs[:, :, :], g_s.bc(1, NSEQ), seqsel.bc(2, 8), ALU.mult)
            b2 = bank()
            B.mm(psb(b2, 0, 128), ONES_f, Rs[:, :, :].m(lambda a: a.rearrange("p n h -> p (n h)")))
            B.act(cdec_s[:, :, :].m(lambda a: a.rearrange("p n h -> p (n h)")), psb(b2, 0, 128), AF.Exp)
            xb = [bank() for _ in range(4)]
            for n in range(NSEQ):
                r = snrot[0] % 2
                snrot[0] += 1
                B.dma(Sn(r)[:, r, :, :], s_ssm_in[n].rearrange("h k v -> k h v"), "sn%d" % r, eng="pool")
                for hh in range(8):
                    for w in range(2):
                        c0 = (hh % 2) * 256 + w * 128 + n * 8
                        B.mm(PS(xb[hh // 2])[:, xb[hh // 2], c0:c0 + 8], Sn(r)[:, r, hh, :],
                             qk[:, w, hh, n * 8:n * 8 + 8])
            for i in range(4):
                B.copy(KQT[:, 2 * i:2 * i + 2, :, :].m(lambda a: a.rearrange("p h w t -> p (h w t)")), psb(xb[i]),
                       eng=("act" if i % 2 else "dve"))

        def grp(G, T):
            tA, tB, tC, decI, PU, PL, XU, XL, qkTm, r0, uu, Vtok, kdec, on_t, ssum = T
            o_t = tA
            hs = [4 * G + i for i in range(4)]
            bt = beta[:, sl, 4 * G:4 * G + 4]
            B.tt(tA[:, :, :], Mgt.bc(1, 4), g_s.m(lambda a: a[:, 4 * G:4 * G + 4]).bc(2, 128), ALU.mult, eng="pool")
            bD = bank()
            for i in range(4):
                B.mm(psb(bD, i * 128, i * 128 + 128), tA[:, i, :], Mincl)
            B.act(decI[:, :, :].m(lambda a: a.rearrange("p h n -> p (h n)")), psb(bD), AF.Exp)
            B.tt(decI[:, :, :], decI[:, :, :], Mincl.bc(1, 4), ALU.mult, eng="pool")
            yield
            bK = bank()
            bQ = bank()
            for i, hh in enumerate(hs):
                B.mm(psb(bK, i * 128, i * 128 + 128), qk[:, 0, hh, cs], qk[:, 0, hh, cs])
            for i, hh in enumerate(hs):
                B.mm(psb(bQ, i * 128, i * 128 + 128), qk[:, 0, hh, cs], qk[:, 1, hh, cs])
            f3 = lambda a: a.rearrange("p (h n) -> p h n", h=4)
            B.tt(qkTm[:, :, :], psb(bQ).m(f3), decI[:, :, :], ALU.mult)
            B.tt(tB[:, :, :], psb(bK).m(f3), decI[:, :, :], ALU.mult)
            B.tt(tC[:, :, :], maskS.bc(1, 4), bt.bc(2, 128), ALU.mult, eng="pool")
            B.tt(tB[:, :, :], tB[:, :, :], tC[:, :, :], ALU.mult, eng="pool")
            B.copy(PU[:, :, :], tB[:, :, :], eng="act")
            yield
            bT = bank()
            tv = psbf(bT, 512).m(f3)
            for i in range(4):
                B.tr(V(PS, tv.ap[:, i, :], (bT,)), PU[:, i, :], ident_b[:, :])
            B.copy(PL[:, :, :], tv, eng="act")
            B.tt(XU[:, :, :], ident_b[:, :].bc(1, 4), PU[:, :, :], ALU.subtract)
            B.tt(XL[:, :, :], ident_b[:, :].bc(1, 4), PL[:, :, :], ALU.subtract, eng="pool")
            yield
            b1 = bank()
            b2 = bank()
            for i in range(4):
                B.mm(psb(b1, i * 128, i * 128 + 128), PL[:, i, :], PU[:, i, :])
            for i in range(4):
                B.mm(psb(b2, i * 128, i * 128 + 128), PU[:, i, :], PL[:, i, :])
            yield
            B.copy(PU[:, :, :], psb(b1).m(f3), eng="act")
            B.copy(PL[:, :, :], psb(b2).m(f3))
            yield
            for lv in range(nlev):
                last = lv == nlev - 1
                b3 = bank()
                b4 = bank()
                for i in range(4):
                    B.mm(psb(b3, i * 128, i * 128 + 128), XL[:, i, :], PU[:, i, :])
                for i in range(4):
                    B.mm(psb(b4, i * 128, i * 128 + 128), XU[:, i, :], PL[:, i, :])
                if not last:
                    b1 = bank()
                    b2 = bank()
                    for i in range(4):
                        B.mm(psb(b1, i * 128, i * 128 + 128), PL[:, i, :], PU[:, i, :])
                    for i in range(4):
                        B.mm(psb(b2, i * 128, i * 128 + 128), PU[:, i, :], PL[:, i, :])
                yield
                B.tt(XU[:, :, :], XU[:, :, :], psb(b3).m(f3), ALU.add)
                B.tt(XL[:, :, :], XL[:, :, :], psb(b4).m(f3), ALU.add, eng="dve")
                if not last:
                    B.copy(PU[:, :, :], psb(b1).m(f3), eng="act")
                    B.copy(PL[:, :, :], psb(b2).m(f3), eng="act")
                yield
            bA = bank()
            for i in range(4):
                B.tr(psb(bA, i * 128, i * 128 + 128), tB[:, i, :], I_f)
            B.tt(tC[:, :, :], psb(bA).m(f3), I_f.bc(1, 4), ALU.add)
            B.copy(decI[:, :, :], XU[:, :, :], eng="act")
            yield
            bR = bank()
            for i in range(4):
                B.mm(psb(bR, i * 128, i * 128 + 128), tC[:, i, :], decI[:, i, :])
            B.tt(PU[:, :, :], I_f.bc(1, 4), psb(bR).m(f3), ALU.subtract)
            yield
            bX = bank()
            for i in range(4):
                B.mm(psb(bX, i * 128, i * 128 + 128), XL[:, i, :], PU[:, i, :])
            B.tt(XU[:, :, :], XU[:, :, :], psb(bX).m(f3), ALU.add)
            yield
            bKS = bank()
            bQS = bank()
            if not smp:
                for i, hh in enumerate(hs):
                    B.mm(psb(bKS, i * 128, i * 128 + 128), qk[:, 0, hh, cs], Sbf(G)[:, hh, :])
                for i, hh in enumerate(hs):
                    B.mm(psb(bQS, i * 128, i * 128 + 128), qk[:, 1, hh, cs], Sbf(G)[:, hh, :])
                KSv = psb(bKS).m(f3)
                QSv = psb(bQS).m(f3)
            else:
                kv_ = psbf(bKS, 512).m(f3)
                qv_ = psbf(bQS, 512).m(f3)
                for i, hh in enumerate(hs):
                    B.tr(V(PS, kv_.ap[:, i, :], (bKS,)), KQT[:, hh, 0, :], ident_b[:, :])
                    B.tr(V(PS, qv_.ap[:, i, :], (bQS,)), KQT[:, hh, 1, :], ident_b[:, :])
                KSv = kv_
                QSv = qv_
            bV = bank()
            vv = psbf(bV, 512).m(f3)
            for i, hh in enumerate(hs):
                B.tr(V(PS, vv.ap[:, i, :], (bV,)), vT[:, hh, cs], ident_b[:, :])
            B.copy(Vtok[:, :, :], vv, eng="act")
            yield
            B.tt(tB[:, :, :], KSv, negegc.m(lambda a: a[:, 4 * G:4 * G + 4]).bc(2, 128), ALU.mult)
            B.tt(r0[:, :, :], tB[:, :, :], Vtok[:, :, :], ALU.add, eng="pool")
            yield
            bU = bank()
            for i in range(4):
                B.mm(psb(bU, i * 128, i * 128 + 128), XU[:, i, :], r0[:, i, :])
            B.tt(uu[:, :, :], psb(bU).m(f3), bt.bc(2, 128), ALU.mult)
            yield
            bO = bank()
            for i in range(4):
                B.mm(psb(bO, i * 128, i * 128 + 128), qkTm[:, i, :], uu[:, i, :])
            B.tt(tB[:, :, :], QSv, egc.m(lambda a: a[:, 4 * G:4 * G + 4]).bc(2, 128), ALU.mult)
            B.tt(o_t[:, :, :], tB[:, :, :], psb(bO).m(f3), ALU.add)
            yield
            B.tt(tC[:, :, :], o_t[:, :, :], o_t[:, :, :], ALU.mult, eng="pool")
            B.emit("dve", lambda e: e.tensor_reduce(ssum.t[:, 0, :], tC.t[:, :, :], AX.X, ALU.add),
                   outs=[ssum[:, 0, :]], ins=[tC[:, :, :]])
            B.act(ssum[:, 1, :], ssum[:, 0, :], AF.Ln, scale=1.0 / 128.0, bias=epsb[:, 0:1])
            B.act(ssum[:, 2, :], ssum[:, 1, :], AF.Exp, scale=-0.5)
            B.tt(on_t[:, :, :], o_t[:, :, :], ssum[:, 2, :].bc(2, 128), ALU.mult, eng="pool")
            yield
            bN = bank()
            nv = psbf(bN, 512).m(f3)
            for i in range(4):
                B.tr(V(PS, nv.ap[:, i, :], (bN,)), on_t[:, i, :], ident_b[:, :])
            zsv = shr[:, 4 * G:4 * G + 4, cs]
            gdv = shr[:, 8 + 4 * G:8 + 4 * G + 4, cs]
            B.stt(tB[:, :, :], nv, dnw, zsv, ALU.mult, ALU.mult)
            B.tt(ybuf[:, 4 * G:4 * G + 4, cs], tB[:, :, :], gdv, ALU.mult, eng="pool")
            yield
            bKd = bank()
            kdv = psbf(bKd, 512).m(f3)
            for i, hh in enumerate(hs):
                B.tr(V(PS, kdv.ap[:, i, :], (bKd,)), qk[:, 0, hh, cs], ident_b[:, :])
            B.tt(kdec[:, :, :], kdv, kscale.m(lambda a: a[:, 4 * G:4 * G + 4]).bc(2, 128), ALU.mult)
            yield
            if not smp:
                bS = bank()
                for i in range(4):
                    B.mm(psb(bS, i * 128, i * 128 + 128), kdec[:, i, :], uu[:, i, :])
                Sg = S(G)[:, 4 * G:4 * G + 4, :]
                B.tt(Sg, Sg, cdec.m(lambda a: a[:, 4 * G:4 * G + 4]).bc(2, 128), ALU.mult, eng="pool")
                B.tt(Sg, Sg, psb(bS).m(f3), ALU.add)
                B.copy(Sbf(G)[:, 4 * G:4 * G + 4, :], Sg, eng="act")
                yield
            else:
                for i, hh in enumerate(hs):
                    B.tt(Uexp[:, :, :], uu[:, i, :].bc(1, NSEQ), seqsel.bc(2, 128), ALU.mult)
                    for q4 in range(4):
                        bS = bank()
                        B.mm(psb(bS), kdec[:, i, :],
                             Uexp[:, 4 * q4:4 * q4 + 4, :].m(lambda a: a.rearrange("p n d -> p (n d)")))
                        k2 = (hh * 4 + q4) % 2
                        X = (Sold, Snew)[k2]
                        B.dma(X[:, :, :], s_ssm_in[4 * q4:4 * q4 + 4, hh].rearrange("n k v -> k n v"), "so%d" % k2)
                        B.tt(X[:, :, :], X[:, :, :], cdec_s[:, 4 * q4:4 * q4 + 4, hh].bc(2, 128), ALU.mult,
                             eng="pool")
                        B.tt(X[:, :, :], X[:, :, :], psb(bS).m(f3), ALU.add)
                        B.dma(sssm_o[4 * q4:4 * q4 + 4, hh].rearrange("n k v -> k n v"), X[:, :, :], "sw%d" % k2)


        if smp:
            for G in range(2):
                for _ in grp(G, TS[0]):
                    pass
        else:
            if gs >= 1:
                gens = [itertools.chain(grp(0, TS[0]), swa_p(sl, gs, 0, TS[0])),
                        itertools.chain(grp(1, TS[1]), swa_p(sl, gs, 1, TS[1]))]
            else:
                gens = [grp(0, TS[0]), grp(1, TS[1])]
            while gens:
                for gen in list(gens):
                    try:
                        next(gen)
                    except StopIteration:
                        gens.remove(gen)
    def swa_p(sl, gs, g, T):
        cs = slice(sl * 128, sl * 128 + 128)
        par = gs % 2
        f3 = lambda a: a.rearrange("p (h n) -> p h n", h=4)
        fl = lambda a: a.rearrange("p h n -> p (h n)")
        scale = 128.0 ** -0.5
        lg_, rden_, PTo_, PTp_ = T[1], T[2], T[4], T[5]
        PTm_ = T[6]
        sk_cur = shr[:, 24 + g, cs]
        sq_g = shr2[:, 4 * g:4 * g + 4, cs]
        use_prev = gs >= 2
        bt_ = bank()
        tv = psbf(bt_, 128)
        B.tr(tv, svb_t[:, g, cs], ident_b[:, :])
        bO = bank()
        B.mm(psb(bO).m(f3), sk_cur, sq_g)
        if use_prev:
            bP = bank()
            kprev = shr[:, 24 + g, (sl - 1) * 128:sl * 128] if sl > 0 else skprev[:, g, :]
            B.mm(psb(bP).m(f3), kprev, sq_g)
        bM = bank()
        B.mm(PS(bM)[0:16, bM, :].m(f3), kmT[:, g, :], sq_g)
        yield
        B.copy(vsw[:, par, g, :], tv, eng="act")
        if g == 0 and gs <= 2:
            B.dma(cbm[:, :, :], cbiasm[:, 1 if gs == 1 else 2], "c4", eng="pool")
        B.stt(lg_[:, :, :], psb(bO).m(f3), scale, cb[:, 1, 4 * g:4 * g + 4, :], ALU.mult, ALU.add)
        B.act(PTo_[:, :, :], lg_[:, :, :], AF.Exp)
        if use_prev:
            B.stt(lg_[:, :, :], psb(bP).m(f3), scale, cb[:, 0, 4 * g:4 * g + 4, :], ALU.mult, ALU.add)
            B.act(PTp_[:, :, :], lg_[:, :, :], AF.Exp)
        B.stt(lg_[0:16, :, :], PS(bM)[0:16, bM, :].m(f3), scale, cbm[:, 4 * g:4 * g + 4, :], ALU.mult, ALU.add)
        B.act(PTm_[0:16, :, :], lg_[0:16, :, :], AF.Exp)
        yield
        bV = bank()
        bDn = bank()
        pvl = [(vsw[:, par, g, :], ones_b[:, :], PTo_[:, :, :].m(fl))]
        if use_prev:
            pvl.append((vsw[:, 1 - par, g, :], ones_b[:, :], PTp_[:, :, :].m(fl)))
        pvl.append((vm[:, g, :], ones_b[0:16, :], PTm_[0:16, :, :].m(fl)))
        for i, (lv_, lo_, r_) in enumerate(pvl):
            B.mm(psb(bDn), lo_, r_, start=(i == 0), stop=(i == len(pvl) - 1))
        for i, (lv_, lo_, r_) in enumerate(pvl):
            B.mm(psb(bV), lv_, r_, start=(i == 0), stop=(i == len(pvl) - 1))
        yield
        B.tt(rden_[:, :, :], psb(bDn).m(f3), esink[:, 4 * g:4 * g + 4].bc(2, 128), ALU.add)
        B.act(rden_[:, :, :], rden_[:, :, :], AF.Ln)
        B.act(rden_[:, :, :], rden_[:, :, :], AF.Exp, scale=-1.0)
        B.tt(lg_[:, :, :], psb(bV).m(f3), rden_[:, :, :], ALU.mult)
        B.tt(lg_[:, :, :], lg_[:, :, :], shr[:, 16 + 4 * g:16 + 4 * g + 4, cs], ALU.mult, eng="pool")
        yv = ybuf[:, 4 * g:4 * g + 4, cs]
        B.tt(yv, yv, lg_[:, :, :], ALU.add, eng="pool")
        yield

    def swa(kind, sl, gs):
        cs = slice(sl * 128, sl * 128 + 128)
        smp = kind == "s"
        par = gs % 2
        f3 = lambda a: a.rearrange("p (h n) -> p h n", h=4)
        fl = lambda a: a.rearrange("p h n -> p (h n)")
        scale = 128.0 ** -0.5
        for g in range(2):
            sk_cur = shr[:, 24 + g, cs]
            sq_g = shr2[:, 4 * g:4 * g + 4, cs]
            bt_ = bank()
            tv = psbf(bt_, 128)
            B.tr(tv, svb_t[:, g, cs], ident_b[:, :])
            B.copy(vsw[:, par, g, :], tv, eng="act")
            if kind == "p" and gs == 0:
                B.copy(kmT[:, g, :], shr[:, 24 + g, 112:128], eng="act")
                bt2 = bank()
                tv2 = PS(bt2)[0:16, bt2, 0:64].bitcast(BF16)
                B.tr(tv2, svb_t[:, g, 112:128], ident_b[:, :])
                B.copy(vm[:, g, :], tv2, eng="act")
            if smp:
                B.dma(kmTs[:, :, :], kmT_in[:, g].rearrange("n d s -> d n s"), "c6", eng="pool")
                B.dma(vms[:, :, :], vm_in[:, :, g, :].rearrange("n s d -> s n d"), "c7", eng="pool")
            use_own = smp or gs >= 1
            use_prev = smp or gs >= 2
            var = 2 if smp else (0 if gs == 0 else (1 if gs == 1 else 2))
            if g == 0 and (smp or gs <= 2):
                B.dma(cbm[:, :, :], cbiasm[:, var], "c4", eng="pool")
            bV = bank()
            bDn = bank()
            pvl = []
            dnl = []

            if use_own:
                bO = bank()
                B.mm(psb(bO).m(f3), sk_cur, sq_g)
                bi = cb[:, 1, 4 * g:4 * g + 4, :]
                B.stt(lg[:, :, :], psb(bO).m(f3), scale, bi, ALU.mult, ALU.add)
                B.act(PTo[:, :, :], lg[:, :, :], AF.Exp)
                pvl.append((psb(bV), vsw[:, par, g, :], PTo[:, :, :].m(fl)))
                dnl.append((psb(bDn), ones_b[:, :], PTo[:, :, :].m(fl)))
            if use_prev:
                bP = bank()
                if not smp:
                    kprev = shr[:, 24 + g, (sl - 1) * 128:sl * 128] if sl > 0 else skprev[:, g, :]
                    B.mm(psb(bP).m(f3), kprev, sq_g)
                else:
                    for n in range(NSEQ):
                        sl4 = (n // 4) % 2
                        if n % 4 == 0:
                            B.dma(kbT(sl4)[:, sl4, :, :], kbT_in[n:n + 4, g].rearrange("n d w -> d n w"),
                                  "kb%d" % sl4, eng="pool")
                        for hh in range(4):
                            B.mm(PS(bP)[:, bP, hh * 128 + n * 8:hh * 128 + n * 8 + 8], kbT(sl4)[:, sl4, n % 4, :],
                                 shr2[:, 4 * g + hh, n * 8:n * 8 + 8],
                                 start=(n == 0 and hh == 0), stop=(n == NSEQ - 1 and hh == 3))
                bi = cb[:, 0, 4 * g:4 * g + 4, :]
                B.stt(lg[:, :, :], psb(bP).m(f3), scale, bi, ALU.mult, ALU.add)
                B.act(PTp[:, :, :], lg[:, :, :], AF.Exp)
                dnl.append((psb(bDn), ones_b[:, :], PTp[:, :, :].m(fl)))
                if not smp:
                    pvl.append((psb(bV), vsw[:, 1 - par, g, :], PTp[:, :, :].m(fl)))
                else:
                    for n in range(NSEQ):
                        for hh in range(4):
                            o = PS(bV)[:, bV, hh * 128 + n * 8:hh * 128 + n * 8 + 8]
                            pvl.append((o, ("vb", n), PTp[:, hh, n * 8:n * 8 + 8]))
            bM = bank()
            if not smp:
                B.mm(PS(bM)[0:16, bM, :].m(f3), kmT[:, g, :], sq_g)
            else:
                for n in range(NSEQ):
                    for hh in range(4):
                        B.mm(PS(bM)[0:16, bM, hh * 128 + n * 8:hh * 128 + n * 8 + 8], kmTs[:, n, :],
                             shr2[:, 4 * g + hh, n * 8:n * 8 + 8],
                             start=(n == 0 and hh == 0), stop=(n == NSEQ - 1 and hh == 3))
            B.stt(lg[0:16, :, :], PS(bM)[0:16, bM, :].m(f3), scale, cbm[:, 4 * g:4 * g + 4, :], ALU.mult, ALU.add)
            B.act(PTm[:, :, :], lg[0:16, :, :], AF.Exp)
            dnl.append((psb(bDn), ones_b[0:16, :], PTm[:, :, :].m(fl)))
            if not smp:
                pvl.append((psb(bV), vm[:, g, :], PTm[:, :, :].m(fl)))
            else:
                for n in range(NSEQ):
                    for hh in range(4):
                        o = PS(bV)[:, bV, hh * 128 + n * 8:hh * 128 + n * 8 + 8]
                        pvl.append((o, vms[:, n, :], PTm[:, hh, n * 8:n * 8 + 8]))
            for i, (o, l, r) in enumerate(dnl):
                B.mm(o, l, r, start=(i == 0), stop=(i == len(dnl) - 1))
            for i, (o, l, r) in enumerate(pvl):
                if isinstance(l, tuple):
                    n = l[1]
                    sl4 = (n // 4) % 2
                    if n % 4 == 0 and (i == 0 or pvl[i - 1][1] != l):
                        B.dma(vb(sl4)[:, sl4, :, :], vb_in[n:n + 4, :, g, :].rearrange("n w d -> w n d"),
                              "vb%d" % sl4, eng="pool")
                    l = vb(sl4)[:, sl4, n % 4, :]
                B.mm(o, l, r, start=(i == 0), stop=(i == len(pvl) - 1))
            B.tt(rden[:, :, :], psb(bDn).m(f3), esink[:, 4 * g:4 * g + 4].bc(2, 128), ALU.add)
            B.act(rden[:, :, :], rden[:, :, :], AF.Ln)
            B.act(rden[:, :, :], rden[:, :, :], AF.Exp, scale=-1.0)
            B.tt(lg[:, :, :], psb(bV).m(f3), rden[:, :, :], ALU.mult)
            B.tt(lg[:, :, :], lg[:, :, :], shr[:, 16 + 4 * g:16 + 4 * g + 4, cs], ALU.mult, eng="pool")
            yv = ybuf[:, 4 * g:4 * g + 4, cs]
            B.tt(yv, yv, lg[:, :, :], ALU.add, eng="pool")

    def mixer(kind, m, N, out_ap=None):
        nsub = N // 128
        barrier_shr()
        if kind == "s":
            B.emit("pool", lambda e: e.memset(tmpn.t[0:1, 1, 0:1], 0.0), outs=[V(R1, R1.t[:, :], (None,)), tmpn(1)[0:1, 1, 0:1]])
            B.dma(V(R1, shist.t[:, :, :], (None,)), shistT.rearrange("(c p) x -> p c x", p=128), "c5")
            B.dma(cb[:, 0, :, :], cbias[:, 3], "c3", eng="pool")
            B.dma(cb[:, 1, :, :], cbias[:, 2], "c3b", eng="pool")
        inproj(kind, m, N)
        if kind == "s":
            B.dma(sconv_o.rearrange("(c p) x -> p c x", p=128), sconv[:, :, :], "sc")
            bt_ = bank()
            for i4 in range(4):
                B.tr(psb(bt_, i4 * 128, i4 * 128 + 128), kvf[:, i4, :], cm[:, 7, :])
            B.copy(kvtok[:, :, :], psb(bt_).m(lambda a: a.rearrange("p (h n) -> p h n", h=4)))
            B.dma(swink_o[:, 0:120], kb_in[:, 8:128], "wk0")
            B.dma(swinv_o[:, 0:120], vb_in[:, 8:128], "wv0")
            for n in range(NSEQ):
                B.dma(swink_o[n, 120:128], kvtok[n * 8:n * 8 + 8, 0:2, :], "wk1")
                B.dma(swinv_o[n, 120:128], kvtok[n * 8:n * 8 + 8, 2:4, :], "wv1")
        for sl in range(nsub):
            gs = m * NSUB + sl
            if "dn" not in cfg.get("skip", ()):
                dn(kind, sl, gs)
            if "swa" not in cfg.get("skip", ()) and (kind == "s" or gs == 0):
                swa(kind, sl, gs)
        if kind == "p":
            for g in range(2):
                B.copy(skprev[:, g, :], shr[:, 24 + g, N - 128:N], eng="act")
        stage("outproj")
        st = Stats(N)
        for d in range(8):
            wv = W.get()
            bo = bank()
            for c in range(8):
                B.mm(psb(bo, 0, N), wv(c * 128, c * 128 + 128), ybuf[:, c, :N], start=(c == 0), stop=(c == 7))
            W.done()
            B.copy(yo(d)[:, d, :N], psb(bo, 0, N), eng="act")
            st.add(psb(bo, 0, N))
        postnorm_add(3, N, False, out_ap, st)
        barrier_shr()

    for kind, m in tiles:
        if kind == "p":
            N = NT
            src = xpT[:, m * NT:(m + 1) * NT]
            dst = ypT[:, m * NT:(m + 1) * NT]
        else:
            N = 128
            src = xsT
            dst = ysT
        for c in range(8):
            B.dma(h(c)[:, c, :N], src[c * 128:(c + 1) * 128, :], "xin%d" % c)
        last_is_ffn2 = do_ffn2
        ffn(0, N, None if (do_mixer or do_ffn2) else dst)
        stage("ffn1_done")
        if do_mixer:
            mixer(kind, m, N, None if do_ffn2 else dst)
        if do_ffn2:
            ffn(1, N, dst)
        if do_mixer and kind == "p" and m == nmac - 1:
            pcv = pconv_o.rearrange("(c p) j -> p c j", p=128)
            for q3 in range(3):
                B.dma(pcv[:, 8 * q3:8 * q3 + 8, :], hist[:, 8 * q3:8 * q3 + 8, :], "pc%d" % q3)
            B.dma(pssm_o.rearrange("h k v -> k h v"), S[:, :, :], "pss")

    B.finalize()
    B.close()
    return nc


def _tile_w(w):
    R, C = w.shape
    return np.ascontiguousarray(w.reshape(R // 128, 128, C // 128, 128).transpose(2, 1, 0, 3)).reshape(C // 128, 128, R)


def _consts():
    f = np.float32
    idx = np.arange(128)
    s_ = idx[:, None]
    i_ = idx[None, :]
    same = (s_ // 8) == (i_ // 8)
    cm = np.zeros((128, 9, 128), f)
    cm[:, 0] = s_ <= i_
    cm[:, 1] = s_ > i_
    cm[:, 2] = s_ < i_
    cm[:, 3] = same & (s_ <= i_)
    cm[:, 4] = same & (s_ > i_)
    cm[:, 5] = same & (s_ < i_)
    cm[:, 6] = same
    cm[:, 7] = np.eye(128)
    cm[:, 8] = 1.0
    slopes = (2.0 ** (-(np.arange(8) + 1.0))).astype(np.float64)
    j = idx[:, None, None].astype(np.float64)
    hh = slopes[None, :, None]
    i = idx[None, None, :].astype(np.float64)
    cb = np.zeros((128, 4, 8, 128), f)
    cb[:, 0] = np.where(i <= j, -hh * (i + 128 - j), NEG)
    cb[:, 1] = np.where(i >= j, -hh * (i - j), NEG)
    samej = (idx[:, None, None] // 8) == (idx[None, None, :] // 8)
    cb[:, 2] = np.where(samej & (i >= j), -hh * (i - j), NEG)
    t = (idx[None, None, :] % 8).astype(np.float64)
    cb[:, 3] = np.where(j >= t, -hh * (t + 128 - j), NEG)
    sp = np.arange(16)[:, None, None].astype(np.float64)
    cbm = np.zeros((16, 3, 8, 128), f)
    pos0 = i - 112
    cbm[:, 0] = np.where(pos0 >= sp, -hh * np.minimum(pos0 - sp, 128), NEG)
    pos1 = i + 16
    cbm[:, 1] = -hh * np.minimum(pos1 - sp, 128) + 0 * sp
    cbm[:, 2] = -hh * 128.0 + 0 * sp + 0 * i
    return cm, cb, cbm


def prep_inputs(inp, cfg):
    f = np.float32
    A = lambda k: np.asarray(inp[k], f)
    wg, wd = [], []
    for nm in ("ffn1", "ffn2"):
        g = _tile_w(A(nm + "_w_gate")[0])
        up = _tile_w(A(nm + "_w_up")[0])
        wg.append(np.ascontiguousarray(np.stack([g, up], axis=1).reshape(2 * NFF, 128, 1024)))
        wd.append(_tile_w(A(nm + "_w_down")[0]))
    gl = [A(k)[0] for k in ("ffn1_norm_pre", "ffn1_norm_post", "mix_norm_pre", "mix_norm_post",
                            "ffn2_norm_pre", "ffn2_norm_post")]
    gains = np.ascontiguousarray(np.stack([g.reshape(8, 128).T for g in gl], axis=1))
    w_in = A("w_in")[0]
    cols = np.concatenate([np.arange(0, 4096), np.arange(4112, 7696), np.arange(4096, 4112)])
    w_in = w_in[:, cols]
    win = _tile_w(np.ascontiguousarray(w_in[:, :7680]))
    wba = np.ascontiguousarray(w_in[:, 7680:].reshape(8, 128, 16).transpose(1, 0, 2)).reshape(128, 128)
    wout = _tile_w(A("w_out")[0])
    cm, cb, cbm = _consts()
    vecs = np.zeros((128, NV), f)
    cw = A("dn_conv_w")[0]
    vecs[:, 0:96] = cw.reshape(4, 24, 128).transpose(2, 1, 0).reshape(128, 96)
    vecs[:, 96] = A("dn_norm_w")[0]
    vecs[:, 97:105] = A("dn_a_log")[0][None, :]
    vecs[:, 105:113] = A("dn_dt_bias")[0][None, :]
    vecs[:, 113:121] = A("swa_sinks")[0][None, :]
    vecs[:, 121:137] = (np.arange(128)[:, None] // 8) == np.arange(16)[None, :]
    xp = A("x_prompt")
    xs = A("x_sample")
    meta = A("meta_tokens")
    sconv = A("state_dn_conv")[0]
    sssm = A("state_dn_ssm")[0]
    cmk, cmv = A("cache_swa_meta_k")[0], A("cache_swa_meta_v")[0]
    ck, cv = A("cache_swa_k")[0], A("cache_swa_v")[0]
    maps = []
    for c in range(NCORES):
        m = {}
        seq = np.zeros((TP, D), f)
        if c < 4:
            seq[PADF:PADF + NMETA] = meta
            seq[PADF + NMETA:] = xp[c]
        m["xpT"] = np.ascontiguousarray(seq.T)
        sl = slice(NSEQ * c, NSEQ * (c + 1))
        m["xsT"] = np.ascontiguousarray(xs[sl].reshape(128, D).T)
        m["wgu1"], m["wgu2"] = wg
        m["wdn1"], m["wdn2"] = wd
        m["win"], m["wba"], m["wout"] = win, wba, wout
        m["gains"], m["cmask"], m["cbias"], m["cbiasm"], m["vecs"] = gains, cm, cb, cbm, vecs
        m["shistT"] = np.ascontiguousarray(sconv[sl].transpose(2, 0, 1)).reshape(3072, NSEQ * 3)
        m["s_ssm_in"] = np.ascontiguousarray(sssm[sl])
        m["kb_in"] = np.ascontiguousarray(ck[sl])
        m["kbT_in"] = np.ascontiguousarray(ck[sl].transpose(0, 2, 3, 1))
        m["kmT_in"] = np.ascontiguousarray(cmk[sl].transpose(0, 2, 3, 1))
        m["vb_in"] = np.ascontiguousarray(cv[sl])
        m["vm_in"] = np.ascontiguousarray(cmv[sl])
        maps.append(m)
    return maps


_CACHE = {}


def run_device(inp, cfg):
    key = repr(sorted(cfg.items()))
    if key not in _CACHE:
        _CACHE[key] = build_program(cfg)
    nc = _CACHE[key]
    maps = prep_inputs(inp, cfg)
    res = run_bass_kernel_spmd(nc, maps, core_ids=list(range(NCORES)))
    return res.results


def assemble(r):
    f = np.float32
    y_p = np.stack([r[b]["ypT"][:, 128:].T for b in range(4)]).astype(f)
    y_s = np.concatenate([r[c]["ysT"].T.reshape(NSEQ, DEC_T, D) for c in range(NCORES)]).astype(f)
    p_conv = np.stack([r[b]["pconv_o"].T for b in range(4)])[None].astype(f)
    p_ssm = np.stack([r[b]["pssm_o"] for b in range(4)])[None].astype(f)

    def kv(name, i0, lo=0):
        return np.stack([np.stack([r[b][name][i0][:, lo:].T, r[b][name][i0 + 1][:, lo:].T], axis=1)
                         for b in range(4)])[None].astype(f)
    p_mk, p_mv = kv("pmeta_o", 0, 112), kv("pmeta_o", 2, 112)
    p_wk, p_wv = kv("pwin_o", 0), kv("pwin_o", 2)
    s_conv = np.concatenate([r[c]["sconv_o"].reshape(3072, NSEQ, 3).transpose(1, 2, 0) for c in range(NCORES)])[None]
    s_ssm = np.concatenate([r[c]["sssm_o"] for c in range(NCORES)])[None]
    s_wk = np.concatenate([r[c]["swink_o"] for c in range(NCORES)])[None]
    s_wv = np.concatenate([r[c]["swinv_o"] for c in range(NCORES)])[None]
    return (y_p, y_s, p_conv, p_ssm, p_mk, p_mv, p_wk, p_wv,
            np.ascontiguousarray(s_conv).astype(f), s_ssm.astype(f), s_wk.astype(f), s_wv.astype(f))


def kernel(**inp):
    r = run_device(inp, {})
    return assemble(r)
```

```python
import contextlib
import itertools
import numpy as np
import ml_dtypes
import concourse.bass as bass
import concourse.mybir as mybir
from concourse.bass_utils import run_bass_kernel_spmd

F32 = mybir.dt.float32
BF16 = mybir.dt.bfloat16
AF = mybir.ActivationFunctionType
ALU = mybir.AluOpType
AX = mybir.AxisListType

D = 1024
DFF = 2816
NFF = 22
NCORES = 8
SEQ = 4096
NMETA = 16
PADF = 112
TP = PADF + NMETA + SEQ
NSUBP = TP // 128
DEC_B = 128
DEC_T = 8
NSEQ = DEC_B // NCORES
RMS_EPS = 1e-6
L2_EPS = 1e-6
SLOT = 2816


class Buf:
    def __init__(self, name, t):
        self.name = name
        self.t = t
        self.st = {}
        self.excl = False
        self.defkey = None

    def __call__(self, *keys):
        return _KV(self, keys if keys else (None,))

    def __getitem__(self, idx):
        return V(self, self.t[idx], (self.defkey,))


class _KV:
    def __init__(self, buf, keys):
        self.buf = buf
        self.keys = keys

    def __getitem__(self, idx):
        return V(self.buf, self.buf.t[idx], self.keys)


class V:
    def __init__(self, buf, ap, keys):
        self.buf = buf
        self.ap = ap
        self.keys = keys

    def bitcast(self, dt):
        return V(self.buf, self.ap.bitcast(dt), self.keys)

    def m(self, fn):
        return V(self.buf, fn(self.ap), self.keys)

    def bc(self, axis, n):
        a = self.ap.unsqueeze(axis)
        shp = list(a.shape)
        shp[axis] = n
        return V(self.buf, a.broadcast_to(shp), self.keys)


class Op:
    __slots__ = ("eng", "fn", "deps", "idx", "sig", "semval", "stream", "sval")

    def __init__(self, eng, fn):
        self.eng = eng
        self.fn = fn
        self.deps = []
        self.idx = -1
        self.sig = False
        self.semval = 0
        self.stream = None
        self.sval = 0


ENGS = ("pe", "dve", "act", "pool", "sp")


class Builder:
    def __init__(self, nc):
        self.nc = nc
        self.stack = contextlib.ExitStack()
        self.ops = {e: [] for e in ENGS}
        self.streams = {}
        self.nbuf = 0

    def sb(self, name, shape, dt):
        t = self.stack.enter_context(self.nc.sbuf_tensor(name, list(shape), dt))
        return Buf(name, t)

    def ps(self, name, shape, dt):
        t = self.stack.enter_context(self.nc.psum_tensor(name, list(shape), dt))
        b = Buf(name, t)
        b.excl = True
        return b

    def emit(self, eng, fn, outs=(), ins=(), stream=None):
        if not getattr(self, "enabled", True):
            return None
        op = Op(eng, fn)
        deps = set()
        for v in ins:
            st = v.buf.st
            for k in v.keys:
                ents = list(st.values()) if k is None else [st.get(k), st.get(None)]
                for e in ents:
                    if e is not None:
                        if e[0] is not None:
                            deps.add(e[0])
                        if v.buf.excl:
                            deps.update(r for r in e[1] if r.eng != eng)
        for v in outs:
            st = v.buf.st
            for k in v.keys:
                ents = list(st.values()) if k is None else [st.get(k), st.get(None)]
                for e in ents:
                    if e is not None:
                        if e[0] is not None:
                            deps.add(e[0])
                        deps.update(e[1])
        for v in ins:
            st = v.buf.st
            for k in v.keys:
                e = st.get(k)
                if e is None:
                    e = st[k] = [None, []]
                e[1].append(op)
        for v in outs:
            st = v.buf.st
            for k in v.keys:
                if k is None:
                    st.clear()
                st[k] = [op, []]
        deps.discard(op)
        if stream is not None:
            op.stream = stream
            self.streams[stream] = self.streams.get(stream, 0) + 16
            op.sval = self.streams[stream]
        op.deps = [d for d in deps if not (d.eng == "pe" and eng == "pe" and d.stream is None)]
        for d in op.deps:
            if d.stream is None:
                d.sig = True
        op.idx = len(self.ops[eng])
        self.ops[eng].append(op)
        return op

    def finalize(self):
        nc = self.nc
        st = self.stack
        esem = {e: st.enter_context(nc.semaphore("sem_" + e)) for e in ENGS}
        ssem = {s: st.enter_context(nc.semaphore("ds_" + s)) for s in self.streams}
        last_ops = []
        for e in ENGS:
            cands = [op for op in self.ops[e] if op.stream is None]
            if cands and e != "sp":
                cands[-1].sig = True
                last_ops.append(cands[-1])
        for e in ENGS:
            c = 0
            for op in self.ops[e]:
                if op.sig and op.stream is None:
                    c += 1
                    op.semval = c
        block = st.enter_context(nc.Block())
        final_streams = dict(self.streams)

        def run(e, eng):
            known = {}
            for op in self.ops[e]:
                need = {}
                for d in op.deps:
                    if d.stream is not None:
                        key, val = ("s", d.stream), d.sval
                    else:
                        key, val = ("e", d.eng), d.semval
                    if val > need.get(key, 0):
                        need[key] = val
                for key, val in need.items():
                    if known.get(key, 0) >= val:
                        continue
                    known[key] = val
                    sem = ssem[key[1]] if key[0] == "s" else esem[key[1]]
                    eng.wait_ge(sem, val)
                ins = op.fn(eng)
                if op.stream is not None:
                    ins.then_inc(ssem[op.stream], 16)
                elif op.sig:
                    ins.then_inc(esem[e], 1)
            if e == "sp":
                for s, val in final_streams.items():
                    if known.get(("s", s), 0) < val:
                        eng.wait_ge(ssem[s], val)
                for lo in last_ops:
                    if known.get(("e", lo.eng), 0) < lo.semval:
                        eng.wait_ge(esem[lo.eng], lo.semval)

        @block.tensor
        def _(eng):
            run("pe", eng)

        @block.vector
        def _(eng):
            run("dve", eng)

        @block.scalar
        def _(eng):
            run("act", eng)

        @block.gpsimd
        def _(eng):
            run("pool", eng)

        @block.sync
        def _(eng):
            run("sp", eng)

    def close(self):
        self.stack.close()

    def mm(self, out, lhsT, rhs, start=True, stop=True):
        return self.emit("pe", lambda e: e.matmul(out.ap, lhsT.ap, rhs.ap, start=start, stop=stop),
                         outs=[out], ins=[lhsT, rhs])

    def tr(self, out, in_, ident):
        return self.emit("pe", lambda e: e.transpose(out.ap, in_.ap, ident.ap), outs=[out], ins=[in_, ident])

    def act(self, out, in_, func, scale=1.0, bias=0.0, accum=None, extra_ins=()):
        sc = scale.ap if isinstance(scale, V) else scale
        bi = bias.ap if isinstance(bias, V) else bias
        ins = [in_] + [x for x in (scale, bias) if isinstance(x, V)] + list(extra_ins)
        outs = [out] + ([accum] if accum is not None else [])
        if accum is None:
            fn = lambda e: e.activation(out.ap, in_.ap, func, bias=bi, scale=sc)
        else:
            fn = lambda e: e.activation(out.ap, in_.ap, func, bias=bi, scale=sc, accum_out=accum.ap)
        return self.emit("act", fn, outs=outs, ins=ins)

    def tt(self, out, a, b, op, eng="dve"):
        return self.emit(eng, lambda e: e.tensor_tensor(out.ap, a.ap, b.ap, op), outs=[out], ins=[a, b])

    def ts(self, out, a, s1, op0, s2=None, op1=None, eng="dve"):
        a1 = s1.ap if isinstance(s1, V) else s1
        a2 = s2.ap if isinstance(s2, V) else s2
        ins = [a] + [x for x in (s1, s2) if isinstance(x, V)]
        if op1 is None:
            fn = lambda e: e.tensor_scalar(out.ap, a.ap, a1, None, op0)
        else:
            fn = lambda e: e.tensor_scalar(out.ap, a.ap, a1, a2, op0, op1)
        return self.emit(eng, fn, outs=[out], ins=ins)

    def stt(self, out, a, s, b, op0, op1):
        a1 = s.ap if isinstance(s, V) else s
        ins = [a, b] + ([s] if isinstance(s, V) else [])
        return self.emit("dve", lambda e: e.scalar_tensor_tensor(out.ap, a.ap, a1, b.ap, op0, op1),
                         outs=[out], ins=ins)

    def copy(self, out, in_, eng="dve"):
        if eng == "act":
            return self.emit("act", lambda e: e.copy(out.ap, in_.ap), outs=[out], ins=[in_])
        if eng == "dve":
            return self.emit(eng, lambda e: e.tensor_scalar(out.ap, in_.ap, 1.0, None, ALU.mult), outs=[out], ins=[in_])
        return self.emit(eng, lambda e: e.tensor_copy(out.ap, in_.ap), outs=[out], ins=[in_])

    def memset(self, out, val, eng="dve"):
        return self.emit(eng, lambda e: e.memset(out.ap, val), outs=[out])

    def dma(self, out, in_, stream, eng="sp", ins=(), outs=()):
        oa = out.ap if isinstance(out, V) else out
        ia = in_.ap if isinstance(in_, V) else in_
        o = [out] if isinstance(out, V) else []
        i = [in_] if isinstance(in_, V) else []
        return self.emit(eng, lambda e: e.dma_start(out=oa, in_=ia), outs=o + list(outs), ins=i + list(ins),
                         stream=stream)


NSUB = 3
NT = 128 * NSUB
NEG = -30000.0
NV = 137
SLOTW = 1024


def _inproj_order():
    rest = list(range(24, 60))
    out = []
    for q in range(6):
        out += list(range(4 * q, 4 * q + 4))
        out += rest[6 * q:6 * q + 6]
    return out


INPROJ_ORDER = _inproj_order()


class WStream:
    def __init__(self, B, nslots):
        self.B = B
        self.nslots = nslots
        self.ring = B.sb("wring", [128, nslots, SLOTW], BF16)
        self.sched = []
        self.nload = 0
        self.nuse = 0

    def add(self, dram_ap, size):
        self.sched.append((dram_ap, size))

    def prefetch(self):
        if self.nload >= len(self.sched):
            return
        ap, size = self.sched[self.nload]
        s = self.nload % self.nslots
        if getattr(self, "halfw", False):
            self.B.dma(self.ring(s)[:, s, 0:size // 2], ap[:, 0:size // 2], stream="w%d" % s, eng="pool")
        else:
            self.B.dma(self.ring(s)[:, s, 0:size], ap, stream="w%d" % s, eng="pool")
        self.nload += 1

    def start(self):
        for _ in range(self.nslots):
            self.prefetch()

    def get(self):
        assert self.nuse < self.nload, "weight schedule underflow"
        s = self.nuse % self.nslots
        self.nuse += 1
        ring = self.ring

        def view(lo, hi):
            return ring(s)[:, s, lo:hi]
        return view

    def done(self):
        self.prefetch()


def build_program(cfg):
    nc = bass.Bass("TRN2", target_bir_lowering=False)
    B = Builder(nc)
    nmac = cfg.get("nmac", 11)
    do_sample = cfg.get("sample", True)
    do_mixer = cfg.get("mixer", True)
    do_ffn2 = cfg.get("ffn2", True)
    dbg = cfg.get("dbg", False)

    def din(name, shape, dt=F32):
        return nc.dram_tensor(name, list(shape), dt, kind="ExternalInput").ap()

    def dout(name, shape, dt=F32):
        return nc.dram_tensor(name, list(shape), dt, kind="ExternalOutput").ap()

    xpT = din("xpT", [D, TP])
    ypT = dout("ypT", [D, TP])
    xsT = din("xsT", [D, 128])
    ysT = dout("ysT", [D, 128])
    wgu = [din("wgu%d" % i, [2 * NFF, 128, 1024]) for i in (1, 2)]
    wdn = [din("wdn%d" % i, [8, 128, DFF]) for i in (1, 2)]
    win = din("win", [60, 128, 1024])
    wba = din("wba", [128, 128])
    wout = din("wout", [8, 128, 1024])
    gains = din("gains", [128, 6, 8])
    cmask = din("cmask", [128, 9, 128])
    cbias = din("cbias", [128, 4, 8, 128])
    cbiasm = din("cbiasm", [16, 3, 8, 128])
    vecs = din("vecs", [128, NV])
    shistT = din("shistT", [3072, NSEQ * 3])
    s_ssm_in = din("s_ssm_in", [NSEQ, 8, 128, 128])
    kbT_in = din("kbT_in", [NSEQ, 2, 128, 128])
    kmT_in = din("kmT_in", [NSEQ, 2, 128, 16])
    vb_in = din("vb_in", [NSEQ, 128, 2, 128])
    vm_in = din("vm_in", [NSEQ, 16, 2, 128])
    kb_in = din("kb_in", [NSEQ, 128, 2, 128])
    pconv_o = dout("pconv_o", [3072, 3])
    pssm_o = dout("pssm_o", [8, 128, 128])
    pmeta_o = dout("pmeta_o", [4, 128, 128])
    pwin_o = dout("pwin_o", [4, 128, 128])
    sconv_o = dout("sconv_o", [3072, NSEQ * 3])
    sssm_o = dout("sssm_o", [NSEQ, 8, 128, 128])
    swink_o = dout("swink_o", [NSEQ, 128, 2, 128])
    swinv_o = dout("swinv_o", [NSEQ, 128, 2, 128])
    dbg_o = dout("dbg_o", [128, 8, NT]) if dbg else None

    cm = B.sb("cm", [128, 9, 128], F32)
    ident_b = B.sb("ident_b", [128, 128], BF16)
    ones_b = B.sb("ones_b", [128, 128], BF16)
    cb = B.sb("cb", [128, 2, 8, 128], BF16)
    cbm = B.sb("cbm", [16, 8, 128], BF16)
    vc = B.sb("vc", [128, NV], F32)
    negA = B.sb("negA", [128, 8], F32)
    esink = B.sb("esink", [128, 8], F32)
    gn = B.sb("gn", [128, 6, 8], F32)
    gnh = B.sb("gnh", [128, 6, 8], F32)
    mhalf = B.sb("mhalf", [128, 1], F32)
    epsb = B.sb("epsb", [128, 1], F32)
    h = B.sb("h", [128, 8, NT], F32)
    u = B.sb("u", [128, 8, NT], BF16)
    sqs = B.sb("sqs", [128, 2, NT], BF16)
    shr = B.sb("shr", [128, 27, NT], BF16)
    shr2 = B.sb("shr2", [128, 9, NT], BF16)
    yo = B.sb("yo", [128, 8, NT], F32)
    sg = B.sb("sg", [128, 2, NT], F32)
    ms = B.sb("ms", [128, NT], F32)
    rinv = B.sb("rinv", [128, NT], F32)
    tmpn = B.sb("tmpn", [128, 2, NT], F32)
    PS = B.ps("PS", [128, 8, 512], F32)
    W = WStream(B, cfg.get("nslots", 9))
    W.halfw = cfg.get("halfw", False)
    psn = [0]

    I_f = cm[:, 7, :]
    ONES_f = cm[:, 8, :]

    reserved = set()

    def bank():
        while True:
            b = psn[0] % 8
            psn[0] += 1
            if b not in reserved:
                return b

    def psb(b, lo=0, hi=512):
        return PS(b)[:, b, lo:hi]

    def psbf(b, n):
        return PS(b)[:, b, 0:(n + 1) // 2].bitcast(BF16)

    B.dma(cm[:, :, :], cmask, "c0")
    B.dma(gn[:, :, :], gains, "c1")
    B.dma(vc[:, :], vecs, "c2")
    B.dma(cb[:, :, :, :], cbias[:, 0:2], "c3", eng="pool")
    B.copy(ident_b[:, :], cm[:, 7, :])
    B.copy(ones_b[:, :], cm[:, 8, :])
    B.ts(gnh[:, :, :], gn[:, :, :], 0.5, ALU.mult)
    B.memset(mhalf[:, :], -0.5)
    B.memset(epsb[:, :], RMS_EPS)
    convw = lambda c, j: vc[:, c * 4 + j:c * 4 + j + 1]
    dnw = vc[:, 96:97]
    alog = vc[:, 97:105]
    dtb = vc[:, 105:113]
    sinks = vc[:, 113:121]
    seqsel = vc[:, 121:137]
    B.act(negA[:, :], alog, AF.Exp)
    B.ts(negA[:, :], negA[:, :], -1.0, ALU.mult)
    B.act(esink[:, :], sinks, AF.Exp)

    def sched_ffn(i):
        for j in range(NFF):
            W.add(wgu[i][2 * j], 1024)
            W.add(wgu[i][2 * j + 1], 1024)
        for d in range(8):
            W.add(wdn[i][d, :, 0:1024], 1024)
            W.add(wdn[i][d, :, 1024:2048], 1024)
            W.add(wdn[i][d, :, 2048:2816], 768)

    def sched_mix():
        for j in INPROJ_ORDER:
            W.add(win[j], 1024)
        W.add(wba, 128)
        for d in range(8):
            W.add(wout[d], 1024)

    tiles = [("p", m) for m in range(nmac)] + ([("s", 0)] if do_sample else [])
    for _ in tiles:
        sched_ffn(0)
        if do_mixer:
            sched_mix()
        if do_ffn2:
            sched_ffn(1)
    W.start()

    sqn = [0]

    class Stats:
        def __init__(self, N):
            self.N = N
            self.b = bank()
            reserved.add(self.b)
            self.pend = []
            self.n = 0

        def add(self, src_ps):
            k = sqn[0] % 2
            sqn[0] += 1
            B.act(sqs(k)[:, k, :self.N], src_ps, AF.Square)
            self.pend.append(k)
            if len(self.pend) > 1:
                self.flush1()

        def flush1(self):
            k = self.pend.pop(0)
            B.mm(psb(self.b, 0, self.N), ones_b[:, :], sqs(k)[:, k, :self.N], start=(self.n == 0), stop=(self.n == 7))
            self.n += 1

        def finish(self):
            while self.pend:
                self.flush1()
            reserved.discard(self.b)
            return self.b

    def rms_rinv(src, N, scale, stats=None):
        if stats is not None:
            b = stats.finish()
        else:
            b = bank()
        for c in range(8 if stats is None else 0):
            k = sqn[0] % 2
            sqn[0] += 1
            B.act(sqs(k)[:, k, :N], src(c)[:, c, :N], AF.Square)
            B.mm(psb(b, 0, N), ones_b[:, :], sqs(k)[:, k, :N], start=(c == 0), stop=(c == 7))
        B.act(ms[:, :N], psb(b, 0, N), AF.Ln, scale=scale, bias=epsb[:, 0:1])
        B.act(rinv[:, :N], ms[:, :N], AF.Exp, scale=-0.5)

    def prenorm(gi, N):
        rms_rinv(h, N, 1.0 / D)
        for c in range(8):
            B.stt(u(c)[:, c, :N], h(c)[:, c, :N], gn[:, gi, c:c + 1], rinv[:, :N], ALU.mult, ALU.mult)

    def postnorm_add(gi, N, half, out_ap=None, stats=None):
        rms_rinv(yo, N, 1.0 / D, stats)
        gsrc = gnh if half else gn
        for c in range(8):
            k = c % 2
            B.stt(tmpn(k)[:, k, :N], yo(c)[:, c, :N], gsrc[:, gi, c:c + 1], rinv[:, :N], ALU.mult, ALU.mult)
            if out_ap is None:
                B.tt(h(c)[:, c, :N], h(c)[:, c, :N], tmpn(k)[:, k, :N], ALU.add, eng="pool")
            else:
                B.tt(yo(c)[:, c, :N], h(c)[:, c, :N], tmpn(k)[:, k, :N], ALU.add, eng="pool")
                B.dma(out_ap[c * 128:(c + 1) * 128, :], yo(c)[:, c, :N], "yout%d" % c)

    def ffn(fi, N, out_ap=None):
        gi = 0 if fi == 0 else 4
        prenorm(gi, N)
        for j in range(NFF):
            wg = W.get()
            bg = bank()
            for c in range(8):
                B.mm(psb(bg, 0, N), wg(c * 128, c * 128 + 128), u(c)[:, c, :N], start=(c == 0), stop=(c == 7))
            W.done()
            wu = W.get()
            bu = bank()
            for c in range(8):
                B.mm(psb(bu, 0, N), wu(c * 128, c * 128 + 128), u(c)[:, c, :N], start=(c == 0), stop=(c == 7))
            W.done()
            k = j % 2
            B.act(sg(k)[:, k, :N], psb(bg, 0, N), AF.Silu)
            B.tt(shr(j)[:, j, :N], sg(k)[:, k, :N], psb(bu, 0, N), ALU.mult)
        st = Stats(N)
        for d in range(8):
            bo = bank()
            for hf, (j0, nj) in enumerate(((0, 8), (8, 8), (16, 6))):
                wd = W.get()
                for jj in range(nj):
                    j = j0 + jj
                    B.mm(psb(bo, 0, N), wd(jj * 128, jj * 128 + 128), shr(j)[:, j, :N], start=(j == 0),
                         stop=(j == NFF - 1))
                W.done()
            B.copy(yo(d)[:, d, :N], psb(bo, 0, N), eng="act")
            st.add(psb(bo, 0, N))
        postnorm_add(gi + 1, N, True, out_ap, st)

    def alias(parent, ap):
        x = Buf(parent.name + "_al", ap)
        x.st = parent.st
        return x

    if do_mixer:
        qk = B.sb("qk", [128, 2, 8, NT], BF16)
        vT = B.sb("vT", [128, 8, NT], BF16)
        ybuf = u
        xp = B.sb("xp", [128, 4, NT + 3], F32)
        xps = alias(xp, xp.t[:, :, 0:NSEQ * 11].rearrange("p k (n j) -> p k n j", j=11))
        acc = B.sb("acc", [128, 4, NT], F32)
        qs = B.sb("qs", [128, 3, NT], F32)
        ms2 = tmpn
        oh16 = B.sb("oh16", [128, 16, 16], BF16)
        hist = B.sb("hist", [128, 24, 3], F32)
        ba = B.sb("ba", [128, NSUB, 16], F32)
        beta = B.sb("beta", [128, NSUB, 8], F32)
        gg = B.sb("gg", [128, NSUB, 8], F32)
        ge = B.sb("ge", [128, NSUB, 8], F32)
        gx = B.sb("gx", [128, NSUB, 8], F32)
        sm = B.sb("sm", [128, 8, 8], F32)
        cdec_s = B.sb("cdec_s", [128, NSEQ, 8], F32)
        Rs = B.sb("Rs", [128, NSEQ, 8], F32)
        kvf = B.sb("kvf", [128, 4, 128], F32)
        kvtok = B.sb("kvtok", [128, 4, 128], F32)
        skprev = B.sb("skprev", [128, 2, 128], BF16)
        kmT = B.sb("kmT", [128, 2, 16], BF16)
        vm = B.sb("vm", [16, 2, 128], BF16)
        vsw = B.sb("vsw", [128, 2, 2, 128], BF16)
        S = B.sb("S", [128, 8, 128], F32)
        Sbf = B.sb("Sbf", [128, 8, 128], BF16)
        def tset0():
            tA_ = B.sb("tA", [128, 4, 128], F32)
            tB_ = B.sb("tB", [128, 4, 128], F32)
            tC_ = B.sb("tC", [128, 4, 128], F32)
            dI_ = B.sb("decI", [128, 4, 128], F32)
            rest = [B.sb(nm, [128, 4, 128], BF16) for nm in ("PU", "PL", "XU", "XL", "qkTm", "r0", "uu", "Vtok", "kdec",
                                                            "on_t")]
            return [tA_, tB_, tC_, dI_] + rest + [B.sb("ssum", [128, 3, 4], F32)]

        R1 = B.sb("R1", [128, 4 * 512 + 10 * 256 + 16], F32)

        def tset1():
            out = []
            off = 0
            for i in range(4):
                x = alias(R1, R1.t[:, off:off + 512].rearrange("p (h n) -> p h n", h=4))
                x.defkey = "f%d" % i
                out.append(x)
                off += 512
            for i in range(10):
                x = alias(R1, R1.t[:, off:off + 256].bitcast(BF16).rearrange("p (h n) -> p h n", h=4))
                x.defkey = "b%d" % i
                out.append(x)
                off += 256
            x = alias(R1, R1.t[:, off:off + 12].rearrange("p (a b) -> p a b", a=3))
            x.defkey = "ss"
            out.append(x)
            return out

        TS = [tset0(), tset1()]
        shist = alias(R1, R1.t[:, 0:1152].rearrange("p (c x) -> p c x", c=24))
        sconv = alias(R1, R1.t[:, 1152:2304].rearrange("p (c x) -> p c x", c=24))
        tA, tB, tC, decI = TS[0][0:4]
        lg = decI
        PTo = B.sb("PTo", [128, 4, 128], BF16)
        PTp = B.sb("PTp", [128, 4, 128], BF16)
        PTm = B.sb("PTm", [16, 4, 128], BF16)
        rden = B.sb("rden", [128, 4, 128], F32)
        svb_t = B.sb("svb_t", [128, 2, NT], BF16)
        KQT = alias(S, S.t[:, :, :].rearrange("p a b -> p (a b)").bitcast(BF16).rearrange(
            "p (h w t) -> p h w t", h=8, w=2))
        Sn = B.sb("Sn", [128, 2, 8, 128], BF16)
        Uexp = alias(qs, qs.t[:, :, :].rearrange("p a b -> p (a b)")[:, 0:1024].bitcast(BF16).rearrange(
            "p (n d) -> p n d", n=NSEQ))
        Sold = B.sb("Sold", [128, 4, 128], F32)
        Snew = B.sb("Snew", [128, 4, 128], F32)
        kbT = B.sb("kbT", [128, 2, 4, 128], BF16)
        vb = B.sb("vb", [128, 2, 4, 128], BF16)
        kmTs = B.sb("kmTs", [128, NSEQ, 16], BF16)
        vms = B.sb("vms", [16, NSEQ, 128], BF16)

        B.memset(oh16[:, :, :], 0.0)
        for c in range(16):
            B.memset(oh16[:, c, c:c + 1], 1.0)
        B.memset(hist[:, :, :], 0.0)
        B.memset(S[:, :, :], 0.0)
        B.memset(Sbf[:, :, :], 0.0)

    def barrier_shr():
        B.emit("pool", lambda e: e.memset(tmpn.t[0:1, 0, 0:1], 0.0), outs=[shr[:, :, :], tmpn(0)[0:1, 0, 0:1]])

    def stage(name):
        if cfg.get("stop") == name:
            B.enabled = False

    def inproj(kind, m, N):
        nsub = N // 128
        stage("ip_start")
        prenorm(2, N)
        bss = bank()
        reserved.add(bss)

        def qkv_post(items):
            ks = [c % 4 for c, _ in items]
            if kind == "p":
                for (c, b), k in zip(items, ks):
                    B.copy(xp(k)[:, k, 0:3], hist(c)[:, c, :], eng="pool")
                for (c, b), k in zip(items, ks):
                    B.copy(xp(k)[:, k, 3:3 + N], psb(b, 0, N), eng="act")
                for (c, b), k in zip(items, ks):
                    B.copy(hist(c)[:, c, :], xp(k)[:, k, N:N + 3], eng="pool")
                for (c, b), k in zip(items, ks):
                    B.ts(acc(k)[:, k, :N], xp(k)[:, k, 0:N], convw(c, 0), ALU.mult)
                for j in range(1, 4):
                    for (c, b), k in zip(items, ks):
                        B.stt(acc(k)[:, k, :N], xp(k)[:, k, j:j + N], convw(c, j), acc(k)[:, k, :N], ALU.mult, ALU.add)
            else:
                f3n = lambda a: a.rearrange("p (n j) -> p n j", j=3)
                f8 = lambda a: a.rearrange("p (n t) -> p n t", t=8)
                for (c, b), k in zip(items, ks):
                    B.copy(xps(k)[:, k, :, 0:3], shist(c)[:, c, :].m(f3n), eng="pool")
                for (c, b), k in zip(items, ks):
                    B.copy(xps(k)[:, k, :, 3:11], psb(b, 0, N).m(f8), eng="act")
                for (c, b), k in zip(items, ks):
                    B.copy(sconv(c)[:, c, :].m(f3n), xps(k)[:, k, :, 8:11], eng="pool")
                for (c, b), k in zip(items, ks):
                    B.ts(acc(k)[:, k, :N].m(f8), xps(k)[:, k, :, 0:8], convw(c, 0), ALU.mult)
                for j in range(1, 4):
                    for (c, b), k in zip(items, ks):
                        B.stt(acc(k)[:, k, :N].m(f8), xps(k)[:, k, :, j:j + 8], convw(c, j), acc(k)[:, k, :N].m(f8),
                              ALU.mult, ALU.add)
            return items, ks

        def qkv_post2(items, ks):
            for (c, b), k in zip(items, ks):
                if c < 16:
                    which = 1 if c < 8 else 0
                    hd_i = c % 8
                    B.act(qk((which, hd_i))[:, which, hd_i, :N], acc(k)[:, k, :N], AF.Silu)
                else:
                    B.act(vT(c - 16)[:, c - 16, :N], acc(k)[:, k, :N], AF.Silu)
            for (c, b), k in zip(items, ks):
                if c < 16:
                    which = 1 if c < 8 else 0
                    hd_i = c % 8
                    k2 = sqn[0] % 2
                    sqn[0] += 1
                    B.act(sqs(k2)[:, k2, :N], qk((which, hd_i))[:, which, hd_i, :N], AF.Square)
                    B.mm(PS(bss)[0:16, bss, 0:N], oh16[:, c, :], sqs(k2)[:, k2, :N], start=(c == 0), stop=(c == 15))

        pair = []
        pend2 = None
        for blk in INPROJ_ORDER:
            wv = W.get()
            b = bank()
            for c in range(8):
                B.mm(psb(b, 0, N), wv(c * 128, c * 128 + 128), u(c)[:, c, :N], start=(c == 0), stop=(c == 7))
            W.done()
            if blk == 24:
                stage("ip_blk24")
            if blk == 44:
                stage("ip_blk44")
            if blk < 24:
                pair.append((blk, b))
                if len(pair) == 4:
                    if pend2 is not None:
                        qkv_post2(*pend2)
                    pend2 = qkv_post(pair)
                    pair = []
                continue
            if True:
                if blk < 32:
                    c = blk - 24
                    B.act(shr(c)[:, c, :N], psb(b, 0, N), AF.Silu)
                elif blk < 40:
                    c = blk - 32
                    B.copy(shr2(c)[:, c, :N], psb(b, 0, N), eng="act")
                elif blk < 44:
                    i4 = blk - 40
                    if i4 < 2:
                        B.copy(shr(24 + i4)[:, 24 + i4, :N], psb(b, 0, N), eng="act")
                    else:
                        B.copy(svb_t(i4 - 2)[:, i4 - 2, :N], psb(b, 0, N), eng="act")
                    if kind == "s":
                        B.copy(kvf(i4)[:, i4, :], psb(b, 0, 128), eng="act")
                    elif m == 0:
                        if not cfg.get("no_kvf"):
                            B.copy(kvf(i4)[:, i4, :], psb(b, 0, 128), eng="act")
                        if not cfg.get("no_pm"):
                            B.dma(pmeta_o[i4], kvf(i4)[:, i4, :], "pm%d" % i4)
                    elif m == nmac - 1:
                        B.copy(kvf(i4)[:, i4, :], psb(b, N - 128, N), eng="act")
                        B.dma(pwin_o[i4], kvf(i4)[:, i4, :], "pm%d" % i4)
                else:
                    c = blk - 44
                    k = c % 2
                    B.act(sg(k)[:, k, :N], psb(b, 0, N), AF.Tanh, scale=0.5)
                    B.ts(shr(8 + c)[:, 8 + c, :N], sg(k)[:, k, :N], 0.5, ALU.mult, 0.5, ALU.add)
        if pend2 is not None:
            qkv_post2(*pend2)
        stage("ip_ba")
        wv = W.get()
        b = bank()
        for s in range(nsub):
            for c in range(8):
                B.mm(psb(b, s * 16, s * 16 + 16), u(c)[:, c, s * 128:(s + 1) * 128], wv(c * 16, c * 16 + 16),
                     start=(c == 0), stop=(c == 7))
        W.done()
        B.copy(ba[:, 0:nsub, :], psb(b, 0, nsub * 16).m(lambda a: a.rearrange("p (s x) -> p s x", x=16)), eng="act")
        B.act(ms[0:16, :N], PS(bss)[0:16, bss, 0:N], AF.Ln, bias=epsb[0:16, 0:1])
        B.act(rinv[0:16, :N], ms[0:16, :N], AF.Exp, scale=-0.5)
        reserved.discard(bss)
        for c in range(16):
            which = 1 if c < 8 else 0
            hd_i = c % 8
            k = c % 2
            B.ts(ms2(k)[0:16, k, :N], rinv[0:16, :N], cm[0:16, 7, c:c + 1], ALU.mult)
            bb = bank()
            B.mm(psb(bb, 0, N), cm[0:16, 8, :], ms2(k)[0:16, k, :N])
            sc = (128.0 ** -0.5) if which == 1 else 1.0
            B.stt(qk((which, hd_i))[:, which, hd_i, :N], qk((which, hd_i))[:, which, hd_i, :N], sc, psb(bb, 0, N),
                  ALU.mult, ALU.mult)
        B.act(beta[:, 0:nsub, :], ba[:, 0:nsub, 0:8], AF.Exp, scale=-1.0)
        B.act(beta[:, 0:nsub, :], beta[:, 0:nsub, :], AF.Ln, bias=1.0)
        B.act(beta[:, 0:nsub, :], beta[:, 0:nsub, :], AF.Exp, scale=-1.0)
        B.tt(gg[:, 0:nsub, :], ba[:, 0:nsub, 8:16], dtb.bc(1, nsub), ALU.add)
        B.act(ge[:, 0:nsub, :], gg[:, 0:nsub, :], AF.Exp)
        B.act(gg[:, 0:nsub, :], ge[:, 0:nsub, :], AF.Ln, bias=1.0)
        B.act(gx[:, 0:nsub, :], gg[:, 0:nsub, :], AF.Exp, scale=-1.0)
        B.stt(gx[:, 0:nsub, :], ge[:, 0:nsub, :], 1.0, gx[:, 0:nsub, :], ALU.add, ALU.mult)
        B.stt(gg[:, 0:nsub, :], gx[:, 0:nsub, :], -1.0, gg[:, 0:nsub, :], ALU.add, ALU.add)
        B.tt(gg[:, 0:nsub, :], gg[:, 0:nsub, :], negA[:, :].bc(1, nsub), ALU.mult)

    snrot = [0]

    def dn(kind, sl, gs):
        cs = slice(sl * 128, sl * 128 + 128)
        smp = kind == "s"
        Mincl = cm[:, 3, :] if smp else cm[:, 0, :]
        Mgt = cm[:, 4, :] if smp else cm[:, 1, :]
        maskS = cm[:, 5, :] if smp else cm[:, 2, :]
        Mall = cm[:, 6, :] if smp else cm[:, 8, :]
        nlev = 1 if smp else 5
        g_s = gg[:, sl, :]
        bsm = bank()
        B.mm(psb(bsm, 0, 8), Mincl, g_s)
        B.mm(psb(bsm, 8, 16), Mall, g_s)
        gc = sm[:, 0, :]
        egc = sm[:, 1, :]
        negegc = sm[:, 2, :]
        kd = sm[:, 3, :]
        kscale = sm[:, 4, :]
        cdec = sm[:, 5, :]
        B.copy(gc, psb(bsm, 0, 8))
        B.act(egc, psb(bsm, 0, 8), AF.Exp)
        B.ts(negegc, egc, -1.0, ALU.mult)
        B.tt(kd, psb(bsm, 8, 16), gc, ALU.subtract)
        B.act(kscale, kd, AF.Exp)
        if not smp:
            B.act(cdec, psb(bsm, 8, 16), AF.Exp)
        else:
            B.tt(Rs[:, :, :], g_s.bc(1, NSEQ), seqsel.bc(2, 8), ALU.mult)
            b2 = bank()
            B.mm(psb(b2, 0, 128), ONES_f, Rs[:, :, :].m(lambda a: a.rearrange("p n h -> p (n h)")))
            B.act(cdec_s[:, :, :].m(lambda a: a.rearrange("p n h -> p (n h)")), psb(b2, 0, 128), AF.Exp)
            xb = [bank() for _ in range(4)]
            for n in range(NSEQ):
                r = snrot[0] % 2
                snrot[0] += 1
                B.dma(Sn(r)[:, r, :, :], s_ssm_in[n].rearrange("h k v -> k h v"), "sn%d" % r, eng="pool")
                for hh in range(8):
                    for w in range(2):
                        c0 = (hh % 2) * 256 + w * 128 + n * 8
                        B.mm(PS(xb[hh // 2])[:, xb[hh // 2], c0:c0 + 8], Sn(r)[:, r, hh, :],
                             qk[:, w, hh, n * 8:n * 8 + 8])
            for i in range(4):
                B.copy(KQT[:, 2 * i:2 * i + 2, :, :].m(lambda a: a.rearrange("p h w t -> p (h w t)")), psb(xb[i]),
                       eng=("act" if i % 2 else "dve"))

        def grp(G, T):
            tA, tB, tC, decI, PU, PL, XU, XL, qkTm, r0, uu, Vtok, kdec, on_t, ssum = T
            o_t = tA
            hs = [4 * G + i for i in range(4)]
            bt = beta[:, sl, 4 * G:4 * G + 4]
            B.tt(tA[:, :, :], Mgt.bc(1, 4), g_s.m(lambda a: a[:, 4 * G:4 * G + 4]).bc(2, 128), ALU.mult, eng="pool")
            bD = bank()
            for i in range(4):
                B.mm(psb(bD, i * 128, i * 128 + 128), tA[:, i, :], Mincl)
            B.act(decI[:, :, :].m(lambda a: a.rearrange("p h n -> p (h n)")), psb(bD), AF.Exp)
            B.tt(decI[:, :, :], decI[:, :, :], Mincl.bc(1, 4), ALU.mult, eng="pool")
            yield
            bK = bank()
            bQ = bank()
            for i, hh in enumerate(hs):
                B.mm(psb(bK, i * 128, i * 128 + 128), qk[:, 0, hh, cs], qk[:, 0, hh, cs])
            for i, hh in enumerate(hs):
                B.mm(psb(bQ, i * 128, i * 128 + 128), qk[:, 0, hh, cs], qk[:, 1, hh, cs])
            f3 = lambda a: a.rearrange("p (h n) -> p h n", h=4)
            B.tt(qkTm[:, :, :], psb(bQ).m(f3), decI[:, :, :], ALU.mult)
            B.tt(tB[:, :, :], psb(bK).m(f3), decI[:, :, :], ALU.mult)
            B.tt(tC[:, :, :], maskS.bc(1, 4), bt.bc(2, 128), ALU.mult, eng="pool")
            B.tt(tB[:, :, :], tB[:, :, :], tC[:, :, :], ALU.mult, eng="pool")
            B.copy(PU[:, :, :], tB[:, :, :], eng="act")
            yield
            bT = bank()
            tv = psbf(bT, 512).m(f3)
            for i in range(4):
                B.tr(V(PS, tv.ap[:, i, :], (bT,)), PU[:, i, :], ident_b[:, :])
            B.copy(PL[:, :, :], tv, eng="act")
            B.tt(XU[:, :, :], ident_b[:, :].bc(1, 4), PU[:, :, :], ALU.subtract)
            B.tt(XL[:, :, :], ident_b[:, :].bc(1, 4), PL[:, :, :], ALU.subtract, eng="pool")
            yield
            b1 = bank()
            b2 = bank()
            for i in range(4):
                B.mm(psb(b1, i * 128, i * 128 + 128), PL[:, i, :], PU[:, i, :])
            for i in range(4):
                B.mm(psb(b2, i * 128, i * 128 + 128), PU[:, i, :], PL[:, i, :])
            yield
            B.copy(PU[:, :, :], psb(b1).m(f3), eng="act")
            B.copy(PL[:, :, :], psb(b2).m(f3))
            yield
            for lv in range(nlev):
                last = lv == nlev - 1
                b3 = bank()
                b4 = bank()
                for i in range(4):
                    B.mm(psb(b3, i * 128, i * 128 + 128), XL[:, i, :], PU[:, i, :])
                for i in range(4):
                    B.mm(psb(b4, i * 128, i * 128 + 128), XU[:, i, :], PL[:, i, :])
                if not last:
                    b1 = bank()
                    b2 = bank()
                    for i in range(4):
                        B.mm(psb(b1, i * 128, i * 128 + 128), PL[:, i, :], PU[:, i, :])
                    for i in range(4):
                        B.mm(psb(b2, i * 128, i * 128 + 128), PU[:, i, :], PL[:, i, :])
                yield
                B.tt(XU[:, :, :], XU[:, :, :], psb(b3).m(f3), ALU.add)
                B.tt(XL[:, :, :], XL[:, :, :], psb(b4).m(f3), ALU.add, eng="dve")
                if not last:
                    B.copy(PU[:, :, :], psb(b1).m(f3), eng="act")
                    B.copy(PL[:, :, :], psb(b2).m(f3), eng="act")
                yield
            bA = bank()
            for i in range(4):
                B.tr(psb(bA, i * 128, i * 128 + 128), tB[:, i, :], I_f)
            B.tt(tC[:, :, :], psb(bA).m(f3), I_f.bc(1, 4), ALU.add)
            B.copy(decI[:, :, :], XU[:, :, :], eng="act")
            yield
            bR = bank()
            for i in range(4):
                B.mm(psb(bR, i * 128, i * 128 + 128), tC[:, i, :], decI[:, i, :])
            B.tt(PU[:, :, :], I_f.bc(1, 4), psb(bR).m(f3), ALU.subtract)
            yield
            bX = bank()
            for i in range(4):
                B.mm(psb(bX, i * 128, i * 128 + 128), XL[:, i, :], PU[:, i, :])
            B.tt(XU[:, :, :], XU[:, :, :], psb(bX).m(f3), ALU.add)
            yield
            bKS = bank()
            bQS = bank()
            if not smp:
                for i, hh in enumerate(hs):
                    B.mm(psb(bKS, i * 128, i * 128 + 128), qk[:, 0, hh, cs], Sbf(G)[:, hh, :])
                for i, hh in enumerate(hs):
                    B.mm(psb(bQS, i * 128, i * 128 + 128), qk[:, 1, hh, cs], Sbf(G)[:, hh, :])
                KSv = psb(bKS).m(f3)
                QSv = psb(bQS).m(f3)
            else:
                kv_ = psbf(bKS, 512).m(f3)
                qv_ = psbf(bQS, 512).m(f3)
                for i, hh in enumerate(hs):
                    B.tr(V(PS, kv_.ap[:, i, :], (bKS,)), KQT[:, hh, 0, :], ident_b[:, :])
                    B.tr(V(PS, qv_.ap[:, i, :], (bQS,)), KQT[:, hh, 1, :], ident_b[:, :])
                KSv = kv_
                QSv = qv_
            bV = bank()
            vv = psbf(bV, 512).m(f3)
            for i, hh in enumerate(hs):
                B.tr(V(PS, vv.ap[:, i, :], (bV,)), vT[:, hh, cs], ident_b[:, :])
            B.copy(Vtok[:, :, :], vv, eng="act")
            yield
            B.tt(tB[:, :, :], KSv, negegc.m(lambda a: a[:, 4 * G:4 * G + 4]).bc(2, 128), ALU.mult)
            B.tt(r0[:, :, :], tB[:, :, :], Vtok[:, :, :], ALU.add, eng="pool")
            yield
            bU = bank()
            for i in range(4):
                B.mm(psb(bU, i * 128, i * 128 + 128), XU[:, i, :], r0[:, i, :])
            B.tt(uu[:, :, :], psb(bU).m(f3), bt.bc(2, 128), ALU.mult)
            yield
            bO = bank()
            for i in range(4):
                B.mm(psb(bO, i * 128, i * 128 + 128), qkTm[:, i, :], uu[:, i, :])
            B.tt(tB[:, :, :], QSv, egc.m(lambda a: a[:, 4 * G:4 * G + 4]).bc(2, 128), ALU.mult)
            B.tt(o_t[:, :, :], tB[:, :, :], psb(bO).m(f3), ALU.add)
            yield
            B.tt(tC[:, :, :], o_t[:, :, :], o_t[:, :, :], ALU.mult, eng="pool")
            B.emit("dve", lambda e: e.tensor_reduce(ssum.t[:, 0, :], tC.t[:, :, :], AX.X, ALU.add),
                   outs=[ssum[:, 0, :]], ins=[tC[:, :, :]])
            B.act(ssum[:, 1, :], ssum[:, 0, :], AF.Ln, scale=1.0 / 128.0, bias=epsb[:, 0:1])
            B.act(ssum[:, 2, :], ssum[:, 1, :], AF.Exp, scale=-0.5)
            B.tt(on_t[:, :, :], o_t[:, :, :], ssum[:, 2, :].bc(2, 128), ALU.mult, eng="pool")
            yield
            bN = bank()
            nv = psbf(bN, 512).m(f3)
            for i in range(4):
                B.tr(V(PS, nv.ap[:, i, :], (bN,)), on_t[:, i, :], ident_b[:, :])
            zsv = shr[:, 4 * G:4 * G + 4, cs]
            gdv = shr[:, 8 + 4 * G:8 + 4 * G + 4, cs]
            B.stt(tB[:, :, :], nv, dnw, zsv, ALU.mult, ALU.mult)
            B.tt(ybuf[:, 4 * G:4 * G + 4, cs], tB[:, :, :], gdv, ALU.mult, eng="pool")
            yield
            bKd = bank()
            kdv = psbf(bKd, 512).m(f3)
            for i, hh in enumerate(hs):
                B.tr(V(PS, kdv.ap[:, i, :], (bKd,)), qk[:, 0, hh, cs], ident_b[:, :])
            B.tt(kdec[:, :, :], kdv, kscale.m(lambda a: a[:, 4 * G:4 * G + 4]).bc(2, 128), ALU.mult)
            yield
            if not smp:
                bS = bank()
                for i in range(4):
                    B.mm(psb(bS, i * 128, i * 128 + 128), kdec[:, i, :], uu[:, i, :])
                Sg = S(G)[:, 4 * G:4 * G + 4, :]
                B.tt(Sg, Sg, cdec.m(lambda a: a[:, 4 * G:4 * G + 4]).bc(2, 128), ALU.mult, eng="pool")
                B.tt(Sg, Sg, psb(bS).m(f3), ALU.add)
                B.copy(Sbf(G)[:, 4 * G:4 * G + 4, :], Sg, eng="act")
                yield
            else:
                for i, hh in enumerate(hs):
                    B.tt(Uexp[:, :, :], uu[:, i, :].bc(1, NSEQ), seqsel.bc(2, 128), ALU.mult)
                    for q4 in range(4):
                        bS = bank()
                        B.mm(psb(bS), kdec[:, i, :],
                             Uexp[:, 4 * q4:4 * q4 + 4, :].m(lambda a: a.rearrange("p n d -> p (n d)")))
                        k2 = (hh * 4 + q4) % 2
                        X = (Sold, Snew)[k2]
                        B.dma(X[:, :, :], s_ssm_in[4 * q4:4 * q4 + 4, hh].rearrange("n k v -> k n v"), "so%d" % k2)
                        B.tt(X[:, :, :], X[:, :, :], cdec_s[:, 4 * q4:4 * q4 + 4, hh].bc(2, 128), ALU.mult,
                             eng="pool")
                        B.tt(X[:, :, :], X[:, :, :], psb(bS).m(f3), ALU.add)
                        B.dma(sssm_o[4 * q4:4 * q4 + 4, hh].rearrange("n k v -> k n v"), X[:, :, :], "sw%d" % k2)


        if smp:
            for G in range(2):
                for _ in grp(G, TS[0]):
                    pass
        else:
            if gs >= 1:
                gens = [itertools.chain(grp(0, TS[0]), swa_p(sl, gs, 0, TS[0])),
                        itertools.chain(grp(1, TS[1]), swa_p(sl, gs, 1, TS[1]))]
            else:
                gens = [grp(0, TS[0]), grp(1, TS[1])]
            while gens:
                for gen in list(gens):
                    try:
                        next(gen)
                    except StopIteration:
                        gens.remove(gen)
    def swa_p(sl, gs, g, T):
        cs = slice(sl * 128, sl * 128 + 128)
        par = gs % 2
        f3 = lambda a: a.rearrange("p (h n) -> p h n", h=4)
        fl = lambda a: a.rearrange("p h n -> p (h n)")
        scale = 128.0 ** -0.5
        lg_, rden_, PTo_, PTp_ = T[1], T[2], T[4], T[5]
        PTm_ = T[6]
        sk_cur = shr[:, 24 + g, cs]
        sq_g = shr2[:, 4 * g:4 * g + 4, cs]
        use_prev = gs >= 2
        bt_ = bank()
        tv = psbf(bt_, 128)
        B.tr(tv, svb_t[:, g, cs], ident_b[:, :])
        bO = bank()
        B.mm(psb(bO).m(f3), sk_cur, sq_g)
        if use_prev:
            bP = bank()
            kprev = shr[:, 24 + g, (sl - 1) * 128:sl * 128] if sl > 0 else skprev[:, g, :]
            B.mm(psb(bP).m(f3), kprev, sq_g)
        bM = bank()
        B.mm(PS(bM)[0:16, bM, :].m(f3), kmT[:, g, :], sq_g)
        yield
        B.copy(vsw[:, par, g, :], tv, eng="act")
        if g == 0 and gs <= 2:
            B.dma(cbm[:, :, :], cbiasm[:, 1 if gs == 1 else 2], "c4", eng="pool")
        B.stt(lg_[:, :, :], psb(bO).m(f3), scale, cb[:, 1, 4 * g:4 * g + 4, :], ALU.mult, ALU.add)
        B.act(PTo_[:, :, :], lg_[:, :, :], AF.Exp)
        if use_prev:
            B.stt(lg_[:, :, :], psb(bP).m(f3), scale, cb[:, 0, 4 * g:4 * g + 4, :], ALU.mult, ALU.add)
            B.act(PTp_[:, :, :], lg_[:, :, :], AF.Exp)
        B.stt(lg_[0:16, :, :], PS(bM)[0:16, bM, :].m(f3), scale, cbm[:, 4 * g:4 * g + 4, :], ALU.mult, ALU.add)
        B.act(PTm_[0:16, :, :], lg_[0:16, :, :], AF.Exp)
        yield
        bV = bank()
        bDn = bank()
        pvl = [(vsw[:, par, g, :], ones_b[:, :], PTo_[:, :, :].m(fl))]
        if use_prev:
            pvl.append((vsw[:, 1 - par, g, :], ones_b[:, :], PTp_[:, :, :].m(fl)))
        pvl.append((vm[:, g, :], ones_b[0:16, :], PTm_[0:16, :, :].m(fl)))
        for i, (lv_, lo_, r_) in enumerate(pvl):
            B.mm(psb(bDn), lo_, r_, start=(i == 0), stop=(i == len(pvl) - 1))
        for i, (lv_, lo_, r_) in enumerate(pvl):
            B.mm(psb(bV), lv_, r_, start=(i == 0), stop=(i == len(pvl) - 1))
        yield
        B.tt(rden_[:, :, :], psb(bDn).m(f3), esink[:, 4 * g:4 * g + 4].bc(2, 128), ALU.add)
        B.act(rden_[:, :, :], rden_[:, :, :], AF.Ln)
        B.act(rden_[:, :, :], rden_[:, :, :], AF.Exp, scale=-1.0)
        B.tt(lg_[:, :, :], psb(bV).m(f3), rden_[:, :, :], ALU.mult)
        B.tt(lg_[:, :, :], lg_[:, :, :], shr[:, 16 + 4 * g:16 + 4 * g + 4, cs], ALU.mult, eng="pool")
        yv = ybuf[:, 4 * g:4 * g + 4, cs]
        B.tt(yv, yv, lg_[:, :, :], ALU.add, eng="pool")
        yield

    def swa(kind, sl, gs):
        cs = slice(sl * 128, sl * 128 + 128)
        smp = kind == "s"
        par = gs % 2
        f3 = lambda a: a.rearrange("p (h n) -> p h n", h=4)
        fl = lambda a: a.rearrange("p h n -> p (h n)")
        scale = 128.0 ** -0.5
        for g in range(2):
            sk_cur = shr[:, 24 + g, cs]
            sq_g = shr2[:, 4 * g:4 * g + 4, cs]
            bt_ = bank()
            tv = psbf(bt_, 128)
            B.tr(tv, svb_t[:, g, cs], ident_b[:, :])
            B.copy(vsw[:, par, g, :], tv, eng="act")
            if kind == "p" and gs == 0:
                B.copy(kmT[:, g, :], shr[:, 24 + g, 112:128], eng="act")
                bt2 = bank()
                tv2 = PS(bt2)[0:16, bt2, 0:64].bitcast(BF16)
                B.tr(tv2, svb_t[:, g, 112:128], ident_b[:, :])
                B.copy(vm[:, g, :], tv2, eng="act")
            if smp:
                B.dma(kmTs[:, :, :], kmT_in[:, g].rearrange("n d s -> d n s"), "c6", eng="pool")
                B.dma(vms[:, :, :], vm_in[:, :, g, :].rearrange("n s d -> s n d"), "c7", eng="pool")
            use_own = smp or gs >= 1
            use_prev = smp or gs >= 2
            var = 2 if smp else (0 if gs == 0 else (1 if gs == 1 else 2))
            if g == 0 and (smp or gs <= 2):
                B.dma(cbm[:, :, :], cbiasm[:, var], "c4", eng="pool")
            bV = bank()
            bDn = bank()
            pvl = []
            dnl = []

            if use_own:
                bO = bank()
                B.mm(psb(bO).m(f3), sk_cur, sq_g)
                bi = cb[:, 1, 4 * g:4 * g + 4, :]
                B.stt(lg[:, :, :], psb(bO).m(f3), scale, bi, ALU.mult, ALU.add)
                B.act(PTo[:, :, :], lg[:, :, :], AF.Exp)
                pvl.append((psb(bV), vsw[:, par, g, :], PTo[:, :, :].m(fl)))
                dnl.append((psb(bDn), ones_b[:, :], PTo[:, :, :].m(fl)))
            if use_prev:
                bP = bank()
                if not smp:
                    kprev = shr[:, 24 + g, (sl - 1) * 128:sl * 128] if sl > 0 else skprev[:, g, :]
                    B.mm(psb(bP).m(f3), kprev, sq_g)
                else:
                    for n in range(NSEQ):
                        sl4 = (n // 4) % 2
                        if n % 4 == 0:
                            B.dma(kbT(sl4)[:, sl4, :, :], kbT_in[n:n + 4, g].rearrange("n d w -> d n w"),
                                  "kb%d" % sl4, eng="pool")
                        for hh in range(4):
                            B.mm(PS(bP)[:, bP, hh * 128 + n * 8:hh * 128 + n * 8 + 8], kbT(sl4)[:, sl4, n % 4, :],
                                 shr2[:, 4 * g + hh, n * 8:n * 8 + 8],
                                 start=(n == 0 and hh == 0), stop=(n == NSEQ - 1 and hh == 3))
                bi = cb[:, 0, 4 * g:4 * g + 4, :]
                B.stt(lg[:, :, :], psb(bP).m(f3), scale, bi, ALU.mult, ALU.add)
                B.act(PTp[:, :, :], lg[:, :, :], AF.Exp)
                dnl.append((psb(bDn), ones_b[:, :], PTp[:, :, :].m(fl)))
                if not smp:
                    pvl.append((psb(bV), vsw[:, 1 - par, g, :], PTp[:, :, :].m(fl)))
                else:
                    for n in range(NSEQ):
                        for hh in range(4):
                            o = PS(bV)[:, bV, hh * 128 + n * 8:hh * 128 + n * 8 + 8]
                            pvl.append((o, ("vb", n), PTp[:, hh, n * 8:n * 8 + 8]))
            bM = bank()
            if not smp:
                B.mm(PS(bM)[0:16, bM, :].m(f3), kmT[:, g, :], sq_g)
            else:
                for n in range(NSEQ):
                    for hh in range(4):
                        B.mm(PS(bM)[0:16, bM, hh * 128 + n * 8:hh * 128 + n * 8 + 8], kmTs[:, n, :],
                             shr2[:, 4 * g + hh, n * 8:n * 8 + 8],
                             start=(n == 0 and hh == 0), stop=(n == NSEQ - 1 and hh == 3))
            B.stt(lg[0:16, :, :], PS(bM)[0:16, bM, :].m(f3), scale, cbm[:, 4 * g:4 * g + 4, :], ALU.mult, ALU.add)
            B.act(PTm[:, :, :], lg[0:16, :, :], AF.Exp)
            dnl.append((psb(bDn), ones_b[0:16, :], PTm[:, :, :].m(fl)))
            if not smp:
                pvl.append((psb(bV), vm[:, g, :], PTm[:, :, :].m(fl)))
            else:
                for n in range(NSEQ):
                    for hh in range(4):
                        o = PS(bV)[:, bV, hh * 128 + n * 8:hh * 128 + n * 8 + 8]
                        pvl.append((o, vms[:, n, :], PTm[:, hh, n * 8:n * 8 + 8]))
            for i, (o, l, r) in enumerate(dnl):
                B.mm(o, l, r, start=(i == 0), stop=(i == len(dnl) - 1))
            for i, (o, l, r) in enumerate(pvl):
                if isinstance(l, tuple):
                    n = l[1]
                    sl4 = (n // 4) % 2
                    if n % 4 == 0 and (i == 0 or pvl[i - 1][1] != l):
                        B.dma(vb(sl4)[:, sl4, :, :], vb_in[n:n + 4, :, g, :].rearrange("n w d -> w n d"),
                              "vb%d" % sl4, eng="pool")
                    l = vb(sl4)[:, sl4, n % 4, :]
                B.mm(o, l, r, start=(i == 0), stop=(i == len(pvl) - 1))
            B.tt(rden[:, :, :], psb(bDn).m(f3), esink[:, 4 * g:4 * g + 4].bc(2, 128), ALU.add)
            B.act(rden[:, :, :], rden[:, :, :], AF.Ln)
            B.act(rden[:, :, :], rden[:, :, :], AF.Exp, scale=-1.0)
            B.tt(lg[:, :, :], psb(bV).m(f3), rden[:, :, :], ALU.mult)
            B.tt(lg[:, :, :], lg[:, :, :], shr[:, 16 + 4 * g:16 + 4 * g + 4, cs], ALU.mult, eng="pool")
            yv = ybuf[:, 4 * g:4 * g + 4, cs]
            B.tt(yv, yv, lg[:, :, :], ALU.add, eng="pool")

    def mixer(kind, m, N, out_ap=None):
        nsub = N // 128
        barrier_shr()
        if kind == "s":
            B.emit("pool", lambda e: e.memset(tmpn.t[0:1, 1, 0:1], 0.0), outs=[V(R1, R1.t[:, :], (None,)), tmpn(1)[0:1, 1, 0:1]])
            B.dma(V(R1, shist.t[:, :, :], (None,)), shistT.rearrange("(c p) x -> p c x", p=128), "c5")
            B.dma(cb[:, 0, :, :], cbias[:, 3], "c3", eng="pool")
            B.dma(cb[:, 1, :, :], cbias[:, 2], "c3b", eng="pool")
        inproj(kind, m, N)
        if kind == "s":
            B.dma(sconv_o.rearrange("(c p) x -> p c x", p=128), sconv[:, :, :], "sc")
            bt_ = bank()
            for i4 in range(4):
                B.tr(psb(bt_, i4 * 128, i4 * 128 + 128), kvf[:, i4, :], cm[:, 7, :])
            B.copy(kvtok[:, :, :], psb(bt_).m(lambda a: a.rearrange("p (h n) -> p h n", h=4)))
            B.dma(swink_o[:, 0:120], kb_in[:, 8:128], "wk0")
            B.dma(swinv_o[:, 0:120], vb_in[:, 8:128], "wv0")
            for n in range(NSEQ):
                B.dma(swink_o[n, 120:128], kvtok[n * 8:n * 8 + 8, 0:2, :], "wk1")
                B.dma(swinv_o[n, 120:128], kvtok[n * 8:n * 8 + 8, 2:4, :], "wv1")
        for sl in range(nsub):
            gs = m * NSUB + sl
            if "dn" not in cfg.get("skip", ()):
                dn(kind, sl, gs)
            if "swa" not in cfg.get("skip", ()) and (kind == "s" or gs == 0):
                swa(kind, sl, gs)
        if kind == "p":
            for g in range(2):
                B.copy(skprev[:, g, :], shr[:, 24 + g, N - 128:N], eng="act")
        stage("outproj")
        st = Stats(N)
        for d in range(8):
            wv = W.get()
            bo = bank()
            for c in range(8):
                B.mm(psb(bo, 0, N), wv(c * 128, c * 128 + 128), ybuf[:, c, :N], start=(c == 0), stop=(c == 7))
            W.done()
            B.copy(yo(d)[:, d, :N], psb(bo, 0, N), eng="act")
            st.add(psb(bo, 0, N))
        postnorm_add(3, N, False, out_ap, st)
        barrier_shr()

    for kind, m in tiles:
        if kind == "p":
            N = NT
            src = xpT[:, m * NT:(m + 1) * NT]
            dst = ypT[:, m * NT:(m + 1) * NT]
        else:
            N = 128
            src = xsT
            dst = ysT
        for c in range(8):
            B.dma(h(c)[:, c, :N], src[c * 128:(c + 1) * 128, :], "xin%d" % c)
        last_is_ffn2 = do_ffn2
        ffn(0, N, None if (do_mixer or do_ffn2) else dst)
        stage("ffn1_done")
        if do_mixer:
            mixer(kind, m, N, None if do_ffn2 else dst)
        if do_ffn2:
            ffn(1, N, dst)
        if do_mixer and kind == "p" and m == nmac - 1:
            pcv = pconv_o.rearrange("(c p) j -> p c j", p=128)
            for q3 in range(3):
                B.dma(pcv[:, 8 * q3:8 * q3 + 8, :], hist[:, 8 * q3:8 * q3 + 8, :], "pc%d" % q3)
            B.dma(pssm_o.rearrange("h k v -> k h v"), S[:, :, :], "pss")

    B.finalize()
    B.close()
    return nc


def _tile_w(w):
    R, C = w.shape
    return np.ascontiguousarray(w.reshape(R // 128, 128, C // 128, 128).transpose(2, 1, 0, 3)).reshape(C // 128, 128, R)


def _consts():
    f = np.float32
    idx = np.arange(128)
    s_ = idx[:, None]
    i_ = idx[None, :]
    same = (s_ // 8) == (i_ // 8)
    cm = np.zeros((128, 9, 128), f)
    cm[:, 0] = s_ <= i_
    cm[:, 1] = s_ > i_
    cm[:, 2] = s_ < i_
    cm[:, 3] = same & (s_ <= i_)
    cm[:, 4] = same & (s_ > i_)
    cm[:, 5] = same & (s_ < i_)
    cm[:, 6] = same
    cm[:, 7] = np.eye(128)
    cm[:, 8] = 1.0
    slopes = (2.0 ** (-(np.arange(8) + 1.0))).astype(np.float64)
    j = idx[:, None, None].astype(np.float64)
    hh = slopes[None, :, None]
    i = idx[None, None, :].astype(np.float64)
    cb = np.zeros((128, 4, 8, 128), f)
    cb[:, 0] = np.where(i <= j, -hh * (i + 128 - j), NEG)
    cb[:, 1] = np.where(i >= j, -hh * (i - j), NEG)
    samej = (idx[:, None, None] // 8) == (idx[None, None, :] // 8)
    cb[:, 2] = np.where(samej & (i >= j), -hh * (i - j), NEG)
    t = (idx[None, None, :] % 8).astype(np.float64)
    cb[:, 3] = np.where(j >= t, -hh * (t + 128 - j), NEG)
    sp = np.arange(16)[:, None, None].astype(np.float64)
    cbm = np.zeros((16, 3, 8, 128), f)
    pos0 = i - 112
    cbm[:, 0] = np.where(pos0 >= sp, -hh * np.minimum(pos0 - sp, 128), NEG)
    pos1 = i + 16
    cbm[:, 1] = -hh * np.minimum(pos1 - sp, 128) + 0 * sp
    cbm[:, 2] = -hh * 128.0 + 0 * sp + 0 * i
    return cm, cb, cbm


def prep_inputs(inp, cfg):
    f = np.float32
    A = lambda k: np.asarray(inp[k], f)
    wg, wd = [], []
    for nm in ("ffn1", "ffn2"):
        g = _tile_w(A(nm + "_w_gate")[0])
        up = _tile_w(A(nm + "_w_up")[0])
        wg.append(np.ascontiguousarray(np.stack([g, up], axis=1).reshape(2 * NFF, 128, 1024)))
        wd.append(_tile_w(A(nm + "_w_down")[0]))
    gl = [A(k)[0] for k in ("ffn1_norm_pre", "ffn1_norm_post", "mix_norm_pre", "mix_norm_post",
                            "ffn2_norm_pre", "ffn2_norm_post")]
    gains = np.ascontiguousarray(np.stack([g.reshape(8, 128).T for g in gl], axis=1))
    w_in = A("w_in")[0]
    cols = np.concatenate([np.arange(0, 4096), np.arange(4112, 7696), np.arange(4096, 4112)])
    w_in = w_in[:, cols]
    win = _tile_w(np.ascontiguousarray(w_in[:, :7680]))
    wba = np.ascontiguousarray(w_in[:, 7680:].reshape(8, 128, 16).transpose(1, 0, 2)).reshape(128, 128)
    wout = _tile_w(A("w_out")[0])
    cm, cb, cbm = _consts()
    vecs = np.zeros((128, NV), f)
    cw = A("dn_conv_w")[0]
    vecs[:, 0:96] = cw.reshape(4, 24, 128).transpose(2, 1, 0).reshape(128, 96)
    vecs[:, 96] = A("dn_norm_w")[0]
    vecs[:, 97:105] = A("dn_a_log")[0][None, :]
    vecs[:, 105:113] = A("dn_dt_bias")[0][None, :]
    vecs[:, 113:121] = A("swa_sinks")[0][None, :]
    vecs[:, 121:137] = (np.arange(128)[:, None] // 8) == np.arange(16)[None, :]
    xp = A("x_prompt")
    xs = A("x_sample")
    meta = A("meta_tokens")
    sconv = A("state_dn_conv")[0]
    sssm = A("state_dn_ssm")[0]
    cmk, cmv = A("cache_swa_meta_k")[0], A("cache_swa_meta_v")[0]
    ck, cv = A("cache_swa_k")[0], A("cache_swa_v")[0]
    maps = []
    for c in range(NCORES):
        m = {}
        seq = np.zeros((TP, D), f)
        if c < 4:
            seq[PADF:PADF + NMETA] = meta
            seq[PADF + NMETA:] = xp[c]
        m["xpT"] = np.ascontiguousarray(seq.T)
        sl = slice(NSEQ * c, NSEQ * (c + 1))
        m["xsT"] = np.ascontiguousarray(xs[sl].reshape(128, D).T)
        m["wgu1"], m["wgu2"] = wg
        m["wdn1"], m["wdn2"] = wd
        m["win"], m["wba"], m["wout"] = win, wba, wout
        m["gains"], m["cmask"], m["cbias"], m["cbiasm"], m["vecs"] = gains, cm, cb, cbm, vecs
        m["shistT"] = np.ascontiguousarray(sconv[sl].transpose(2, 0, 1)).reshape(3072, NSEQ * 3)
        m["s_ssm_in"] = np.ascontiguousarray(sssm[sl])
        m["kb_in"] = np.ascontiguousarray(ck[sl])
        m["kbT_in"] = np.ascontiguousarray(ck[sl].transpose(0, 2, 3, 1))
        m["kmT_in"] = np.ascontiguousarray(cmk[sl].transpose(0, 2, 3, 1))
        m["vb_in"] = np.ascontiguousarray(cv[sl])
        m["vm_in"] = np.ascontiguousarray(cmv[sl])
        maps.append(m)
    return maps


_CACHE = {}


def run_device(inp, cfg):
    key = repr(sorted(cfg.items()))
    if key not in _CACHE:
        _CACHE[key] = build_program(cfg)
    nc = _CACHE[key]
    maps = prep_inputs(inp, cfg)
    res = run_bass_kernel_spmd(nc, maps, core_ids=list(range(NCORES)))
    return res.results


def assemble(r):
    f = np.float32
    y_p = np.stack([r[b]["ypT"][:, 128:].T for b in range(4)]).astype(f)
    y_s = np.concatenate([r[c]["ysT"].T.reshape(NSEQ, DEC_T, D) for c in range(NCORES)]).astype(f)
    p_conv = np.stack([r[b]["pconv_o"].T for b in range(4)])[None].astype(f)
    p_ssm = np.stack([r[b]["pssm_o"] for b in range(4)])[None].astype(f)

    def kv(name, i0, lo=0):
        return np.stack([np.stack([r[b][name][i0][:, lo:].T, r[b][name][i0 + 1][:, lo:].T], axis=1)
                         for b in range(4)])[None].astype(f)
    p_mk, p_mv = kv("pmeta_o", 0, 112), kv("pmeta_o", 2, 112)
    p_wk, p_wv = kv("pwin_o", 0), kv("pwin_o", 2)
    s_conv = np.concatenate([r[c]["sconv_o"].reshape(3072, NSEQ, 3).transpose(1, 2, 0) for c in range(NCORES)])[None]
    s_ssm = np.concatenate([r[c]["sssm_o"] for c in range(NCORES)])[None]
    s_wk = np.concatenate([r[c]["swink_o"] for c in range(NCORES)])[None]
    s_wv = np.concatenate([r[c]["swinv_o"] for c in range(NCORES)])[None]
    return (y_p, y_s, p_conv, p_ssm, p_mk, p_mv, p_wk, p_wv,
            np.ascontiguousarray(s_conv).astype(f), s_ssm.astype(f), s_wk.astype(f), s_wv.astype(f))


def kernel(**inp):
    r = run_device(inp, {})
    return assemble(r)
```
